# Optimizing a Trainium2 kernel written in Bass

```python
import math
import jax, jax.numpy as jnp
from jax import lax
import numpy as np

D_MODEL = 2048
BATCH = 1
SEQ = 16384
DEPTH = 2

CHUNK = 64
Q_BLOCK = 128
N_A = DEPTH // 2
N_B = DEPTH - N_A
CONV_E = D_MODEL
CONV_WIDTH = 31
DIFF_HEADS = 8
DIFF_HEAD_DIM = 128
DIFF_QK = DIFF_HEADS * 2 * DIFF_HEAD_DIM
DIFF_V = DIFF_HEADS * 2 * DIFF_HEAD_DIM
EPS = 1e-6

kernel_name = "yoco_conformer_conv_diff_attention"


def _rms_norm(x, g):
    xf = x.astype(jnp.float32)
    y = xf * lax.rsqrt(jnp.mean(xf * xf, axis=-1, keepdims=True) + EPS)
    return (y * g.astype(jnp.float32)).astype(x.dtype)


def _layer_norm(x, g, b):
    xf = x.astype(jnp.float32)
    mu = jnp.mean(xf, axis=-1, keepdims=True)
    xc = xf - mu
    y = xc * lax.rsqrt(jnp.mean(xc * xc, axis=-1, keepdims=True) + EPS)
    return (y * g.astype(jnp.float32) + b.astype(jnp.float32)).astype(x.dtype)


def _depthwise_causal_conv(y, w, b):
    e = y.shape[-1]
    out = lax.conv_general_dilated(
        y, w.reshape(CONV_WIDTH, 1, e).astype(y.dtype),
        window_strides=(1,), padding=[(CONV_WIDTH - 1, 0)],
        dimension_numbers=("NWC", "WIO", "NWC"), feature_group_count=e)
    return out + b.astype(y.dtype)


def _conformer_layer(x, norm_g, w_in, dw_w, dw_b, ln_g, ln_b, w_out):
    h = _rms_norm(x, norm_g)
    u = h @ w_in
    a, b, z = jnp.split(u, 3, axis=-1)
    y = a * jax.nn.sigmoid(b)
    y = _depthwise_causal_conv(y, dw_w, dw_b)
    y = jax.nn.silu(_layer_norm(y, ln_g, ln_b))
    y = y * jax.nn.silu(z)
    return x + y @ w_out


def _diff_attention(q, k, v, lam):
    bsz, s, h, _, dh = q.shape
    nb = s // Q_BLOCK
    qb = q.reshape(bsz, nb, Q_BLOCK, h, 2, dh).transpose(1, 0, 2, 3, 4, 5)
    key_chunk = jnp.arange(s) // CHUNK

    def block(args):
        q_blk, start = args
        q_chunk = (start + jnp.arange(Q_BLOCK)) // CHUNK
        mask = key_chunk[None, :] <= q_chunk[:, None]
        sc = jnp.einsum("bqhmd,bkhmd->bhmqk", q_blk, k,
                        preferred_element_type=jnp.float32)
        sc = jnp.where(mask, sc, -jnp.inf)
        p = jax.nn.softmax(sc, axis=-1).astype(v.dtype)
        o = jnp.einsum("bhmqk,bkhe->bqhme", p, v)
        return o[:, :, :, 0] - lam * o[:, :, :, 1]

    starts = jnp.arange(nb) * Q_BLOCK
    out = lax.map(block, (qb, starts))
    return out.transpose(1, 0, 2, 3, 4).reshape(bsz, s, h, 2 * dh)


def _diff_layer(x, k, v, norm_g, w_in, q_norm_g, lam_p, subln_g, w_out, layer_idx):
    bsz, s, _ = x.shape
    lam_init = 0.8 - 0.6 * math.exp(-0.3 * (layer_idx - 1))
    h = _rms_norm(x, norm_g)
    u = h @ w_in
    q = u[..., :DIFF_QK].reshape(bsz, s, DIFF_HEADS, 2, DIFF_HEAD_DIM)
    z = u[..., DIFF_QK:]
    q = _rms_norm(q, q_norm_g) * (DIFF_HEAD_DIM ** -0.5)
    lp = lam_p.astype(jnp.float32)
    lam = (jnp.exp(jnp.sum(lp[0] * lp[1])) - jnp.exp(jnp.sum(lp[2] * lp[3]))
           + lam_init).astype(x.dtype)
    o = _diff_attention(q, k, v, lam)
    o = _rms_norm(o, subln_g) * (1.0 - lam_init)
    o = o.reshape(bsz, s, DIFF_V) * jax.nn.silu(z)
    return x + o @ w_out


def setup_inputs(seed: int = 0) -> dict:
    key = jax.random.key(seed)
    ks = jax.random.split(key, 20)

    def w(k, shape, fan_in):
        return jax.random.normal(k, shape, jnp.float32) * (fan_in ** -0.5)

    def gain(k, shape):
        return 1.0 + 0.02 * jax.random.normal(k, shape, jnp.float32)

    def bias(k, shape):
        return 0.02 * jax.random.normal(k, shape, jnp.float32)

    return {
        "x": jax.random.normal(ks[0], (BATCH, SEQ, D_MODEL), jnp.float32),
        "a_norm_g": gain(ks[1], (N_A, D_MODEL)),
        "a_w_in": w(ks[2], (N_A, D_MODEL, 3 * CONV_E), D_MODEL),
        "a_dw_w": w(ks[3], (N_A, CONV_WIDTH, CONV_E), CONV_WIDTH),
        "a_dw_b": bias(ks[4], (N_A, CONV_E)),
        "a_ln_g": gain(ks[5], (N_A, CONV_E)),
        "a_ln_b": bias(ks[6], (N_A, CONV_E)),
        "a_w_out": w(ks[7], (N_A, CONV_E, D_MODEL), CONV_E),
        "kv_norm_g": gain(ks[8], (D_MODEL,)),
        "w_kv": w(ks[9], (D_MODEL, DIFF_QK + DIFF_V), D_MODEL),
        "k_norm_g": gain(ks[10], (DIFF_HEAD_DIM,)),
        "b_norm_g": gain(ks[11], (N_B, D_MODEL)),
        "b_w_in": w(ks[12], (N_B, D_MODEL, DIFF_QK + DIFF_V), D_MODEL),
        "b_q_norm_g": gain(ks[13], (N_B, DIFF_HEAD_DIM)),
        "b_lambda": 0.1 * jax.random.normal(ks[14], (N_B, 4, DIFF_HEAD_DIM), jnp.float32),
        "b_subln_g": gain(ks[15], (N_B, 2 * DIFF_HEAD_DIM)),
        "b_w_out": w(ks[16], (N_B, DIFF_V, D_MODEL), DIFF_V),
    }


def reference(x, a_norm_g, a_w_in, a_dw_w, a_dw_b, a_ln_g, a_ln_b, a_w_out,
              kv_norm_g, w_kv, k_norm_g, b_norm_g, b_w_in, b_q_norm_g,
              b_lambda, b_subln_g, b_w_out):
    bsz, s, _ = x.shape
    k = None
    v = None
    for layer in range(DEPTH):
        if layer < N_A:
            x = _conformer_layer(x, a_norm_g[layer], a_w_in[layer], a_dw_w[layer],
                                 a_dw_b[layer], a_ln_g[layer], a_ln_b[layer],
                                 a_w_out[layer])
        else:
            if layer == N_A:
                kv = _rms_norm(x, kv_norm_g) @ w_kv
                k = _rms_norm(kv[..., :DIFF_QK].reshape(
                    bsz, s, DIFF_HEADS, 2, DIFF_HEAD_DIM), k_norm_g)
                v = kv[..., DIFF_QK:].reshape(bsz, s, DIFF_HEADS, 2 * DIFF_HEAD_DIM)
            j = layer - N_A
            x = _diff_layer(x, k, v, b_norm_g[j], b_w_in[j], b_q_norm_g[j],
                            b_lambda[j], b_subln_g[j], b_w_out[j], layer + 1)
    return x
```

```python
import contextlib
import math
import numpy as np
import ml_dtypes
import concourse.bass as bass
import concourse.mybir as mybir
from concourse.bass_utils import run_bass_kernel_spmd

F32 = mybir.dt.float32
BF16 = mybir.dt.bfloat16
ALU = mybir.AluOpType
ACT = mybir.ActivationFunctionType
AX = mybir.AxisListType

NCORES = 8
D = 2048
S = 16384
EPS = 1e-6
CONV_W = 31
HALO = CONV_W - 1
DH = 128
LAM_INIT = 0.8 - 0.6 * math.exp(-0.3 * (2 - 1))

ENGS = ("tensor", "vector", "scalar", "gpsimd", "sync")


class T:
    __slots__ = ("name", "last_w", "readers")

    def __init__(self, name=""):
        self.name = name
        self.last_w = None
        self.readers = []


class Prog:
    def __init__(self, nc):
        self.nc = nc
        self.ops = []
        self.groups = {}

    def op(self, eng, fn, reads=(), writes=(), dma=None, inc=16):
        i = len(self.ops)
        deps = set()
        for t in reads:
            if t.last_w is not None:
                deps.add(t.last_w)
        for t in writes:
            if t.last_w is not None:
                deps.add(t.last_w)
            deps.update(t.readers)
        deps.discard(i)
        for t in reads:
            t.readers.append(i)
        for t in writes:
            t.last_w = i
            t.readers = []
        gneed = {}
        for jd in deps:
            g = self.ops[jd]["dma"]
            if g is not None:
                gneed[g] = self.groups[g]
        o = dict(i=i, eng=eng, fn=fn, deps=deps, dma=dma, sig=False, sidx=0, inc=inc, gneed=gneed)
        if dma is not None:
            self.groups[dma] = self.groups.get(dma, 0) + inc
            o["sidx"] = self.groups[dma]
        self.ops.append(o)
        return i

    def emit(self, final_wait_engine="sync"):
        nc = self.nc
        ops = self.ops
        for o in ops:
            for j in o["deps"]:
                d = ops[j]
                if d["dma"] is not None:
                    continue
                if d["eng"] == "tensor" and o["eng"] == "tensor" and o["dma"] is None:
                    continue
                d["sig"] = True
        cnt = {e: 0 for e in ENGS}
        for o in ops:
            if o["dma"] is None and o["sig"]:
                cnt[o["eng"]] += 1
                o["sidx"] = cnt[o["eng"]]
        with contextlib.ExitStack() as es:
            esem = {e: es.enter_context(nc.semaphore("p_" + e)) for e in ENGS}
            gsem = {g: es.enter_context(nc.semaphore("g_%d" % k))
                    for k, g in enumerate(self.groups)}
            block = es.enter_context(nc.Block())
            final = dict(self.groups)

            def make(ename):
                def body(eng):
                    waited_e = {e: 0 for e in ENGS}
                    waited_g = {g: 0 for g in self.groups}
                    for o in ops:
                        if o["eng"] != ename:
                            continue
                        need_e = {}
                        need_g = o["gneed"]
                        for j in o["deps"]:
                            d = ops[j]
                            if d["dma"] is not None:
                                continue
                            else:
                                if d["eng"] == "tensor" and ename == "tensor" and o["dma"] is None:
                                    continue
                                need_e[d["eng"]] = max(need_e.get(d["eng"], 0), d["sidx"])
                        for e, v in need_e.items():
                            if v > waited_e[e]:
                                eng.wait_ge(esem[e], v)
                                waited_e[e] = v
                        for g, v in need_g.items():
                            if v > waited_g[g]:
                                eng.wait_ge(gsem[g], v)
                                waited_g[g] = v
                        ins = o["fn"](eng)
                        if o["dma"] is not None:
                            ins.then_inc(gsem[o["dma"]], o["inc"])
                        elif o["sig"]:
                            ins.then_inc(esem[ename], 1)
                    if ename == final_wait_engine:
                        for g, v in final.items():
                            if v > waited_g[g]:
                                eng.wait_ge(gsem[g], v)
                        for e in ENGS:
                            if e != ename and cnt[e] > waited_e[e]:
                                eng.wait_ge(esem[e], cnt[e])
                return body

            block.tensor(make("tensor"))
            block.vector(make("vector"))
            block.scalar(make("scalar"))
            block.gpsimd(make("gpsimd"))
            block.sync(make("sync"))


def build_l2(s_len=S, dbg=False):
    QB = 256
    nblk = s_len // QB
    ntile = s_len // 128
    nc = bass.Bass("TRN2", target_bir_lowering=False)
    xnT = nc.dram_tensor("xnT", [nblk, 128, 16, QB], BF16, kind="ExternalInput").ap()
    wA = nc.dram_tensor("wA", [128, 16, 512], F32, kind="ExternalInput").ap()
    wB = nc.dram_tensor("wB", [128, 16, 512], F32, kind="ExternalInput").ap()
    gcol = nc.dram_tensor("gcol", [128, 32], F32, kind="ExternalInput").ap()
    gqk = nc.dram_tensor("gqk", [128, 2], F32, kind="ExternalInput").ap()
    gsub = nc.dram_tensor("gsub", [128, 256], F32, kind="ExternalInput").ap()
    lamp = nc.dram_tensor("lamp", [128, 512], F32, kind="ExternalInput").ap()
    o_out = nc.dram_tensor("o", [s_len, 256], BF16, kind="ExternalOutput").ap()

    P = Prog(nc)
    with contextlib.ExitStack() as es:
        def sb(name, shape, dt):
            return es.enter_context(nc.sbuf_tensor(name, shape, dt))

        KT = sb("KT", [128, 2, s_len], BF16)
        V = sb("V", [128, ntile, 257], BF16)
        wAb = sb("wAb", [128, 16, 512], BF16)
        wBb = sb("wBb", [128, 16, 512], BF16)
        xb = [sb("xb%d" % i, [128, 16, QB], BF16) for i in range(2)]
        QTb = [sb("QTb%d" % i, [128, 2, QB], BF16) for i in range(2)]
        PT = [sb("PT%d" % i, [128, 4, QB], BF16) for i in range(3)]
        sq = sb("sq", [128, 1024], BF16)
        rst = sb("rst", [128, 1024], F32)
        th = [sb("th%d" % i, [128, 256], F32) for i in range(2)]
        Zg = [sb("Zg%d" % i, [128, 256], BF16) for i in range(4)]
        ea = [sb("ea%d" % i, [128, 256], F32) for i in range(2)]
        eo = [sb("eo%d" % i, [128, 256], F32) for i in range(2)]
        esq = ea
        eof = ea
        ob = [sb("ob%d" % i, [128, 256], BF16) for i in range(4)]
        small = [sb("small%d" % i, [128, 8], F32) for i in range(2)]
        gcol_s = sb("gcol_s", [128, 32], F32)
        gqk_s = sb("gqk_s", [128, 2], F32)
        gsub_s = sb("gsub_s", [128, 256], F32)
        lam_v = sb("lam_v", [128, 4], F32)
        ones = sb("ones", [128, 128], BF16)
        wst = [sb("wst%d" % i, [128, 1, 512], F32) for i in range(2)]
        PS = [es.enter_context(nc.psum_tensor("ps%d" % i, [128, 1024], F32)) for i in range(4)]

        Tb = [T("bank%d" % i) for i in range(8)]
        T_KT = [T() for _ in range(nblk)]
        T_V = [T() for _ in range(ntile)]
        T_wA, T_wB, T_g, T_lam, T_ones = T(), T(), T(), T(), T()
        T_wst = [T(), T()]
        T_xb = [T(), T()]
        T_QT = [T(), T()]
        T_PT = [T(), T(), T()]
        T_sq, T_rst = T(), T()
        T_th = [T(), T()]
        T_Zg = [T() for _ in range(4)]
        T_e = [T(), T()]
        T_ob = [T() for _ in range(4)]
        T_out = T()

        P.op("sync", lambda e: e.dma_start(out=gcol_s[:], in_=gcol), writes=[T_g], dma="c0")
        P.op("sync", lambda e: e.dma_start(out=gqk_s[:], in_=gqk), writes=[T_g], dma="c0")
        P.op("sync", lambda e: e.dma_start(out=gsub_s[:], in_=gsub), writes=[T_g], dma="c0")
        P.op("sync", lambda e: e.dma_start(out=ea[0][:], in_=lamp[:, 0:256]), writes=[T_lam], dma="c0")
        P.op("sync", lambda e: e.dma_start(out=ea[1][:], in_=lamp[:, 256:512]), writes=[T_lam], dma="c0")
        P.op("gpsimd", lambda e: e.memset(ones[:], 1.0), writes=[T_ones])
        P.op("gpsimd", lambda e: e.memset(V[:, :, 256:257], 1.0), writes=T_V)
        P.op("vector", lambda e: e.tensor_scalar_mul(out=gqk_s[:, 0:1], in0=gqk_s[:, 0:1], scalar1=math.sqrt(128.0)),
             reads=[T_g], writes=[T_g])
        P.op("vector", lambda e: e.tensor_scalar_mul(out=gsub_s[:], in0=gsub_s[:], scalar1=(1.0 - LAM_INIT) * 16.0),
             reads=[T_g], writes=[T_g])
        P.op("vector", lambda e: e.tensor_tensor(out=eo[0][:, 0:128], in0=ea[0][:, 0:128], in1=ea[0][:, 128:256], op=ALU.mult),
             reads=[T_lam], writes=[T_lam])
        P.op("vector", lambda e: e.tensor_tensor(out=eo[0][:, 128:256], in0=ea[1][:, 0:128], in1=ea[1][:, 128:256], op=ALU.mult),
             reads=[T_lam], writes=[T_lam])
        P.op("vector", lambda e: e.reduce_sum(out=lam_v[:, 0:2], in_=eo[0][:].rearrange("p (a b) -> p a b", a=2), axis=AX.X),
             reads=[T_lam], writes=[T_lam, T_e[0], T_e[1]])
        P.op("scalar", lambda e: e.activation(out=lam_v[:, 2:4], in_=lam_v[:, 0:2], func=ACT.Exp),
             reads=[T_lam], writes=[T_lam])
        P.op("vector", lambda e: e.tensor_tensor(out=lam_v[:, 0:1], in0=lam_v[:, 3:4], in1=lam_v[:, 2:3], op=ALU.subtract),
             reads=[T_lam], writes=[T_lam])
        P.op("vector", lambda e: e.tensor_scalar_add(out=lam_v[:, 1:2], in0=lam_v[:, 0:1], scalar1=-LAM_INIT),
             reads=[T_lam], writes=[T_lam])
        nlam = lam_v[:, 1:2]
        k = 0
        for (wsrc, wdst, Tw) in ((wA, wAb, T_wA), (wB, wBb, T_wB)):
            for ch in range(16):
                b = k % 2
                P.op("sync", lambda e, b=b, ch=ch, wsrc=wsrc: e.dma_start(out=wst[b][:], in_=wsrc[:, ch:ch + 1, :]),
                     writes=[T_wst[b]], dma="wst%d" % b)
                for f in range(1):
                    ft = ch + f
                    for half in range(2):
                        if wsrc is wA:
                            gi = ft if half == 0 else 16 + ft
                        else:
                            gi = ft if half == 0 else 16 + ft
                        P.op("vector", lambda e, b=b, f=f, ft=ft, half=half, gi=gi, wdst=wdst: e.tensor_scalar_mul(
                            out=wdst[:, ft, half * 256:(half + 1) * 256], in0=wst[b][:, f, half * 256:(half + 1) * 256],
                            scalar1=gcol_s[:, gi:gi + 1]),
                            reads=[T_wst[b], T_g], writes=[Tw])
                k += 1

        def load_xb(j):
            b = j % 2
            P.op("sync", lambda e: e.dma_start(out=xb[b][:], in_=xnT[j]), writes=[T_xb[b]], dma="xb%d" % b)

        load_xb(0)
        def do_block(j):
            jb = j % 2
            if j + 1 < nblk:
                load_xb(j + 1)
            for i in range(4):
                for ft in range(16):
                    P.op("tensor", lambda e, i=i, ft=ft: e.matmul(
                        PS[0][:, i * 256:(i + 1) * 256], lhsT=wAb[:, ft, i * 128:(i + 1) * 128], rhs=xb[jb][:, ft, :],
                        start=(ft == 0), stop=(ft == 15)),
                        reads=[T_wA, T_xb[jb]], writes=[Tb[i // 2]])
            for ti in range(2):
                for ft in range(16):
                    P.op("tensor", lambda e, ti=ti, ft=ft: e.matmul(
                        PS[1][:, ti * 512:(ti + 1) * 512], lhsT=xb[jb][:, ft, ti * 128:(ti + 1) * 128], rhs=wBb[:, ft, :],
                        start=(ft == 0), stop=(ft == 15)),
                        reads=[T_wB, T_xb[jb]], writes=[Tb[2 + ti]])
            P.op("scalar", lambda e: e.activation(out=sq[:], in_=PS[0][:], func=ACT.Square),
                 reads=[Tb[0], Tb[1]], writes=[T_sq])
            for i in range(4):
                P.op("tensor", lambda e, i=i: e.matmul(PS[2][:, i * 256:(i + 1) * 256], lhsT=ones[:], rhs=sq[:, i * 256:(i + 1) * 256],
                                                      start=True, stop=True),
                     reads=[T_ones, T_sq], writes=[Tb[4 + i // 2]])
            P.op("scalar", lambda e: e.activation(out=rst[:], in_=PS[2][:], func=ACT.Ln, bias=128.0 * EPS),
                 reads=[Tb[4], Tb[5]], writes=[T_rst])
            P.op("scalar", lambda e: e.activation(out=rst[:], in_=rst[:], func=ACT.Exp, scale=-0.5),
                 reads=[T_rst], writes=[T_rst])
            P.op("vector", lambda e: e.scalar_tensor_tensor(
                out=KT[:, :, j * QB:(j + 1) * QB], in0=PS[0][:, 0:512].rearrange("p (m t) -> p m t", m=2),
                scalar=gqk_s[:, 0:1], in1=rst[:, 0:512].rearrange("p (m t) -> p m t", m=2), op0=ALU.mult, op1=ALU.mult),
                reads=[Tb[0], T_rst, T_g], writes=[T_KT[j]])
            P.op("vector", lambda e: e.scalar_tensor_tensor(
                out=QTb[jb][:], in0=PS[0][:, 512:1024].rearrange("p (m t) -> p m t", m=2),
                scalar=gqk_s[:, 1:2], in1=rst[:, 512:1024].rearrange("p (m t) -> p m t", m=2), op0=ALU.mult, op1=ALU.mult),
                reads=[Tb[1], T_rst, T_g], writes=[T_QT[jb]])
            def vz_tile(ti):
                tt = 2 * j + ti
                zi = jb * 2 + ti
                P.op("scalar", lambda e, ti=ti, tt=tt: e.copy(out=V[:, tt, 0:256], in_=PS[1][:, ti * 512:ti * 512 + 256]),
                     reads=[Tb[2 + ti]], writes=[T_V[tt]])
                P.op("scalar", lambda e, ti=ti: e.activation(out=th[ti][:], in_=PS[1][:, ti * 512 + 256:(ti + 1) * 512],
                                                             func=ACT.Exp, scale=-1.0),
                     reads=[Tb[2 + ti]], writes=[T_th[ti]])
                P.op("gpsimd", lambda e, ti=ti: e.tensor_scalar_add(out=th[ti][:], in0=th[ti][:], scalar1=1.0),
                     reads=[T_th[ti]], writes=[T_th[ti]])
                P.op("vector", lambda e, ti=ti: e.reciprocal(out=th[ti][:], in_=th[ti][:]),
                     reads=[T_th[ti]], writes=[T_th[ti]])
                P.op("vector", lambda e, ti=ti, zi=zi: e.tensor_tensor(
                    out=Zg[zi][:], in0=th[ti][:], in1=PS[1][:, ti * 512 + 256:(ti + 1) * 512], op=ALU.mult),
                    reads=[T_th[ti], Tb[2 + ti]], writes=[T_Zg[zi]])

            for ti in range(2):
                vz_tile(ti)

            npair = j + 1
            def do_pair(p):
                sp = p % 2
                pb = p % 3
                last = (p == npair - 1)
                for kl in range(2):
                    kt = 2 * p + kl
                    for m in range(2):
                        P.op("tensor", lambda e, kt=kt, kl=kl, m=m, sp=sp: e.matmul(
                            PS[sp][:, (kl * 2 + m) * 256:(kl * 2 + m + 1) * 256],
                            lhsT=KT[:, m, kt * 128:(kt + 1) * 128], rhs=QTb[jb][:, m, :], start=True, stop=True),
                            reads=[T_KT[kt // 2], T_QT[jb]], writes=[Tb[2 * sp + kl]])
                P.op("scalar", lambda e, sp=sp, pb=pb: e.activation(
                    out=PT[pb][:].rearrange("p a q -> p (a q)"), in_=PS[sp][:], func=ACT.Exp),
                    reads=[Tb[2 * sp], Tb[2 * sp + 1]], writes=[T_PT[pb]])
                if last:
                    P.op("gpsimd", lambda e, pb=pb: e.memset(PT[pb][64:128, 0:2, 0:64], 0.0), writes=[T_PT[pb]])
                    P.op("gpsimd", lambda e, pb=pb: e.memset(PT[pb][64:128, 2:4, 128:192], 0.0), writes=[T_PT[pb]])
                for kl in range(2):
                    kt = 2 * p + kl
                    for qt in range(2):
                        if last and kl == 1 and qt == 0:
                            continue
                        for m in range(2):
                            P.op("tensor", lambda e, kt=kt, kl=kl, m=m, qt=qt, pb=pb: e.matmul(
                                PS[2 + qt][:, m * 512:m * 512 + 257],
                                lhsT=PT[pb][:, kl * 2 + m, qt * 128:(qt + 1) * 128], rhs=V[:, kt, :],
                                start=(kt == 0), stop=(kt == 2 * j + qt)),
                                reads=[T_PT[pb], T_V[kt]], writes=[Tb[4 + 2 * qt + m]])

            for p in range(npair):
                do_pair(p)

            def epilogue(qt):
                tt = 2 * j + qt
                zi = jb * 2 + qt
                O1 = PS[2 + qt][:, 0:257]
                O2 = PS[2 + qt][:, 512:769]
                b1, b2 = Tb[4 + 2 * qt], Tb[4 + 2 * qt + 1]
                sm = small[qt]
                P.op("vector", lambda e, O1=O1, sm=sm: e.reciprocal(out=sm[:, 0:1], in_=O1[:, 256:257]),
                     reads=[b1], writes=[T_e[qt]])
                P.op("vector", lambda e, O2=O2, sm=sm: e.reciprocal(out=sm[:, 1:2], in_=O2[:, 256:257]),
                     reads=[b2], writes=[T_e[qt]])
                P.op("vector", lambda e, sm=sm: e.tensor_tensor(out=sm[:, 2:3], in0=sm[:, 1:2], in1=nlam, op=ALU.mult),
                     reads=[T_e[qt], T_lam], writes=[T_e[qt]])
                P.op("vector", lambda e, O1=O1, sm=sm, qt=qt: e.tensor_scalar_mul(out=ea[qt][:], in0=O1[:, 0:256], scalar1=sm[:, 0:1]),
                     reads=[b1, T_e[qt]], writes=[T_e[qt]])
                P.op("vector", lambda e, O2=O2, sm=sm, qt=qt: e.scalar_tensor_tensor(
                    out=eo[qt][:], in0=O2[:, 0:256], scalar=sm[:, 2:3], in1=ea[qt][:], op0=ALU.mult, op1=ALU.add),
                    reads=[b2, T_e[qt]], writes=[T_e[qt]])
                P.op("scalar", lambda e, qt=qt: e.activation(out=esq[qt][:], in_=eo[qt][:], func=ACT.Square),
                     reads=[T_e[qt]], writes=[T_e[qt]])
                P.op("vector", lambda e, sm=sm, qt=qt: e.reduce_sum(out=sm[:, 3:4], in_=esq[qt][:], axis=AX.X),
                     reads=[T_e[qt]], writes=[T_e[qt]])
                P.op("scalar", lambda e, sm=sm: e.activation(out=sm[:, 4:5], in_=sm[:, 3:4], func=ACT.Ln, bias=256.0 * EPS),
                     reads=[T_e[qt]], writes=[T_e[qt]])
                P.op("scalar", lambda e, sm=sm: e.activation(out=sm[:, 4:5], in_=sm[:, 4:5], func=ACT.Exp, scale=-0.5),
                     reads=[T_e[qt]], writes=[T_e[qt]])
                P.op("vector", lambda e, sm=sm, qt=qt: e.scalar_tensor_tensor(
                    out=eof[qt][:], in0=eo[qt][:], scalar=sm[:, 4:5], in1=gsub_s[:], op0=ALU.mult, op1=ALU.mult),
                    reads=[T_e[qt], T_g], writes=[T_e[qt]])
                P.op("gpsimd", lambda e, qt=qt, zi=zi: e.tensor_tensor(out=ob[zi][:], in0=eof[qt][:], in1=Zg[zi][:], op=ALU.mult),
                     reads=[T_e[qt], T_Zg[zi]], writes=[T_ob[zi]])
                P.op("sync", lambda e, tt=tt, zi=zi: e.dma_start(out=o_out[tt * 128:(tt + 1) * 128, :], in_=ob[zi][:]),
                     reads=[T_ob[zi]], writes=[T_out], dma="ob%d" % zi)
            for qt in range(2):
                epilogue(qt)

        for j in range(nblk):
            do_block(j)

        if dbg:
            dk = nc.dram_tensor("d_KT", [128, 2, s_len], BF16, kind="ExternalOutput").ap()
            dv = nc.dram_tensor("d_V", [128, ntile, 257], BF16, kind="ExternalOutput").ap()
            dq = nc.dram_tensor("d_QT", [128, 2, QB], BF16, kind="ExternalOutput").ap()
            dr = nc.dram_tensor("d_rst", [128, 1024], F32, kind="ExternalOutput").ap()
            dz = nc.dram_tensor("d_Zg", [128, 256], BF16, kind="ExternalOutput").ap()
            dp = nc.dram_tensor("d_PT", [128, 4, QB], BF16, kind="ExternalOutput").ap()
            de = nc.dram_tensor("d_eo", [128, 256], F32, kind="ExternalOutput").ap()
            dea = nc.dram_tensor("d_ea", [128, 256], F32, kind="ExternalOutput").ap()
            dsm = nc.dram_tensor("d_sm", [128, 8], F32, kind="ExternalOutput").ap()
            dl = nc.dram_tensor("d_lam", [128, 4], F32, kind="ExternalOutput").ap()
            dw = nc.dram_tensor("d_wA", [128, 16, 512], BF16, kind="ExternalOutput").ap()
            allT = T_KT + T_V + T_QT + [T_rst] + T_Zg + T_PT + T_e + [T_lam, T_wA]
            for (dst, src) in ((dk, KT), (dv, V), (dq, QTb[(nblk - 1) % 2]), (dr, rst), (dz, Zg[((nblk - 1) % 2) * 2]),
                               (dp, PT[(nblk - 1) % 3]), (de, eo[0]), (dea, ea[0]), (dsm, small[0]), (dl, lam_v), (dw, wAb)):
                P.op("sync", lambda e, dst=dst, src=src: e.dma_start(out=dst, in_=src[:]), reads=allT, writes=[T_out], dma="dbg")
        P.emit()
    return nc


def l2_inputs(xnT_full, w_kv, b_w_in, kv_norm_g, b_norm_g, k_norm_g, q_norm_g, lam, subln_g, s_len=S):
    QB = 256
    nblk = s_len // QB
    xb = np.ascontiguousarray(
        xnT_full.reshape(16, 128, nblk, QB).transpose(2, 1, 0, 3))

    def wl(w):
        return np.ascontiguousarray(w.reshape(16, 128, w.shape[1]).transpose(1, 0, 2))

    gcol = np.ascontiguousarray(np.concatenate(
        [kv_norm_g.reshape(16, 128).T, b_norm_g.reshape(16, 128).T], axis=1)).astype(np.float32)
    gqk = np.ascontiguousarray(np.stack([k_norm_g, q_norm_g], axis=1)).astype(np.float32)
    gsub = np.ascontiguousarray(np.broadcast_to(subln_g.reshape(1, 256), (128, 256))).astype(np.float32)
    lamp = np.ascontiguousarray(np.broadcast_to(lam.reshape(1, 512), (128, 512))).astype(np.float32)
    maps = []
    for c in range(NCORES):
        kc = w_kv[:, c * 256:(c + 1) * 256]
        vc = w_kv[:, 2048 + c * 256:2048 + (c + 1) * 256]
        qc = b_w_in[:, c * 256:(c + 1) * 256]
        zc = b_w_in[:, 2048 + c * 256:2048 + (c + 1) * 256]
        maps.append({
            "xnT": xb,
            "wA": wl(np.concatenate([kc, qc], axis=1)),
            "wB": wl(np.concatenate([vc, zc], axis=1)),
            "gcol": gcol, "gqk": gqk, "gsub": gsub, "lamp": lamp,
        })
    return maps


def build_l3(ntok=2048):
    ntt = ntok // 128
    nc = bass.Bass("TRN2", target_bir_lowering=False)
    oT = nc.dram_tensor("oT", [ntt, 128, 16, 128], BF16, kind="ExternalInput").ap()
    x1 = nc.dram_tensor("x1", [ntok, D], F32, kind="ExternalInput").ap()
    wo = nc.dram_tensor("wo", [128, 16, D], F32, kind="ExternalInput").ap()
    y = nc.dram_tensor("y", [ntok, D], F32, kind="ExternalOutput").ap()
    P = Prog(nc)
    with contextlib.ExitStack() as es:
        def sb(name, shape, dt):
            return es.enter_context(nc.sbuf_tensor(name, shape, dt))
        wob = sb("wob", [128, 16, D], BF16)
        ot = [sb("ot%d" % i, [128, 16, 128], BF16) for i in range(2)]
        xt = [sb("xt%d" % i, [128, D], F32) for i in range(2)]
        yt = [sb("yt%d" % i, [128, D], F32) for i in range(2)]
        PS = [es.enter_context(nc.psum_tensor("ps%d" % i, [128, 2048], F32)) for i in range(2)]
        T_w = [T() for _ in range(16)]
        T_ot, T_xt, T_yt, T_ps = [T(), T()], [T(), T()], [T(), T()], [T(), T()]
        T_y = T()
        for ct in range(16):
            P.op("gpsimd", lambda e, ct=ct: e.dma_start(out=wob[:, ct, :], in_=wo[:, ct, :]), writes=[T_w[ct]], dma="w%d" % (ct % 4))

        def tile(tt):
            b = tt % 2
            P.op("sync", lambda e: e.dma_start(out=ot[b][:], in_=oT[tt]), writes=[T_ot[b]], dma="ot%d" % b)
            P.op("sync", lambda e: e.dma_start(out=xt[b][:], in_=x1[tt * 128:(tt + 1) * 128, :]), writes=[T_xt[b]], dma="xt%d" % b)
            for fb in range(4):
                for ct in range(16):
                    P.op("tensor", lambda e, fb=fb, ct=ct: e.matmul(
                        PS[b][:, fb * 512:(fb + 1) * 512], lhsT=ot[b][:, ct, :], rhs=wob[:, ct, fb * 512:(fb + 1) * 512],
                        start=(ct == 0), stop=(ct == 15)),
                        reads=[T_ot[b], T_w[ct]], writes=[T_ps[b]])
            P.op("vector", lambda e: e.tensor_tensor(out=yt[b][:], in0=PS[b][:], in1=xt[b][:], op=ALU.add),
                 reads=[T_ps[b], T_xt[b]], writes=[T_yt[b]])
            P.op("sync", lambda e: e.dma_start(out=y[tt * 128:(tt + 1) * 128, :], in_=yt[b][:]),
                 reads=[T_yt[b]], writes=[T_y], dma="yt%d" % b)

        for tt in range(ntt):
            tile(tt)
        P.emit()
    return nc


def build_l1(ntok=2048, NP=1024):
    npass = ntok // NP
    NB = NP // 512
    NL = NP + HALO
    nc = bass.Bass("TRN2", target_bir_lowering=False)
    x = nc.dram_tensor("x", [ntok + HALO, D], F32, kind="ExternalInput").ap()
    w_in = nc.dram_tensor("w_in", [48, 128, 16, 128], F32, kind="ExternalInput").ap()
    w_out = nc.dram_tensor("w_out", [128, 16, D], F32, kind="ExternalInput").ap()
    gbc_d = nc.dram_tensor("gbc", [128, D], F32, kind="ExternalInput").ap()
    dww_d = nc.dram_tensor("dww", [128, 16, CONV_W], F32, kind="ExternalInput").ap()
    cvec_d = nc.dram_tensor("cvec", [128, 48], F32, kind="ExternalInput").ap()
    identb_d = nc.dram_tensor("identb", [128, 128], BF16, kind="ExternalInput").ap()
    identf_d = nc.dram_tensor("identf", [128, 128], F32, kind="ExternalInput").ap()
    x1_o = nc.dram_tensor("x1", [ntok, D], F32, kind="ExternalOutput").ap()
    xnT_o = nc.dram_tensor("xnT", [ntok // 256, 128, 16, 256], BF16, kind="ExternalOutput").ap()

    P = Prog(nc)
    with contextlib.ExitStack() as es:
        def sb(name, shape, dt):
            return es.enter_context(nc.sbuf_tensor(name, shape, dt))

        HT_N = 16 * NL
        ARENA = max(HT_N + 6 * 2048 + 2 * CONV_W * 128, 16 * D)
        arena = sb("arena", [128, ARENA], BF16)
        hT = arena[:, 0:HT_N].rearrange("p (f t) -> p f t", f=16)
        wt = [arena[:, HT_N + i * 2048: HT_N + (i + 1) * 2048].rearrange("p (f n) -> p f n", f=16) for i in range(6)]
        wa, wb_, wz = wt[0:2], wt[2:4], wt[4:6]
        dgo = HT_N + 6 * 2048
        dg = [arena[:, dgo + i * CONV_W * 128: dgo + (i + 1) * CONV_W * 128].rearrange("p (j n) -> p j n", j=CONV_W) for i in range(2)]
        wob = arena[:, 0:16 * D].rearrange("p (c n) -> p c n", c=16)
        cvT = sb("cvT", [128, 16, NP], BF16)
        yT = [sb("yT%d" % i, [128, NL], BF16) for i in range(2)]
        acc_s = sb("acc_s", [128, NP], F32)
        acc_q = sb("acc_q", [128, NP], F32)
        rstd_bc = sb("rstd_bc", [128, NP], F32)
        nmr_bc = sb("nmr_bc", [128, NP], F32)
        xt = [sb("xt%d" % i, [128, D], F32) for i in range(2)]
        sqx = sb("sqx", [128, D], F32)
        tmp = {k: [sb("%s%d" % (k, i), [128, 512], F32) for i in range(2)] for k in ("ta", "tb", "tc", "td", "te", "tf")}
        xnblk = sb("xnblk", [128, 16, 256], BF16)
        gbc = sb("gbc_s", [128, D], F32)
        dww = sb("dww_s", [128, 16, CONV_W], F32)
        cvec = sb("cvec_s", [128, 48], F32)
        identb = sb("identb_s", [128, 128], BF16)
        identf = sb("identf_s", [128, 128], F32)
        onesf = sb("onesf", [128, 128], F32)
        small = sb("small", [128, 8], F32)
        B = [es.enter_context(nc.psum_tensor("b%d" % i, [128, 512], F32)) for i in range(8)]

        Tb = [T("bank%d" % i) for i in range(8)]
        T_c = T()
        T_hT = T()
        T_wa, T_wb, T_wz, T_dg = [T(), T()], [T(), T()], [T(), T()], [T(), T()]
        T_cv = [T() for _ in range(16)]
        T_yT = [T(), T()]
        T_acc, T_st = T(), T()
        T_xt = [T(), T()]
        T_sqx = T()
        T_tmp = {k: [T(), T()] for k in tmp}
        T_xnblk = T()
        T_sm = T()
        T_x1o, T_xno = T(), T()
        T_wob = T()

        dwb = lambda c: cvec[:, c:c + 1]
        lng = lambda c: cvec[:, 16 + c:17 + c]
        lnb = lambda c: cvec[:, 32 + c:33 + c]

        for (dst, src) in ((gbc, gbc_d), (dww, dww_d), (cvec, cvec_d), (identb, identb_d), (identf, identf_d)):
            P.op("sync", lambda e, dst=dst, src=src: e.dma_start(out=dst[:], in_=src), writes=[T_c], dma="c0")
        P.op("gpsimd", lambda e: e.memset(onesf[:], 1.0), writes=[T_c])

        cnt = {"xt": 0}

        def rstd_from_ss(np_, col_in, col_out, n):
            P.op("scalar", lambda e: e.activation(out=small[0:np_, col_out:col_out + 1], in_=small[0:np_, col_in:col_in + 1],
                                                  func=ACT.Ln, scale=1.0 / n, bias=EPS), reads=[T_sm], writes=[T_sm])
            P.op("scalar", lambda e: e.activation(out=small[0:np_, col_out:col_out + 1], in_=small[0:np_, col_out:col_out + 1],
                                                  func=ACT.Exp, scale=-0.5), reads=[T_sm], writes=[T_sm])

        def norm_tile(row0, np_, col0):
            b = cnt["xt"] % 2
            cnt["xt"] += 1
            P.op("sync", lambda e: e.dma_start(out=xt[b][0:np_, :], in_=x[row0:row0 + np_, :]), writes=[T_xt[b]], dma="xt%d" % b)
            P.op("scalar", lambda e: e.activation(out=sqx[0:np_, :], in_=xt[b][0:np_, :], func=ACT.Square),
                 reads=[T_xt[b]], writes=[T_sqx])
            P.op("vector", lambda e: e.reduce_sum(out=small[0:np_, 0:1], in_=sqx[0:np_, :], axis=AX.X),
                 reads=[T_sqx], writes=[T_sm])
            rstd_from_ss(np_, 0, 1, float(D))
            P.op("gpsimd", lambda e: e.tensor_scalar_mul(out=sqx[0:np_, :], in0=xt[b][0:np_, :], scalar1=small[0:np_, 1:2]),
                 reads=[T_xt[b], T_sm], writes=[T_sqx])
            for ft in range(16):
                P.op("tensor", lambda e, ft=ft: e.transpose(
                    B[ft // 4][:, (ft % 4) * 128:(ft % 4) * 128 + np_], sqx[0:np_, ft * 128:(ft + 1) * 128], identf[0:np_, 0:np_]),
                    reads=[T_sqx, T_c], writes=[Tb[ft // 4]])
            for q in range(4):
                P.op("vector", lambda e, q=q: e.tensor_tensor(
                    out=hT[:, q * 4:(q + 1) * 4, col0:col0 + np_],
                    in0=B[q][:].rearrange("p (f t) -> p f t", f=4)[:, :, 0:np_],
                    in1=gbc[:, q * 512:(q + 1) * 512].rearrange("p (f t) -> p f t", f=4)[:, :, 0:np_], op=ALU.mult),
                    reads=[Tb[q], T_c], writes=[T_hT])

        def do_pass(h):
            r_h = h * NP
            r_o = HALO + h * NP
            norm_tile(r_h, HALO, 0)
            for t in range(NP // 128):
                norm_tile(r_o + t * 128, 128, HALO + t * 128)
            P.op("gpsimd", lambda e: e.memset(acc_s[:], 0.0), writes=[T_acc])
            P.op("gpsimd", lambda e: e.memset(acc_q[:], 0.0), writes=[T_acc])

            def prefetch1(c):
                b = c % 2
                P.op("gpsimd", lambda e: e.dma_start(out=wa[b], in_=w_in[c]), writes=[T_wa[b]], dma="wa%d" % b)
                P.op("gpsimd", lambda e: e.dma_start(out=wb_[b], in_=w_in[16 + c]), writes=[T_wb[b]], dma="wb%d" % b)
                for j in range(CONV_W):
                    P.op("gpsimd", lambda e, j=j: e.tensor_scalar_mul(out=dg[b][:, j, :], in0=identb[:], scalar1=dww[:, c, j:j + 1]),
                         reads=[T_c], writes=[T_dg[b]])

            def stage2(c):
                b = c % 2
                yt_ = yT[b]
                for (w_, T_w, off) in ((wa[b], T_wa[b], 0), (wb_[b], T_wb[b], 32)):
                    for ft in range(16):
                        P.op("tensor", lambda e, w_=w_, off=off, ft=ft: e.matmul(
                            B[6][:, off:off + HALO], lhsT=w_[:, ft, :], rhs=hT[:, ft, 0:HALO], start=(ft == 0), stop=(ft == 15)),
                            reads=[T_w, T_hT], writes=[Tb[6]])
                P.op("scalar", lambda e: e.activation(out=tmp["ta"][0][:, 0:HALO], in_=B[6][:, 32:32 + HALO], func=ACT.Sigmoid),
                     reads=[Tb[6]], writes=[T_tmp["ta"][0]])
                P.op("vector", lambda e: e.tensor_tensor(out=yt_[:, 0:HALO], in0=B[6][:, 0:HALO], in1=tmp["ta"][0][:, 0:HALO], op=ALU.mult),
                     reads=[Tb[6], T_tmp["ta"][0]], writes=[T_yT[b]])
                for tb in range(NB):
                    pa, pb = B[2 * (tb % 2)], B[2 * (tb % 2) + 1]
                    Ta, Tbb = Tb[2 * (tb % 2)], Tb[2 * (tb % 2) + 1]
                    c0 = HALO + tb * 512
                    for (w_, T_w, ps, Tp) in ((wa[b], T_wa[b], pa, Ta), (wb_[b], T_wb[b], pb, Tbb)):
                        for ft in range(16):
                            P.op("tensor", lambda e, w_=w_, ps=ps, ft=ft, c0=c0: e.matmul(
                                ps[:], lhsT=w_[:, ft, :], rhs=hT[:, ft, c0:c0 + 512], start=(ft == 0), stop=(ft == 15)),
                                reads=[T_w, T_hT], writes=[Tp])
                    sg = tmp["ta"][tb % 2]
                    P.op("scalar", lambda e, pb=pb, sg=sg: e.activation(out=sg[:], in_=pb[:], func=ACT.Sigmoid),
                         reads=[Tbb], writes=[T_tmp["ta"][tb % 2]])
                    P.op("vector", lambda e, pa=pa, sg=sg, c0=c0: e.tensor_tensor(out=yt_[:, c0:c0 + 512], in0=pa[:], in1=sg[:], op=ALU.mult),
                         reads=[Ta, T_tmp["ta"][tb % 2]], writes=[T_yT[b]])
                for tb in range(NB):
                    pc, Tc_ = B[4 + tb % 2], Tb[4 + tb % 2]
                    for j in range(CONV_W):
                        P.op("tensor", lambda e, pc=pc, j=j, tb=tb: e.matmul(
                            pc[:], lhsT=dg[b][:, j, :], rhs=yt_[:, tb * 512 + j: tb * 512 + j + 512],
                            start=(j == 0), stop=(j == CONV_W - 1)),
                            reads=[T_dg[b], T_yT[b]], writes=[Tc_])
                    blk = slice(tb * 512, (tb + 1) * 512)
                    sq_ = tmp["tb"][tb % 2]
                    P.op("scalar", lambda e, pc=pc, blk=blk: e.activation(out=cvT[:, c, blk], in_=pc[:], func=ACT.Identity, bias=dwb(c)),
                         reads=[Tc_, T_c], writes=[T_cv[c]])
                    P.op("scalar", lambda e, pc=pc, sq_=sq_: e.activation(out=sq_[:], in_=pc[:], func=ACT.Square, bias=dwb(c)),
                         reads=[Tc_, T_c], writes=[T_tmp["tb"][tb % 2]])
                    P.op("gpsimd", lambda e, blk=blk: e.tensor_tensor(out=acc_s[:, blk], in0=acc_s[:, blk], in1=cvT[:, c, blk], op=ALU.add),
                         reads=[T_cv[c], T_acc], writes=[T_acc])
                    P.op("gpsimd", lambda e, blk=blk, sq_=sq_: e.tensor_tensor(out=acc_q[:, blk], in0=acc_q[:, blk], in1=sq_[:], op=ALU.add),
                         reads=[T_tmp["tb"][tb % 2], T_acc], writes=[T_acc])

            prefetch1(0)
            for c in range(16):
                if c + 1 < 16:
                    prefetch1(c + 1)
                stage2(c)

            def prefetch_z(c):
                b = c % 2
                P.op("gpsimd", lambda e: e.dma_start(out=wz[b], in_=w_in[32 + c]), writes=[T_wz[b]], dma="wz%d" % b)
            prefetch_z(0)
            for tb in range(NB):
                blk = slice(tb * 512, (tb + 1) * 512)
                P.op("tensor", lambda e, blk=blk: e.matmul(B[0][:], lhsT=onesf[:], rhs=acc_s[:, blk], start=True, stop=True),
                     reads=[T_c, T_acc], writes=[Tb[0]])
                P.op("tensor", lambda e, blk=blk: e.matmul(B[1][:], lhsT=onesf[:], rhs=acc_q[:, blk], start=True, stop=True),
                     reads=[T_c, T_acc], writes=[Tb[1]])
                mean, m2 = tmp["tc"][0], tmp["td"][0]
                P.op("vector", lambda e: e.tensor_scalar_mul(out=mean[:], in0=B[0][:], scalar1=1.0 / D),
                     reads=[Tb[0]], writes=[T_tmp["tc"][0]])
                P.op("vector", lambda e: e.tensor_tensor(out=m2[:], in0=mean[:], in1=mean[:], op=ALU.mult),
                     reads=[T_tmp["tc"][0]], writes=[T_tmp["td"][0]])
                P.op("vector", lambda e: e.scalar_tensor_tensor(out=m2[:], in0=B[1][:], scalar=1.0 / D, in1=m2[:],
                                                                op0=ALU.mult, op1=ALU.subtract),
                     reads=[Tb[1], T_tmp["td"][0]], writes=[T_tmp["td"][0]])
                P.op("scalar", lambda e, blk=blk: e.activation(out=rstd_bc[:, blk], in_=m2[:], func=ACT.Ln, bias=EPS),
                     reads=[T_tmp["td"][0]], writes=[T_st])
                P.op("scalar", lambda e, blk=blk: e.activation(out=rstd_bc[:, blk], in_=rstd_bc[:, blk], func=ACT.Exp, scale=-0.5),
                     reads=[T_st], writes=[T_st])
                P.op("vector", lambda e, blk=blk: e.scalar_tensor_tensor(out=nmr_bc[:, blk], in0=mean[:], scalar=-1.0, in1=rstd_bc[:, blk],
                                                                         op0=ALU.mult, op1=ALU.mult),
                     reads=[T_tmp["tc"][0], T_st], writes=[T_st])

            def stage4(c):
                b = c % 2
                for tb in range(NB):
                    k = tb % 2
                    pz, Tz = B[2 + k], Tb[2 + k]
                    c0 = HALO + tb * 512
                    blk = slice(tb * 512, (tb + 1) * 512)
                    for ft in range(16):
                        P.op("tensor", lambda e, pz=pz, ft=ft, c0=c0: e.matmul(
                            pz[:], lhsT=wz[b][:, ft, :], rhs=hT[:, ft, c0:c0 + 512], start=(ft == 0), stop=(ft == 15)),
                            reads=[T_wz[b], T_hT], writes=[Tz])
                    sz, gz, t1, s2, l_, u_ = (tmp[n][k] for n in ("ta", "tb", "tc", "td", "te", "tf"))
                    Ts = {n: T_tmp[n][k] for n in ("ta", "tb", "tc", "td", "te", "tf")}
                    P.op("scalar", lambda e, pz=pz, sz=sz: e.activation(out=sz[:], in_=pz[:], func=ACT.Sigmoid),
                         reads=[Tz], writes=[Ts["ta"]])
                    P.op("vector", lambda e, pz=pz, sz=sz, gz=gz: e.tensor_tensor(out=gz[:], in0=pz[:], in1=sz[:], op=ALU.mult),
                         reads=[Tz, Ts["ta"]], writes=[Ts["tb"]])
                    P.op("gpsimd", lambda e, t1=t1, blk=blk: e.tensor_tensor(out=t1[:], in0=cvT[:, c, blk], in1=rstd_bc[:, blk], op=ALU.mult),
                         reads=[T_cv[c], T_st], writes=[Ts["tc"]])
                    P.op("gpsimd", lambda e, t1=t1, blk=blk: e.tensor_tensor(out=t1[:], in0=t1[:], in1=nmr_bc[:, blk], op=ALU.add),
                         reads=[Ts["tc"], T_st], writes=[Ts["tc"]])
                    P.op("scalar", lambda e, t1=t1, s2=s2: e.activation(out=s2[:], in_=t1[:], func=ACT.Sigmoid, scale=lng(c), bias=lnb(c)),
                         reads=[Ts["tc"], T_c], writes=[Ts["td"]])
                    P.op("vector", lambda e, t1=t1, l_=l_: e.tensor_scalar(out=l_[:], in0=t1[:], scalar1=lng(c), scalar2=lnb(c),
                                                                          op0=ALU.mult, op1=ALU.add),
                         reads=[Ts["tc"], T_c], writes=[Ts["te"]])
                    P.op("vector", lambda e, l_=l_, s2=s2, u_=u_: e.tensor_tensor(out=u_[:], in0=l_[:], in1=s2[:], op=ALU.mult),
                         reads=[Ts["te"], Ts["td"]], writes=[Ts["tf"]])
                    P.op("vector", lambda e, u_=u_, gz=gz, blk=blk: e.tensor_tensor(out=cvT[:, c, blk], in0=u_[:], in1=gz[:], op=ALU.mult),
                         reads=[Ts["tf"], Ts["tb"]], writes=[T_cv[c]])

            for c in range(16):
                if c + 1 < 16:
                    prefetch_z(c + 1)
                stage4(c)

            alias = [T_hT] + T_wa + T_wb + T_wz + T_dg
            for ct in range(16):
                P.op("gpsimd", lambda e, ct=ct: e.dma_start(out=wob[:, ct, :], in_=w_out[:, ct, :]), writes=[T_wob] + alias,
                     dma="wo%d" % (ct % 4))

            def stage5(t):
                b = cnt["xt"] % 2
                cnt["xt"] += 1
                row = h * NP + t * 128
                P.op("sync", lambda e: e.dma_start(out=xt[b][:], in_=x[HALO + row:HALO + row + 128, :]), writes=[T_xt[b]], dma="xt%d" % b)
                for fb in range(4):
                    for ct in range(16):
                        P.op("tensor", lambda e, fb=fb, ct=ct: e.matmul(
                            B[fb][:], lhsT=cvT[:, ct, t * 128:(t + 1) * 128], rhs=wob[:, ct, fb * 512:(fb + 1) * 512],
                            start=(ct == 0), stop=(ct == 15)),
                            reads=[T_cv[ct], T_wob], writes=[Tb[fb]])
                for fb in range(4):
                    P.op("vector", lambda e, fb=fb: e.tensor_tensor(out=xt[b][:, fb * 512:(fb + 1) * 512], in0=B[fb][:],
                                                                    in1=xt[b][:, fb * 512:(fb + 1) * 512], op=ALU.add),
                         reads=[Tb[fb], T_xt[b]], writes=[T_xt[b]])
                P.op("sync", lambda e: e.dma_start(out=x1_o[row:row + 128, :], in_=xt[b][:]), reads=[T_xt[b]], writes=[T_x1o], dma="x1o%d" % b)
                P.op("scalar", lambda e: e.activation(out=sqx[:], in_=xt[b][:], func=ACT.Square), reads=[T_xt[b]], writes=[T_sqx])
                P.op("vector", lambda e: e.reduce_sum(out=small[:, 0:1], in_=sqx[:], axis=AX.X), reads=[T_sqx], writes=[T_sm])
                rstd_from_ss(128, 0, 1, float(D))
                P.op("gpsimd", lambda e: e.tensor_scalar_mul(out=sqx[:], in0=xt[b][:], scalar1=small[:, 1:2]),
                     reads=[T_xt[b], T_sm], writes=[T_sqx])
                for ft in range(16):
                    P.op("tensor", lambda e, ft=ft: e.transpose(
                        B[4 + ft // 4][:, (ft % 4) * 128:(ft % 4 + 1) * 128], sqx[:, ft * 128:(ft + 1) * 128], identf[:]),
                        reads=[T_sqx, T_c], writes=[Tb[4 + ft // 4]])
                half = t % 2
                for q in range(4):
                    P.op("scalar", lambda e, q=q: e.copy(out=xnblk[:, q * 4:(q + 1) * 4, half * 128:(half + 1) * 128],
                                                         in_=B[4 + q][:].rearrange("p (f t) -> p f t", f=4)),
                         reads=[Tb[4 + q]], writes=[T_xnblk])
                if half == 1:
                    blk_i = (h * NP + t * 128) // 256
                    P.op("sync", lambda e: e.dma_start(out=xnT_o[blk_i], in_=xnblk[:]), reads=[T_xnblk], writes=[T_xno], dma="xno")

            for t in range(NP // 128):
                stage5(t)

        for h in range(npass):
            do_pass(h)
        P.emit()
    return nc


def l1_inputs(x, a_norm_g, a_w_in, a_dw_w, a_dw_b, a_ln_g, a_ln_b, a_w_out, ntok=2048, ncores=NCORES):
    w_in_l = np.ascontiguousarray(a_w_in.reshape(16, 128, 48, 128).transpose(2, 1, 0, 3))
    w_out_l = np.ascontiguousarray(a_w_out.reshape(16, 128, D).transpose(1, 0, 2))
    gbc = np.ascontiguousarray(np.broadcast_to(a_norm_g.reshape(16, 128).T[:, :, None], (128, 16, 128)).reshape(128, D)).astype(np.float32)
    dww = np.ascontiguousarray(a_dw_w.reshape(CONV_W, 16, 128).transpose(2, 1, 0)).astype(np.float32)
    cvec = np.ascontiguousarray(np.concatenate(
        [a_dw_b.reshape(16, 128).T, a_ln_g.reshape(16, 128).T, a_ln_b.reshape(16, 128).T], axis=1)).astype(np.float32)
    identf = np.eye(128, dtype=np.float32)
    identb = identf.astype(ml_dtypes.bfloat16)
    xp = np.concatenate([np.zeros((HALO, D), np.float32), x], axis=0)
    maps = []
    for c in range(ncores):
        maps.append({"x": np.ascontiguousarray(xp[c * ntok: c * ntok + ntok + HALO]), "w_in": w_in_l, "w_out": w_out_l,
                     "gbc": gbc, "dww": dww, "cvec": cvec, "identb": identb, "identf": identf})
    return maps


_CACHE = {}


def _get(name, fn):
    if name not in _CACHE:
        _CACHE[name] = fn()
    return _CACHE[name]


def kernel(x, a_norm_g, a_w_in, a_dw_w, a_dw_b, a_ln_g, a_ln_b, a_w_out,
           kv_norm_g, w_kv, k_norm_g, b_norm_g, b_w_in, b_q_norm_g,
           b_lambda, b_subln_g, b_w_out):
    f = lambda a: np.asarray(a, dtype=np.float32)
    x2 = f(x).reshape(S, D)
    cores = list(range(NCORES))
    ntok = S // NCORES
    nc1 = _get("l1", lambda: build_l1(ntok, 1024))
    m1 = l1_inputs(x2, f(a_norm_g)[0], f(a_w_in)[0], f(a_dw_w)[0], f(a_dw_b)[0], f(a_ln_g)[0], f(a_ln_b)[0], f(a_w_out)[0],
                   ntok=ntok, ncores=NCORES)
    r1 = run_bass_kernel_spmd(nc1, m1, core_ids=cores).results
    x1 = [r1[c]["x1"] for c in cores]
    xnT_blocks = np.concatenate([r1[c]["xnT"] for c in cores], axis=0)
    nc2 = _get("l2", lambda: build_l2(S))
    xnT_full = np.ascontiguousarray(xnT_blocks.transpose(2, 1, 0, 3)).reshape(16, 128, S)
    m2 = l2_inputs(xnT_full, f(w_kv), f(b_w_in)[0], f(kv_norm_g), f(b_norm_g)[0], f(k_norm_g), f(b_q_norm_g)[0],
                   f(b_lambda)[0], f(b_subln_g)[0], s_len=S)
    r2 = run_bass_kernel_spmd(nc2, m2, core_ids=cores).results
    o_full = np.concatenate([r2[c]["o"] for c in cores], axis=1)
    nc3 = _get("l3", lambda: build_l3(ntok))
    wo = np.ascontiguousarray(f(b_w_out)[0].reshape(16, 128, D).transpose(1, 0, 2))
    m3 = []
    for c in cores:
        oc = o_full[c * ntok:(c + 1) * ntok]
        oT = np.ascontiguousarray(oc.reshape(ntok // 128, 128, 16, 128).transpose(0, 3, 2, 1))
        m3.append({"oT": oT, "x1": x1[c], "wo": wo})
    r3 = run_bass_kernel_spmd(nc3, m3, core_ids=cores).results
    out = np.concatenate([r3[c]["y"] for c in cores], axis=0).reshape(1, S, D).astype(np.float32)
    return out
```

```python
import contextlib
import math
import numpy as np
import ml_dtypes
import concourse.bass as bass
import concourse.mybir as mybir
from concourse.bass_utils import run_bass_kernel_spmd

F32 = mybir.dt.float32
BF16 = mybir.dt.bfloat16
ALU = mybir.AluOpType
ACT = mybir.ActivationFunctionType
AX = mybir.AxisListType

NCORES = 8
D = 2048
S = 16384
EPS = 1e-6
CONV_W = 31
HALO = CONV_W - 1
DH = 128
LAM_INIT = 0.8 - 0.6 * math.exp(-0.3 * (2 - 1))

ENGS = ("tensor", "vector", "scalar", "gpsimd", "sync")


class T:
    __slots__ = ("name", "last_w", "readers")

    def __init__(self, name=""):
        self.name = name
        self.last_w = None
        self.readers = []


class Prog:
    def __init__(self, nc):
        self.nc = nc
        self.ops = []
        self.groups = {}

    def op(self, eng, fn, reads=(), writes=(), dma=None, inc=16):
        i = len(self.ops)
        deps = set()
        for t in reads:
            if t.last_w is not None:
                deps.add(t.last_w)
        for t in writes:
            if t.last_w is not None:
                deps.add(t.last_w)
            deps.update(t.readers)
        deps.discard(i)
        for t in reads:
            t.readers.append(i)
        for t in writes:
            t.last_w = i
            t.readers = []
        gneed = {}
        for jd in deps:
            g = self.ops[jd]["dma"]
            if g is not None:
                gneed[g] = self.groups[g]
        o = dict(i=i, eng=eng, fn=fn, deps=deps, dma=dma, sig=False, sidx=0, inc=inc, gneed=gneed)
        if dma is not None:
            self.groups[dma] = self.groups.get(dma, 0) + inc
            o["sidx"] = self.groups[dma]
        self.ops.append(o)
        return i

    def emit(self, final_wait_engine="sync"):
        nc = self.nc
        ops = self.ops
        for o in ops:
            for j in o["deps"]:
                d = ops[j]
                if d["dma"] is not None:
                    continue
                if d["eng"] == "tensor" and o["eng"] == "tensor" and o["dma"] is None:
                    continue
                d["sig"] = True
        cnt = {e: 0 for e in ENGS}
        for o in ops:
            if o["dma"] is None and o["sig"]:
                cnt[o["eng"]] += 1
                o["sidx"] = cnt[o["eng"]]
        with contextlib.ExitStack() as es:
            esem = {e: es.enter_context(nc.semaphore("p_" + e)) for e in ENGS}
            gsem = {g: es.enter_context(nc.semaphore("g_%d" % k))
                    for k, g in enumerate(self.groups)}
            block = es.enter_context(nc.Block())
            final = dict(self.groups)

            def make(ename):
                def body(eng):
                    waited_e = {e: 0 for e in ENGS}
                    waited_g = {g: 0 for g in self.groups}
                    for o in ops:
                        if o["eng"] != ename:
                            continue
                        need_e = {}
                        need_g = o["gneed"]
                        for j in o["deps"]:
                            d = ops[j]
                            if d["dma"] is not None:
                                continue
                            else:
                                if d["eng"] == "tensor" and ename == "tensor" and o["dma"] is None:
                                    continue
                                need_e[d["eng"]] = max(need_e.get(d["eng"], 0), d["sidx"])
                        for e, v in need_e.items():
                            if v > waited_e[e]:
                                eng.wait_ge(esem[e], v)
                                waited_e[e] = v
                        for g, v in need_g.items():
                            if v > waited_g[g]:
                                eng.wait_ge(gsem[g], v)
                                waited_g[g] = v
                        ins = o["fn"](eng)
                        if o["dma"] is not None:
                            ins.then_inc(gsem[o["dma"]], o["inc"])
                        elif o["sig"]:
                            ins.then_inc(esem[ename], 1)
                    if ename == final_wait_engine:
                        for g, v in final.items():
                            if v > waited_g[g]:
                                eng.wait_ge(gsem[g], v)
                        for e in ENGS:
                            if e != ename and cnt[e] > waited_e[e]:
                                eng.wait_ge(esem[e], cnt[e])
                return body

            block.tensor(make("tensor"))
            block.vector(make("vector"))
            block.scalar(make("scalar"))
            block.gpsimd(make("gpsimd"))
            block.sync(make("sync"))


def build_l2(s_len=S, dbg=False):
    QB = 256
    nblk = s_len // QB
    ntile = s_len // 128
    nc = bass.Bass("TRN2", target_bir_lowering=False)
    xnT = nc.dram_tensor("xnT", [nblk, 128, 16, QB], BF16, kind="ExternalInput").ap()
    wA = nc.dram_tensor("wA", [128, 16, 512], F32, kind="ExternalInput").ap()
    wB = nc.dram_tensor("wB", [128, 16, 512], F32, kind="ExternalInput").ap()
    gcol = nc.dram_tensor("gcol", [128, 32], F32, kind="ExternalInput").ap()
    gqk = nc.dram_tensor("gqk", [128, 2], F32, kind="ExternalInput").ap()
    gsub = nc.dram_tensor("gsub", [128, 256], F32, kind="ExternalInput").ap()
    lamp = nc.dram_tensor("lamp", [128, 512], F32, kind="ExternalInput").ap()
    o_out = nc.dram_tensor("o", [s_len, 256], BF16, kind="ExternalOutput").ap()

    P = Prog(nc)
    with contextlib.ExitStack() as es:
        def sb(name, shape, dt):
            return es.enter_context(nc.sbuf_tensor(name, shape, dt))

        KT = sb("KT", [128, 2, s_len], BF16)
        V = sb("V", [128, ntile, 257], BF16)
        wAb = sb("wAb", [128, 16, 512], BF16)
        wBb = sb("wBb", [128, 16, 512], BF16)
        xb = [sb("xb%d" % i, [128, 16, QB], BF16) for i in range(2)]
        QTb = [sb("QTb%d" % i, [128, 2, QB], BF16) for i in range(2)]
        PT = [sb("PT%d" % i, [128, 4, QB], BF16) for i in range(3)]
        sq = sb("sq", [128, 1024], BF16)
        rst = sb("rst", [128, 1024], F32)
        th = [sb("th%d" % i, [128, 256], F32) for i in range(2)]
        Zg = [sb("Zg%d" % i, [128, 256], BF16) for i in range(4)]
        ea = [sb("ea%d" % i, [128, 256], F32) for i in range(2)]
        eo = [sb("eo%d" % i, [128, 256], F32) for i in range(2)]
        esq = ea
        eof = ea
        ob = [sb("ob%d" % i, [128, 256], BF16) for i in range(4)]
        small = [sb("small%d" % i, [128, 8], F32) for i in range(2)]
        gcol_s = sb("gcol_s", [128, 32], F32)
        gqk_s = sb("gqk_s", [128, 2], F32)
        gsub_s = sb("gsub_s", [128, 256], F32)
        lam_v = sb("lam_v", [128, 4], F32)
        ones = sb("ones", [128, 128], BF16)
        wst = [sb("wst%d" % i, [128, 1, 512], F32) for i in range(2)]
        PS = [es.enter_context(nc.psum_tensor("ps%d" % i, [128, 1024], F32)) for i in range(4)]

        Tb = [T("bank%d" % i) for i in range(8)]
        T_KT = [T() for _ in range(nblk)]
        T_V = [T() for _ in range(ntile)]
        T_wA, T_wB, T_g, T_lam, T_ones = T(), T(), T(), T(), T()
        T_wst = [T(), T()]
        T_xb = [T(), T()]
        T_QT = [T(), T()]
        T_PT = [T(), T(), T()]
        T_sq, T_rst = T(), T()
        T_th = [T(), T()]
        T_Zg = [T() for _ in range(4)]
        T_e = [T(), T()]
        T_ob = [T() for _ in range(4)]
        T_out = T()

        P.op("sync", lambda e: e.dma_start(out=gcol_s[:], in_=gcol), writes=[T_g], dma="c0")
        P.op("sync", lambda e: e.dma_start(out=gqk_s[:], in_=gqk), writes=[T_g], dma="c0")
        P.op("sync", lambda e: e.dma_start(out=gsub_s[:], in_=gsub), writes=[T_g], dma="c0")
        P.op("sync", lambda e: e.dma_start(out=ea[0][:], in_=lamp[:, 0:256]), writes=[T_lam], dma="c0")
        P.op("sync", lambda e: e.dma_start(out=ea[1][:], in_=lamp[:, 256:512]), writes=[T_lam], dma="c0")
        P.op("gpsimd", lambda e: e.memset(ones[:], 1.0), writes=[T_ones])
        P.op("gpsimd", lambda e: e.memset(V[:, :, 256:257], 1.0), writes=T_V)
        P.op("vector", lambda e: e.tensor_scalar_mul(out=gqk_s[:, 0:1], in0=gqk_s[:, 0:1], scalar1=math.sqrt(128.0)),
             reads=[T_g], writes=[T_g])
        P.op("vector", lambda e: e.tensor_scalar_mul(out=gsub_s[:], in0=gsub_s[:], scalar1=(1.0 - LAM_INIT) * 16.0),
             reads=[T_g], writes=[T_g])
        P.op("vector", lambda e: e.tensor_tensor(out=eo[0][:, 0:128], in0=ea[0][:, 0:128], in1=ea[0][:, 128:256], op=ALU.mult),
             reads=[T_lam], writes=[T_lam])
        P.op("vector", lambda e: e.tensor_tensor(out=eo[0][:, 128:256], in0=ea[1][:, 0:128], in1=ea[1][:, 128:256], op=ALU.mult),
             reads=[T_lam], writes=[T_lam])
        P.op("vector", lambda e: e.reduce_sum(out=lam_v[:, 0:2], in_=eo[0][:].rearrange("p (a b) -> p a b", a=2), axis=AX.X),
             reads=[T_lam], writes=[T_lam, T_e[0], T_e[1]])
        P.op("scalar", lambda e: e.activation(out=lam_v[:, 2:4], in_=lam_v[:, 0:2], func=ACT.Exp),
             reads=[T_lam], writes=[T_lam])
        P.op("vector", lambda e: e.tensor_tensor(out=lam_v[:, 0:1], in0=lam_v[:, 3:4], in1=lam_v[:, 2:3], op=ALU.subtract),
             reads=[T_lam], writes=[T_lam])
        P.op("vector", lambda e: e.tensor_scalar_add(out=lam_v[:, 1:2], in0=lam_v[:, 0:1], scalar1=-LAM_INIT),
             reads=[T_lam], writes=[T_lam])
        nlam = lam_v[:, 1:2]
        k = 0
        for (wsrc, wdst, Tw) in ((wA, wAb, T_wA), (wB, wBb, T_wB)):
            for ch in range(16):
                b = k % 2
                P.op("sync", lambda e, b=b, ch=ch, wsrc=wsrc: e.dma_start(out=wst[b][:], in_=wsrc[:, ch:ch + 1, :]),
                     writes=[T_wst[b]], dma="wst%d" % b)
                for f in range(1):
                    ft = ch + f
                    for half in range(2):
                        if wsrc is wA:
                            gi = ft if half == 0 else 16 + ft
                        else:
                            gi = ft if half == 0 else 16 + ft
                        P.op("vector", lambda e, b=b, f=f, ft=ft, half=half, gi=gi, wdst=wdst: e.tensor_scalar_mul(
                            out=wdst[:, ft, half * 256:(half + 1) * 256], in0=wst[b][:, f, half * 256:(half + 1) * 256],
                            scalar1=gcol_s[:, gi:gi + 1]),
                            reads=[T_wst[b], T_g], writes=[Tw])
                k += 1

        def load_xb(j):
            b = j % 2
            P.op("sync", lambda e: e.dma_start(out=xb[b][:], in_=xnT[j]), writes=[T_xb[b]], dma="xb%d" % b)

        load_xb(0)
        def do_block(j):
            jb = j % 2
            if j + 1 < nblk:
                load_xb(j + 1)
            for i in range(4):
                for ft in range(16):
                    P.op("tensor", lambda e, i=i, ft=ft: e.matmul(
                        PS[0][:, i * 256:(i + 1) * 256], lhsT=wAb[:, ft, i * 128:(i + 1) * 128], rhs=xb[jb][:, ft, :],
                        start=(ft == 0), stop=(ft == 15)),
                        reads=[T_wA, T_xb[jb]], writes=[Tb[i // 2]])
            for ti in range(2):
                for ft in range(16):
                    P.op("tensor", lambda e, ti=ti, ft=ft: e.matmul(
                        PS[1][:, ti * 512:(ti + 1) * 512], lhsT=xb[jb][:, ft, ti * 128:(ti + 1) * 128], rhs=wBb[:, ft, :],
                        start=(ft == 0), stop=(ft == 15)),
                        reads=[T_wB, T_xb[jb]], writes=[Tb[2 + ti]])
            P.op("scalar", lambda e: e.activation(out=sq[:], in_=PS[0][:], func=ACT.Square),
                 reads=[Tb[0], Tb[1]], writes=[T_sq])
            for i in range(4):
                P.op("tensor", lambda e, i=i: e.matmul(PS[2][:, i * 256:(i + 1) * 256], lhsT=ones[:], rhs=sq[:, i * 256:(i + 1) * 256],
                                                      start=True, stop=True),
                     reads=[T_ones, T_sq], writes=[Tb[4 + i // 2]])
            P.op("scalar", lambda e: e.activation(out=rst[:], in_=PS[2][:], func=ACT.Ln, bias=128.0 * EPS),
                 reads=[Tb[4], Tb[5]], writes=[T_rst])
            P.op("scalar", lambda e: e.activation(out=rst[:], in_=rst[:], func=ACT.Exp, scale=-0.5),
                 reads=[T_rst], writes=[T_rst])
            P.op("vector", lambda e: e.scalar_tensor_tensor(
                out=KT[:, :, j * QB:(j + 1) * QB], in0=PS[0][:, 0:512].rearrange("p (m t) -> p m t", m=2),
                scalar=gqk_s[:, 0:1], in1=rst[:, 0:512].rearrange("p (m t) -> p m t", m=2), op0=ALU.mult, op1=ALU.mult),
                reads=[Tb[0], T_rst, T_g], writes=[T_KT[j]])
            P.op("vector", lambda e: e.scalar_tensor_tensor(
                out=QTb[jb][:], in0=PS[0][:, 512:1024].rearrange("p (m t) -> p m t", m=2),
                scalar=gqk_s[:, 1:2], in1=rst[:, 512:1024].rearrange("p (m t) -> p m t", m=2), op0=ALU.mult, op1=ALU.mult),
                reads=[Tb[1], T_rst, T_g], writes=[T_QT[jb]])
            def vz_tile(ti):
                tt = 2 * j + ti
                zi = jb * 2 + ti
                P.op("scalar", lambda e, ti=ti, tt=tt: e.copy(out=V[:, tt, 0:256], in_=PS[1][:, ti * 512:ti * 512 + 256]),
                     reads=[Tb[2 + ti]], writes=[T_V[tt]])
                P.op("scalar", lambda e, ti=ti: e.activation(out=th[ti][:], in_=PS[1][:, ti * 512 + 256:(ti + 1) * 512],
                                                             func=ACT.Exp, scale=-1.0),
                     reads=[Tb[2 + ti]], writes=[T_th[ti]])
                P.op("gpsimd", lambda e, ti=ti: e.tensor_scalar_add(out=th[ti][:], in0=th[ti][:], scalar1=1.0),
                     reads=[T_th[ti]], writes=[T_th[ti]])
                P.op("vector", lambda e, ti=ti: e.reciprocal(out=th[ti][:], in_=th[ti][:]),
                     reads=[T_th[ti]], writes=[T_th[ti]])
                P.op("vector", lambda e, ti=ti, zi=zi: e.tensor_tensor(
                    out=Zg[zi][:], in0=th[ti][:], in1=PS[1][:, ti * 512 + 256:(ti + 1) * 512], op=ALU.mult),
                    reads=[T_th[ti], Tb[2 + ti]], writes=[T_Zg[zi]])

            for ti in range(2):
                vz_tile(ti)

            npair = j + 1
            def do_qk(p):
                sp = p % 2
                for kl in range(2):
                    kt = 2 * p + kl
                    for m in range(2):
                        P.op("tensor", lambda e, kt=kt, kl=kl, m=m, sp=sp: e.matmul(
                            PS[sp][:, (kl * 2 + m) * 256:(kl * 2 + m + 1) * 256],
                            lhsT=KT[:, m, kt * 128:(kt + 1) * 128], rhs=QTb[jb][:, m, :], start=True, stop=True),
                            reads=[T_KT[kt // 2], T_QT[jb]], writes=[Tb[2 * sp + kl]])

            def do_pair(p):
                sp = p % 2
                pb = p % 3
                last = (p == npair - 1)
                P.op("scalar", lambda e, sp=sp, pb=pb: e.activation(
                    out=PT[pb][:].rearrange("p a q -> p (a q)"), in_=PS[sp][:], func=ACT.Exp),
                    reads=[Tb[2 * sp], Tb[2 * sp + 1]], writes=[T_PT[pb]])
                if last:
                    P.op("gpsimd", lambda e, pb=pb: e.memset(PT[pb][64:128, 0:2, 0:64], 0.0), writes=[T_PT[pb]])
                    P.op("gpsimd", lambda e, pb=pb: e.memset(PT[pb][64:128, 2:4, 128:192], 0.0), writes=[T_PT[pb]])
                for kl in range(2):
                    kt = 2 * p + kl
                    for qt in range(2):
                        if last and kl == 1 and qt == 0:
                            continue
                        for m in range(2):
                            P.op("tensor", lambda e, kt=kt, kl=kl, m=m, qt=qt, pb=pb: e.matmul(
                                PS[2 + qt][:, m * 512:m * 512 + 257],
                                lhsT=PT[pb][:, kl * 2 + m, qt * 128:(qt + 1) * 128], rhs=V[:, kt, :],
                                start=(kt == 0), stop=(kt == 2 * j + qt)),
                                reads=[T_PT[pb], T_V[kt]], writes=[Tb[4 + 2 * qt + m]])

            do_qk(0)
            for p in range(npair):
                if p + 1 < npair:
                    do_qk(p + 1)
                do_pair(p)

            def epilogue(qt):
                tt = 2 * j + qt
                zi = jb * 2 + qt
                O1 = PS[2 + qt][:, 0:257]
                O2 = PS[2 + qt][:, 512:769]
                b1, b2 = Tb[4 + 2 * qt], Tb[4 + 2 * qt + 1]
                sm = small[qt]
                P.op("vector", lambda e, O1=O1, sm=sm: e.reciprocal(out=sm[:, 0:1], in_=O1[:, 256:257]),
                     reads=[b1], writes=[T_e[qt]])
                P.op("vector", lambda e, O2=O2, sm=sm: e.reciprocal(out=sm[:, 1:2], in_=O2[:, 256:257]),
                     reads=[b2], writes=[T_e[qt]])
                P.op("vector", lambda e, sm=sm: e.tensor_tensor(out=sm[:, 2:3], in0=sm[:, 1:2], in1=nlam, op=ALU.mult),
                     reads=[T_e[qt], T_lam], writes=[T_e[qt]])
                P.op("vector", lambda e, O1=O1, sm=sm, qt=qt: e.tensor_scalar_mul(out=ea[qt][:], in0=O1[:, 0:256], scalar1=sm[:, 0:1]),
                     reads=[b1, T_e[qt]], writes=[T_e[qt]])
                P.op("vector", lambda e, O2=O2, sm=sm, qt=qt: e.scalar_tensor_tensor(
                    out=eo[qt][:], in0=O2[:, 0:256], scalar=sm[:, 2:3], in1=ea[qt][:], op0=ALU.mult, op1=ALU.add),
                    reads=[b2, T_e[qt]], writes=[T_e[qt]])
                P.op("scalar", lambda e, qt=qt: e.activation(out=esq[qt][:], in_=eo[qt][:], func=ACT.Square),
                     reads=[T_e[qt]], writes=[T_e[qt]])
                P.op("vector", lambda e, sm=sm, qt=qt: e.reduce_sum(out=sm[:, 3:4], in_=esq[qt][:], axis=AX.X),
                     reads=[T_e[qt]], writes=[T_e[qt]])
                P.op("scalar", lambda e, sm=sm: e.activation(out=sm[:, 4:5], in_=sm[:, 3:4], func=ACT.Ln, bias=256.0 * EPS),
                     reads=[T_e[qt]], writes=[T_e[qt]])
                P.op("scalar", lambda e, sm=sm: e.activation(out=sm[:, 4:5], in_=sm[:, 4:5], func=ACT.Exp, scale=-0.5),
                     reads=[T_e[qt]], writes=[T_e[qt]])
                P.op("vector", lambda e, sm=sm, qt=qt: e.scalar_tensor_tensor(
                    out=eof[qt][:], in0=eo[qt][:], scalar=sm[:, 4:5], in1=gsub_s[:], op0=ALU.mult, op1=ALU.mult),
                    reads=[T_e[qt], T_g], writes=[T_e[qt]])
                P.op("gpsimd", lambda e, qt=qt, zi=zi: e.tensor_tensor(out=ob[zi][:], in0=eof[qt][:], in1=Zg[zi][:], op=ALU.mult),
                     reads=[T_e[qt], T_Zg[zi]], writes=[T_ob[zi]])
                P.op("sync", lambda e, tt=tt, zi=zi: e.dma_start(out=o_out[tt * 128:(tt + 1) * 128, :], in_=ob[zi][:]),
                     reads=[T_ob[zi]], writes=[T_out], dma="ob%d" % zi)
            for qt in range(2):
                epilogue(qt)

        for j in range(nblk):
            do_block(j)

        if dbg:
            dk = nc.dram_tensor("d_KT", [128, 2, s_len], BF16, kind="ExternalOutput").ap()
            dv = nc.dram_tensor("d_V", [128, ntile, 257], BF16, kind="ExternalOutput").ap()
            dq = nc.dram_tensor("d_QT", [128, 2, QB], BF16, kind="ExternalOutput").ap()
            dr = nc.dram_tensor("d_rst", [128, 1024], F32, kind="ExternalOutput").ap()
            dz = nc.dram_tensor("d_Zg", [128, 256], BF16, kind="ExternalOutput").ap()
            dp = nc.dram_tensor("d_PT", [128, 4, QB], BF16, kind="ExternalOutput").ap()
            de = nc.dram_tensor("d_eo", [128, 256], F32, kind="ExternalOutput").ap()
            dea = nc.dram_tensor("d_ea", [128, 256], F32, kind="ExternalOutput").ap()
            dsm = nc.dram_tensor("d_sm", [128, 8], F32, kind="ExternalOutput").ap()
            dl = nc.dram_tensor("d_lam", [128, 4], F32, kind="ExternalOutput").ap()
            dw = nc.dram_tensor("d_wA", [128, 16, 512], BF16, kind="ExternalOutput").ap()
            allT = T_KT + T_V + T_QT + [T_rst] + T_Zg + T_PT + T_e + [T_lam, T_wA]
            for (dst, src) in ((dk, KT), (dv, V), (dq, QTb[(nblk - 1) % 2]), (dr, rst), (dz, Zg[((nblk - 1) % 2) * 2]),
                               (dp, PT[(nblk - 1) % 3]), (de, eo[0]), (dea, ea[0]), (dsm, small[0]), (dl, lam_v), (dw, wAb)):
                P.op("sync", lambda e, dst=dst, src=src: e.dma_start(out=dst, in_=src[:]), reads=allT, writes=[T_out], dma="dbg")
        P.emit()
    return nc


def l2_inputs(xnT_full, w_kv, b_w_in, kv_norm_g, b_norm_g, k_norm_g, q_norm_g, lam, subln_g, s_len=S):
    QB = 256
    nblk = s_len // QB
    xb = np.ascontiguousarray(
        xnT_full.reshape(16, 128, nblk, QB).transpose(2, 1, 0, 3))

    def wl(w):
        return np.ascontiguousarray(w.reshape(16, 128, w.shape[1]).transpose(1, 0, 2))

    gcol = np.ascontiguousarray(np.concatenate(
        [kv_norm_g.reshape(16, 128).T, b_norm_g.reshape(16, 128).T], axis=1)).astype(np.float32)
    gqk = np.ascontiguousarray(np.stack([k_norm_g, q_norm_g], axis=1)).astype(np.float32)
    gsub = np.ascontiguousarray(np.broadcast_to(subln_g.reshape(1, 256), (128, 256))).astype(np.float32)
    lamp = np.ascontiguousarray(np.broadcast_to(lam.reshape(1, 512), (128, 512))).astype(np.float32)
    maps = []
    for c in range(NCORES):
        kc = w_kv[:, c * 256:(c + 1) * 256]
        vc = w_kv[:, 2048 + c * 256:2048 + (c + 1) * 256]
        qc = b_w_in[:, c * 256:(c + 1) * 256]
        zc = b_w_in[:, 2048 + c * 256:2048 + (c + 1) * 256]
        maps.append({
            "xnT": xb,
            "wA": wl(np.concatenate([kc, qc], axis=1)),
            "wB": wl(np.concatenate([vc, zc], axis=1)),
            "gcol": gcol, "gqk": gqk, "gsub": gsub, "lamp": lamp,
        })
    return maps


def build_l3(ntok=2048):
    ntt = ntok // 128
    nc = bass.Bass("TRN2", target_bir_lowering=False)
    oT = nc.dram_tensor("oT", [ntt, 128, 16, 128], BF16, kind="ExternalInput").ap()
    x1 = nc.dram_tensor("x1", [ntok, D], F32, kind="ExternalInput").ap()
    wo = nc.dram_tensor("wo", [128, 16, D], F32, kind="ExternalInput").ap()
    y = nc.dram_tensor("y", [ntok, D], F32, kind="ExternalOutput").ap()
    P = Prog(nc)
    with contextlib.ExitStack() as es:
        def sb(name, shape, dt):
            return es.enter_context(nc.sbuf_tensor(name, shape, dt))
        wob = sb("wob", [128, 16, D], BF16)
        ot = [sb("ot%d" % i, [128, 16, 128], BF16) for i in range(2)]
        xt = [sb("xt%d" % i, [128, D], F32) for i in range(2)]
        yt = [sb("yt%d" % i, [128, D], F32) for i in range(2)]
        PS = [es.enter_context(nc.psum_tensor("ps%d" % i, [128, 2048], F32)) for i in range(2)]
        T_w = [T() for _ in range(16)]
        T_ot, T_xt, T_yt, T_ps = [T(), T()], [T(), T()], [T(), T()], [T(), T()]
        T_y = T()
        wstg = [sb("wstg%d" % i, [128, D], F32) for i in range(3)]
        T_wstg = [T(), T(), T()]
        for ct in range(16):
            i = ct % 3
            P.op("sync", lambda e, ct=ct, i=i: e.dma_start(out=wstg[i][:], in_=wo[:, ct, :]), writes=[T_wstg[i]], dma="w%d" % i)
            if ct % 2 == 0:
                P.op("vector", lambda e, ct=ct, i=i: e.tensor_copy(out=wob[:, ct, :], in_=wstg[i][:]), reads=[T_wstg[i]], writes=[T_w[ct]])
            else:
                P.op("scalar", lambda e, ct=ct, i=i: e.copy(out=wob[:, ct, :], in_=wstg[i][:]), reads=[T_wstg[i]], writes=[T_w[ct]])

        def tile(tt):
            b = tt % 2
            P.op("sync", lambda e: e.dma_start(out=ot[b][:], in_=oT[tt]), writes=[T_ot[b]], dma="ot%d" % b)
            P.op("sync", lambda e: e.dma_start(out=xt[b][:], in_=x1[tt * 128:(tt + 1) * 128, :]), writes=[T_xt[b]], dma="xt%d" % b)
            for fb in range(4):
                for ct in range(16):
                    P.op("tensor", lambda e, fb=fb, ct=ct: e.matmul(
                        PS[b][:, fb * 512:(fb + 1) * 512], lhsT=ot[b][:, ct, :], rhs=wob[:, ct, fb * 512:(fb + 1) * 512],
                        start=(ct == 0), stop=(ct == 15)),
                        reads=[T_ot[b], T_w[ct]], writes=[T_ps[b]])
            P.op("vector", lambda e: e.tensor_tensor(out=yt[b][:], in0=PS[b][:], in1=xt[b][:], op=ALU.add),
                 reads=[T_ps[b], T_xt[b]], writes=[T_yt[b]])
            P.op("sync", lambda e: e.dma_start(out=y[tt * 128:(tt + 1) * 128, :], in_=yt[b][:]),
                 reads=[T_yt[b]], writes=[T_y], dma="yt%d" % b)

        for tt in range(ntt):
            tile(tt)
        P.emit()
    return nc


def build_l1(ntok=2048, NP=1024):
    npass = ntok // NP
    NB = NP // 512
    NL = NP + HALO
    nc = bass.Bass("TRN2", target_bir_lowering=False)
    x = nc.dram_tensor("x", [ntok + HALO, D], F32, kind="ExternalInput").ap()
    w_in = nc.dram_tensor("w_in", [48, 128, 16, 128], F32, kind="ExternalInput").ap()
    w_out = nc.dram_tensor("w_out", [128, 16, D], F32, kind="ExternalInput").ap()
    gbc_d = nc.dram_tensor("gbc", [128, D], F32, kind="ExternalInput").ap()
    dww_d = nc.dram_tensor("dww", [128, 16, CONV_W], F32, kind="ExternalInput").ap()
    cvec_d = nc.dram_tensor("cvec", [128, 48], F32, kind="ExternalInput").ap()
    identb_d = nc.dram_tensor("identb", [128, 128], BF16, kind="ExternalInput").ap()
    identf_d = nc.dram_tensor("identf", [128, 128], F32, kind="ExternalInput").ap()
    x1_o = nc.dram_tensor("x1", [ntok, D], F32, kind="ExternalOutput").ap()
    xnT_o = nc.dram_tensor("xnT", [ntok // 256, 128, 16, 256], BF16, kind="ExternalOutput").ap()
    wi_b = nc.dram_tensor("wi_b", [48, 128, 2048], BF16).ap()
    wo_b = nc.dram_tensor("wo_b", [16, 128, 2048], BF16).ap()

    P = Prog(nc)
    with contextlib.ExitStack() as es:
        def sb(name, shape, dt):
            return es.enter_context(nc.sbuf_tensor(name, shape, dt))

        HT_N = 16 * NL
        ARENA = max(HT_N + 6 * 2048 + 2 * CONV_W * 128, 16 * D)
        arena = sb("arena", [128, ARENA], BF16)
        hT = arena[:, 0:HT_N].rearrange("p (f t) -> p f t", f=16)
        wt = [arena[:, HT_N + i * 2048: HT_N + (i + 1) * 2048].rearrange("p (f n) -> p f n", f=16) for i in range(6)]
        wa, wb_, wz = wt[0:2], wt[2:4], wt[4:6]
        dgo = HT_N + 6 * 2048
        dg = [arena[:, dgo + i * CONV_W * 128: dgo + (i + 1) * CONV_W * 128].rearrange("p (j n) -> p j n", j=CONV_W) for i in range(2)]
        wob = arena[:, 0:16 * D].rearrange("p (c n) -> p c n", c=16)
        cvT = sb("cvT", [128, 16, NP], BF16)
        yT = [sb("yT%d" % i, [128, NL], BF16) for i in range(2)]
        acc_s = sb("acc_s", [128, NP], F32)
        acc_q = sb("acc_q", [128, NP], F32)
        rstd_bc = sb("rstd_bc", [128, NP], F32)
        nmr_bc = sb("nmr_bc", [128, NP], F32)
        xt = [sb("xt%d" % i, [128, D], F32) for i in range(2)]
        sqx = sb("sqx", [128, D], F32)
        tmp = {k: [sb("%s%d" % (k, i), [128, 512], F32) for i in range(2)] for k in ("ta", "tb", "tc", "td", "te", "tf")}
        xnblk = sb("xnblk", [128, 16, 256], BF16)
        gbc = sb("gbc_s", [128, D], F32)
        dww = sb("dww_s", [128, 16, CONV_W], F32)
        cvec = sb("cvec_s", [128, 48], F32)
        identb = sb("identb_s", [128, 128], BF16)
        identf = sb("identf_s", [128, 128], F32)
        onesf = sb("onesf", [128, 128], F32)
        small = sb("small", [128, 8], F32)
        B = [es.enter_context(nc.psum_tensor("b%d" % i, [128, 512], F32)) for i in range(8)]

        Tb = [T("bank%d" % i) for i in range(8)]
        T_c = T()
        T_hT = T()
        T_wa, T_wb, T_wz, T_dg = [T(), T()], [T(), T()], [T(), T()], [T(), T()]
        T_cv = [T() for _ in range(16)]
        T_yT = [T(), T()]
        T_acc, T_st = T(), T()
        T_xt = [T(), T()]
        T_sqx = T()
        T_tmp = {k: [T(), T()] for k in tmp}
        T_xnblk = T()
        T_sm = T()
        T_x1o, T_xno = T(), T()
        T_wob = T()

        dwb = lambda c: cvec[:, c:c + 1]
        lng = lambda c: cvec[:, 16 + c:17 + c]
        lnb = lambda c: cvec[:, 32 + c:33 + c]

        for (dst, src) in ((gbc, gbc_d), (dww, dww_d), (cvec, cvec_d), (identb, identb_d), (identf, identf_d)):
            P.op("sync", lambda e, dst=dst, src=src: e.dma_start(out=dst[:], in_=src), writes=[T_c], dma="c0")
        P.op("gpsimd", lambda e: e.memset(onesf[:], 1.0), writes=[T_c])

        T_wib = [T() for _ in range(48)]
        T_wobd = [T() for _ in range(16)]
        cvflat = cvT[:].rearrange("p c t -> p (c t)")
        stg_in = [cvflat[:, i * (4 * NP):(i + 1) * (4 * NP)].bitcast(F32) for i in range(4)]
        T_stg_in = [T_cv[4 * i:4 * i + 4] for i in range(4)]
        xnflat = xnblk[:].rearrange("p f t -> p (f t)")
        stg_out = [xnflat[:, i * 2048:(i + 1) * 2048] for i in range(2)]
        T_stg_out = [T(), T()]
        CH = 2 * NP
        per = 2048 // CH
        assert per * CH == 2048

        def wsrc(k):
            if k < 48:
                return w_in[k].rearrange("p f n -> p (f n)"), wi_b[k], T_wib[k]
            return w_out[:, k - 48, :], wo_b[k - 48], T_wobd[k - 48]

        def w_load(k):
            src, _, _ = wsrc(k)
            i = k % 4
            P.op("sync", lambda e: e.dma_start(out=stg_in[i], in_=src), writes=T_stg_in[i], dma="wl%d" % i)

        def w_cast_store(k):
            _, dst, Tdst = wsrc(k)
            i, o = k % 4, k % 2
            if k % 2 == 0:
                P.op("vector", lambda e: e.tensor_copy(out=stg_out[o], in_=stg_in[i]), reads=T_stg_in[i], writes=[T_stg_out[o], T_xnblk])
            else:
                P.op("scalar", lambda e: e.copy(out=stg_out[o], in_=stg_in[i]), reads=T_stg_in[i], writes=[T_stg_out[o], T_xnblk])
            P.op("sync", lambda e: e.dma_start(out=dst, in_=stg_out[o]), reads=[T_stg_out[o]], writes=[Tdst], dma="ws%d" % o)

        assert per == 1
        for k in range(3):
            w_load(k)
        for k in range(64):
            if k + 3 < 64:
                w_load(k + 3)
            w_cast_store(k)

        cnt = {"xt": 0}

        def rstd_from_ss(np_, col_in, col_out, n):
            P.op("scalar", lambda e: e.activation(out=small[0:np_, col_out:col_out + 1], in_=small[0:np_, col_in:col_in + 1],
                                                  func=ACT.Ln, scale=1.0 / n, bias=EPS), reads=[T_sm], writes=[T_sm])
            P.op("scalar", lambda e: e.activation(out=small[0:np_, col_out:col_out + 1], in_=small[0:np_, col_out:col_out + 1],
                                                  func=ACT.Exp, scale=-0.5), reads=[T_sm], writes=[T_sm])

        def norm_tile(row0, np_, col0):
            b = cnt["xt"] % 2
            cnt["xt"] += 1
            P.op("sync", lambda e: e.dma_start(out=xt[b][0:np_, :], in_=x[row0:row0 + np_, :]), writes=[T_xt[b]], dma="xt%d" % b)
            P.op("scalar", lambda e: e.activation(out=sqx[0:np_, :], in_=xt[b][0:np_, :], func=ACT.Square),
                 reads=[T_xt[b]], writes=[T_sqx])
            P.op("vector", lambda e: e.reduce_sum(out=small[0:np_, 0:1], in_=sqx[0:np_, :], axis=AX.X),
                 reads=[T_sqx], writes=[T_sm])
            rstd_from_ss(np_, 0, 1, float(D))
            P.op("scalar", lambda e: e.activation(out=sqx[0:np_, :], in_=xt[b][0:np_, :], func=ACT.Copy, scale=small[0:np_, 1:2]),
                 reads=[T_xt[b], T_sm], writes=[T_sqx])
            for ft in range(16):
                P.op("tensor", lambda e, ft=ft: e.transpose(
                    B[ft // 4][:, (ft % 4) * 128:(ft % 4) * 128 + np_], sqx[0:np_, ft * 128:(ft + 1) * 128], identf[0:np_, 0:np_]),
                    reads=[T_sqx, T_c], writes=[Tb[ft // 4]])
            for q in range(4):
                P.op("vector", lambda e, q=q: e.tensor_tensor(
                    out=hT[:, q * 4:(q + 1) * 4, col0:col0 + np_],
                    in0=B[q][:].rearrange("p (f t) -> p f t", f=4)[:, :, 0:np_],
                    in1=gbc[:, q * 512:(q + 1) * 512].rearrange("p (f t) -> p f t", f=4)[:, :, 0:np_], op=ALU.mult),
                    reads=[Tb[q], T_c], writes=[T_hT])

        def do_pass(h):
            r_h = h * NP
            r_o = HALO + h * NP
            norm_tile(r_h, HALO, 0)
            for t in range(NP // 128):
                norm_tile(r_o + t * 128, 128, HALO + t * 128)
            P.op("gpsimd", lambda e: e.memset(acc_s[:], 0.0), writes=[T_acc])
            P.op("gpsimd", lambda e: e.memset(acc_q[:], 0.0), writes=[T_acc])

            def prefetch1(c):
                b = c % 2
                P.op("sync", lambda e: e.dma_start(out=wa[b], in_=wi_b[c].rearrange("p (f n) -> p f n", f=16)),
                     reads=[T_wib[c]], writes=[T_wa[b]], dma="wa%d" % b)
                P.op("sync", lambda e: e.dma_start(out=wb_[b], in_=wi_b[16 + c].rearrange("p (f n) -> p f n", f=16)),
                     reads=[T_wib[16 + c]], writes=[T_wb[b]], dma="wb%d" % b)
                for j in range(CONV_W):
                    P.op("vector", lambda e, j=j: e.tensor_scalar_mul(out=dg[b][:, j, :], in0=identb[:], scalar1=dww[:, c, j:j + 1]),
                         reads=[T_c], writes=[T_dg[b]])

            def stage2(c):
                b = c % 2
                yt_ = yT[b]
                for (w_, T_w, off) in ((wa[b], T_wa[b], 0), (wb_[b], T_wb[b], 32)):
                    for ft in range(16):
                        P.op("tensor", lambda e, w_=w_, off=off, ft=ft: e.matmul(
                            B[6][:, off:off + HALO], lhsT=w_[:, ft, :], rhs=hT[:, ft, 0:HALO], start=(ft == 0), stop=(ft == 15)),
                            reads=[T_w, T_hT], writes=[Tb[6]])
                P.op("scalar", lambda e: e.activation(out=tmp["ta"][0][:, 0:HALO], in_=B[6][:, 32:32 + HALO], func=ACT.Sigmoid),
                     reads=[Tb[6]], writes=[T_tmp["ta"][0]])
                P.op("vector", lambda e: e.tensor_tensor(out=yt_[:, 0:HALO], in0=B[6][:, 0:HALO], in1=tmp["ta"][0][:, 0:HALO], op=ALU.mult),
                     reads=[Tb[6], T_tmp["ta"][0]], writes=[T_yT[b]])
                for tb in range(NB):
                    pa, pb = B[2 * (tb % 2)], B[2 * (tb % 2) + 1]
                    Ta, Tbb = Tb[2 * (tb % 2)], Tb[2 * (tb % 2) + 1]
                    c0 = HALO + tb * 512
                    for (w_, T_w, ps, Tp) in ((wa[b], T_wa[b], pa, Ta), (wb_[b], T_wb[b], pb, Tbb)):
                        for ft in range(16):
                            P.op("tensor", lambda e, w_=w_, ps=ps, ft=ft, c0=c0: e.matmul(
                                ps[:], lhsT=w_[:, ft, :], rhs=hT[:, ft, c0:c0 + 512], start=(ft == 0), stop=(ft == 15)),
                                reads=[T_w, T_hT], writes=[Tp])
                    sg = tmp["ta"][tb % 2]
                    P.op("scalar", lambda e, pb=pb, sg=sg: e.activation(out=sg[:], in_=pb[:], func=ACT.Sigmoid),
                         reads=[Tbb], writes=[T_tmp["ta"][tb % 2]])
                    P.op("vector", lambda e, pa=pa, sg=sg, c0=c0: e.tensor_tensor(out=yt_[:, c0:c0 + 512], in0=pa[:], in1=sg[:], op=ALU.mult),
                         reads=[Ta, T_tmp["ta"][tb % 2]], writes=[T_yT[b]])
                for tb in range(NB):
                    pc, Tc_ = B[4 + tb % 2], Tb[4 + tb % 2]
                    for j in range(CONV_W):
                        P.op("tensor", lambda e, pc=pc, j=j, tb=tb: e.matmul(
                            pc[:], lhsT=dg[b][:, j, :], rhs=yt_[:, tb * 512 + j: tb * 512 + j + 512],
                            start=(j == 0), stop=(j == CONV_W - 1)),
                            reads=[T_dg[b], T_yT[b]], writes=[Tc_])
                    blk = slice(tb * 512, (tb + 1) * 512)
                    sq_ = tmp["tb"][tb % 2]
                    P.op("scalar", lambda e, pc=pc, blk=blk: e.activation(out=cvT[:, c, blk], in_=pc[:], func=ACT.Identity, bias=dwb(c)),
                         reads=[Tc_, T_c], writes=[T_cv[c]])
                    P.op("scalar", lambda e, pc=pc, sq_=sq_: e.activation(out=sq_[:], in_=pc[:], func=ACT.Square, bias=dwb(c)),
                         reads=[Tc_, T_c], writes=[T_tmp["tb"][tb % 2]])
                    P.op("vector", lambda e, blk=blk: e.tensor_tensor(out=acc_s[:, blk], in0=acc_s[:, blk], in1=cvT[:, c, blk], op=ALU.add),
                         reads=[T_cv[c], T_acc], writes=[T_acc])
                    P.op("vector", lambda e, blk=blk, sq_=sq_: e.tensor_tensor(out=acc_q[:, blk], in0=acc_q[:, blk], in1=sq_[:], op=ALU.add),
                         reads=[T_tmp["tb"][tb % 2], T_acc], writes=[T_acc])

            prefetch1(0)
            for c in range(16):
                if c + 1 < 16:
                    prefetch1(c + 1)
                stage2(c)

            def prefetch_z(c):
                b = c % 2
                P.op("sync", lambda e: e.dma_start(out=wz[b], in_=wi_b[32 + c].rearrange("p (f n) -> p f n", f=16)),
                     reads=[T_wib[32 + c]], writes=[T_wz[b]], dma="wz%d" % b)
            prefetch_z(0)
            for tb in range(NB):
                blk = slice(tb * 512, (tb + 1) * 512)
                P.op("tensor", lambda e, blk=blk: e.matmul(B[0][:], lhsT=onesf[:], rhs=acc_s[:, blk], start=True, stop=True),
                     reads=[T_c, T_acc], writes=[Tb[0]])
                P.op("tensor", lambda e, blk=blk: e.matmul(B[1][:], lhsT=onesf[:], rhs=acc_q[:, blk], start=True, stop=True),
                     reads=[T_c, T_acc], writes=[Tb[1]])
                mean, m2 = tmp["tc"][0], tmp["td"][0]
                P.op("vector", lambda e: e.tensor_scalar_mul(out=mean[:], in0=B[0][:], scalar1=1.0 / D),
                     reads=[Tb[0]], writes=[T_tmp["tc"][0]])
                P.op("vector", lambda e: e.tensor_tensor(out=m2[:], in0=mean[:], in1=mean[:], op=ALU.mult),
                     reads=[T_tmp["tc"][0]], writes=[T_tmp["td"][0]])
                P.op("vector", lambda e: e.scalar_tensor_tensor(out=m2[:], in0=B[1][:], scalar=1.0 / D, in1=m2[:],
                                                                op0=ALU.mult, op1=ALU.subtract),
                     reads=[Tb[1], T_tmp["td"][0]], writes=[T_tmp["td"][0]])
                P.op("scalar", lambda e, blk=blk: e.activation(out=rstd_bc[:, blk], in_=m2[:], func=ACT.Ln, bias=EPS),
                     reads=[T_tmp["td"][0]], writes=[T_st])
                P.op("scalar", lambda e, blk=blk: e.activation(out=rstd_bc[:, blk], in_=rstd_bc[:, blk], func=ACT.Exp, scale=-0.5),
                     reads=[T_st], writes=[T_st])
                P.op("vector", lambda e, blk=blk: e.scalar_tensor_tensor(out=nmr_bc[:, blk], in0=mean[:], scalar=-1.0, in1=rstd_bc[:, blk],
                                                                         op0=ALU.mult, op1=ALU.mult),
                     reads=[T_tmp["tc"][0], T_st], writes=[T_st])

            def stage4(c):
                b = c % 2
                for tb in range(NB):
                    k = tb % 2
                    pz, Tz = B[2 + k], Tb[2 + k]
                    c0 = HALO + tb * 512
                    blk = slice(tb * 512, (tb + 1) * 512)
                    for ft in range(16):
                        P.op("tensor", lambda e, pz=pz, ft=ft, c0=c0: e.matmul(
                            pz[:], lhsT=wz[b][:, ft, :], rhs=hT[:, ft, c0:c0 + 512], start=(ft == 0), stop=(ft == 15)),
                            reads=[T_wz[b], T_hT], writes=[Tz])
                    sz, gz, t1, s2, l_, u_ = (tmp[n][k] for n in ("ta", "tb", "tc", "td", "te", "tf"))
                    Ts = {n: T_tmp[n][k] for n in ("ta", "tb", "tc", "td", "te", "tf")}
                    P.op("scalar", lambda e, pz=pz, sz=sz: e.activation(out=sz[:], in_=pz[:], func=ACT.Sigmoid),
                         reads=[Tz], writes=[Ts["ta"]])
                    P.op("vector", lambda e, pz=pz, sz=sz, gz=gz: e.tensor_tensor(out=gz[:], in0=pz[:], in1=sz[:], op=ALU.mult),
                         reads=[Tz, Ts["ta"]], writes=[Ts["tb"]])
                    P.op("vector", lambda e, t1=t1, blk=blk: e.tensor_tensor(out=t1[:], in0=cvT[:, c, blk], in1=rstd_bc[:, blk], op=ALU.mult),
                         reads=[T_cv[c], T_st], writes=[Ts["tc"]])
                    P.op("vector", lambda e, t1=t1, blk=blk: e.tensor_tensor(out=t1[:], in0=t1[:], in1=nmr_bc[:, blk], op=ALU.add),
                         reads=[Ts["tc"], T_st], writes=[Ts["tc"]])
                    P.op("scalar", lambda e, t1=t1, s2=s2: e.activation(out=s2[:], in_=t1[:], func=ACT.Sigmoid, scale=lng(c), bias=lnb(c)),
                         reads=[Ts["tc"], T_c], writes=[Ts["td"]])
                    P.op("vector", lambda e, t1=t1, l_=l_: e.tensor_scalar(out=l_[:], in0=t1[:], scalar1=lng(c), scalar2=lnb(c),
                                                                          op0=ALU.mult, op1=ALU.add),
                         reads=[Ts["tc"], T_c], writes=[Ts["te"]])
                    P.op("vector", lambda e, l_=l_, s2=s2, u_=u_: e.tensor_tensor(out=u_[:], in0=l_[:], in1=s2[:], op=ALU.mult),
                         reads=[Ts["te"], Ts["td"]], writes=[Ts["tf"]])
                    P.op("vector", lambda e, u_=u_, gz=gz, blk=blk: e.tensor_tensor(out=cvT[:, c, blk], in0=u_[:], in1=gz[:], op=ALU.mult),
                         reads=[Ts["tf"], Ts["tb"]], writes=[T_cv[c]])

            for c in range(16):
                if c + 1 < 16:
                    prefetch_z(c + 1)
                stage4(c)

            alias = [T_hT] + T_wa + T_wb + T_wz + T_dg
            for ct in range(16):
                P.op("sync", lambda e, ct=ct: e.dma_start(out=wob[:, ct, :], in_=wo_b[ct]), reads=[T_wobd[ct]], writes=[T_wob] + alias,
                     dma="wo%d" % (ct % 4))

            def stage5(t):
                b = cnt["xt"] % 2
                cnt["xt"] += 1
                row = h * NP + t * 128
                P.op("sync", lambda e: e.dma_start(out=xt[b][:], in_=x[HALO + row:HALO + row + 128, :]), writes=[T_xt[b]], dma="xt%d" % b)
                for fb in range(4):
                    for ct in range(16):
                        P.op("tensor", lambda e, fb=fb, ct=ct: e.matmul(
                            B[fb][:], lhsT=cvT[:, ct, t * 128:(t + 1) * 128], rhs=wob[:, ct, fb * 512:(fb + 1) * 512],
                            start=(ct == 0), stop=(ct == 15)),
                            reads=[T_cv[ct], T_wob], writes=[Tb[fb]])
                for fb in range(4):
                    P.op("vector", lambda e, fb=fb: e.tensor_tensor(out=xt[b][:, fb * 512:(fb + 1) * 512], in0=B[fb][:],
                                                                    in1=xt[b][:, fb * 512:(fb + 1) * 512], op=ALU.add),
                         reads=[Tb[fb], T_xt[b]], writes=[T_xt[b]])
                P.op("sync", lambda e: e.dma_start(out=x1_o[row:row + 128, :], in_=xt[b][:]), reads=[T_xt[b]], writes=[T_x1o], dma="x1o%d" % b)
                P.op("scalar", lambda e: e.activation(out=sqx[:], in_=xt[b][:], func=ACT.Square), reads=[T_xt[b]], writes=[T_sqx])
                P.op("vector", lambda e: e.reduce_sum(out=small[:, 0:1], in_=sqx[:], axis=AX.X), reads=[T_sqx], writes=[T_sm])
                rstd_from_ss(128, 0, 1, float(D))
                P.op("scalar", lambda e: e.activation(out=sqx[:], in_=xt[b][:], func=ACT.Copy, scale=small[:, 1:2]),
                     reads=[T_xt[b], T_sm], writes=[T_sqx])
                for ft in range(16):
                    P.op("tensor", lambda e, ft=ft: e.transpose(
                        B[4 + ft // 4][:, (ft % 4) * 128:(ft % 4 + 1) * 128], sqx[:, ft * 128:(ft + 1) * 128], identf[:]),
                        reads=[T_sqx, T_c], writes=[Tb[4 + ft // 4]])
                half = t % 2
                for q in range(4):
                    P.op("scalar", lambda e, q=q: e.copy(out=xnblk[:, q * 4:(q + 1) * 4, half * 128:(half + 1) * 128],
                                                         in_=B[4 + q][:].rearrange("p (f t) -> p f t", f=4)),
                         reads=[Tb[4 + q]], writes=[T_xnblk])
                if half == 1:
                    blk_i = (h * NP + t * 128) // 256
                    P.op("sync", lambda e: e.dma_start(out=xnT_o[blk_i], in_=xnblk[:]), reads=[T_xnblk], writes=[T_xno], dma="xno")

            for t in range(NP // 128):
                stage5(t)

        for h in range(npass):
            do_pass(h)
        P.emit()
    return nc


def l1_inputs(x, a_norm_g, a_w_in, a_dw_w, a_dw_b, a_ln_g, a_ln_b, a_w_out, ntok=2048, ncores=NCORES):
    w_in_l = np.ascontiguousarray(a_w_in.reshape(16, 128, 48, 128).transpose(2, 1, 0, 3))
    w_out_l = np.ascontiguousarray(a_w_out.reshape(16, 128, D).transpose(1, 0, 2))
    gbc = np.ascontiguousarray(np.broadcast_to(a_norm_g.reshape(16, 128).T[:, :, None], (128, 16, 128)).reshape(128, D)).astype(np.float32)
    dww = np.ascontiguousarray(a_dw_w.reshape(CONV_W, 16, 128).transpose(2, 1, 0)).astype(np.float32)
    cvec = np.ascontiguousarray(np.concatenate(
        [a_dw_b.reshape(16, 128).T, a_ln_g.reshape(16, 128).T, a_ln_b.reshape(16, 128).T], axis=1)).astype(np.float32)
    identf = np.eye(128, dtype=np.float32)
    identb = identf.astype(ml_dtypes.bfloat16)
    xp = np.concatenate([np.zeros((HALO, D), np.float32), x], axis=0)
    maps = []
    for c in range(ncores):
        maps.append({"x": np.ascontiguousarray(xp[c * ntok: c * ntok + ntok + HALO]), "w_in": w_in_l, "w_out": w_out_l,
                     "gbc": gbc, "dww": dww, "cvec": cvec, "identb": identb, "identf": identf})
    return maps


_CACHE = {}


def _get(name, fn):
    if name not in _CACHE:
        _CACHE[name] = fn()
    return _CACHE[name]


def kernel(x, a_norm_g, a_w_in, a_dw_w, a_dw_b, a_ln_g, a_ln_b, a_w_out,
           kv_norm_g, w_kv, k_norm_g, b_norm_g, b_w_in, b_q_norm_g,
           b_lambda, b_subln_g, b_w_out):
    f = lambda a: np.asarray(a, dtype=np.float32)
    x2 = f(x).reshape(S, D)
    cores = list(range(NCORES))
    ntok = S // NCORES
    nc1 = _get("l1", lambda: build_l1(ntok, 1024))
    m1 = l1_inputs(x2, f(a_norm_g)[0], f(a_w_in)[0], f(a_dw_w)[0], f(a_dw_b)[0], f(a_ln_g)[0], f(a_ln_b)[0], f(a_w_out)[0],
                   ntok=ntok, ncores=NCORES)
    r1 = run_bass_kernel_spmd(nc1, m1, core_ids=cores).results
    x1 = [r1[c]["x1"] for c in cores]
    xnT_blocks = np.concatenate([r1[c]["xnT"] for c in cores], axis=0)
    nc2 = _get("l2", lambda: build_l2(S))
    xnT_full = np.ascontiguousarray(xnT_blocks.transpose(2, 1, 0, 3)).reshape(16, 128, S)
    m2 = l2_inputs(xnT_full, f(w_kv), f(b_w_in)[0], f(kv_norm_g), f(b_norm_g)[0], f(k_norm_g), f(b_q_norm_g)[0],
                   f(b_lambda)[0], f(b_subln_g)[0], s_len=S)
    r2 = run_bass_kernel_spmd(nc2, m2, core_ids=cores).results
    o_full = np.concatenate([r2[c]["o"] for c in cores], axis=1)
    nc3 = _get("l3", lambda: build_l3(ntok))
    wo = np.ascontiguousarray(f(b_w_out)[0].reshape(16, 128, D).transpose(1, 0, 2))
    m3 = []
    for c in cores:
        oc = o_full[c * ntok:(c + 1) * ntok]
        oT = np.ascontiguousarray(oc.reshape(ntok // 128, 128, 16, 128).transpose(0, 3, 2, 1))
        m3.append({"oT": oT, "x1": x1[c], "wo": wo})
    r3 = run_bass_kernel_spmd(nc3, m3, core_ids=cores).results
    out = np.concatenate([r3[c]["y"] for c in cores], axis=0).reshape(1, S, D).astype(np.float32)
    return out
```

```python
import contextlib
import math
import numpy as np
import ml_dtypes
import concourse.bass as bass
import concourse.mybir as mybir
from concourse.bass_utils import run_bass_kernel_spmd

F32 = mybir.dt.float32
BF16 = mybir.dt.bfloat16
ALU = mybir.AluOpType
ACT = mybir.ActivationFunctionType
AX = mybir.AxisListType

NCORES = 8
D = 2048
S = 16384
EPS = 1e-6
CONV_W = 31
HALO = CONV_W - 1
DH = 128
LAM_INIT = 0.8 - 0.6 * math.exp(-0.3 * (2 - 1))

ENGS = ("tensor", "vector", "scalar", "gpsimd", "sync")


class T:
    __slots__ = ("name", "last_w", "readers")

    def __init__(self, name=""):
        self.name = name
        self.last_w = None
        self.readers = []


class Prog:
    def __init__(self, nc):
        self.nc = nc
        self.ops = []
        self.groups = {}

    def op(self, eng, fn, reads=(), writes=(), dma=None, inc=16):
        i = len(self.ops)
        deps = set()
        for t in reads:
            if t.last_w is not None:
                deps.add(t.last_w)
        for t in writes:
            if t.last_w is not None:
                deps.add(t.last_w)
            deps.update(t.readers)
        deps.discard(i)
        for t in reads:
            t.readers.append(i)
        for t in writes:
            t.last_w = i
            t.readers = []
        gneed = {}
        for jd in deps:
            g = self.ops[jd]["dma"]
            if g is not None:
                gneed[g] = self.groups[g]
        o = dict(i=i, eng=eng, fn=fn, deps=deps, dma=dma, sig=False, sidx=0, inc=inc, gneed=gneed)
        if dma is not None:
            self.groups[dma] = self.groups.get(dma, 0) + inc
            o["sidx"] = self.groups[dma]
        self.ops.append(o)
        return i

    def emit(self, final_wait_engine="sync"):
        nc = self.nc
        ops = self.ops
        for o in ops:
            for j in o["deps"]:
                d = ops[j]
                if d["dma"] is not None:
                    continue
                if d["eng"] == "tensor" and o["eng"] == "tensor" and o["dma"] is None:
                    continue
                d["sig"] = True
        cnt = {e: 0 for e in ENGS}
        for o in ops:
            if o["dma"] is None and o["sig"]:
                cnt[o["eng"]] += 1
                o["sidx"] = cnt[o["eng"]]
        with contextlib.ExitStack() as es:
            esem = {e: es.enter_context(nc.semaphore("p_" + e)) for e in ENGS}
            gsem = {g: es.enter_context(nc.semaphore("g_%d" % k))
                    for k, g in enumerate(self.groups)}
            block = es.enter_context(nc.Block())
            final = dict(self.groups)

            def make(ename):
                def body(eng):
                    waited_e = {e: 0 for e in ENGS}
                    waited_g = {g: 0 for g in self.groups}
                    for o in ops:
                        if o["eng"] != ename:
                            continue
                        need_e = {}
                        need_g = o["gneed"]
                        for j in o["deps"]:
                            d = ops[j]
                            if d["dma"] is not None:
                                continue
                            else:
                                if d["eng"] == "tensor" and ename == "tensor" and o["dma"] is None:
                                    continue
                                need_e[d["eng"]] = max(need_e.get(d["eng"], 0), d["sidx"])
                        for e, v in need_e.items():
                            if v > waited_e[e]:
                                eng.wait_ge(esem[e], v)
                                waited_e[e] = v
                        for g, v in need_g.items():
                            if v > waited_g[g]:
                                eng.wait_ge(gsem[g], v)
                                waited_g[g] = v
                        ins = o["fn"](eng)
                        if o["dma"] is not None:
                            ins.then_inc(gsem[o["dma"]], o["inc"])
                        elif o["sig"]:
                            ins.then_inc(esem[ename], 1)
                    if ename == final_wait_engine:
                        for g, v in final.items():
                            if v > waited_g[g]:
                                eng.wait_ge(gsem[g], v)
                        for e in ENGS:
                            if e != ename and cnt[e] > waited_e[e]:
                                eng.wait_ge(esem[e], cnt[e])
                return body

            block.tensor(make("tensor"))
            block.vector(make("vector"))
            block.scalar(make("scalar"))
            block.gpsimd(make("gpsimd"))
            block.sync(make("sync"))


def build_l2(s_len=S, dbg=False):
    QB = 256
    nblk = s_len // QB
    ntile = s_len // 128
    nc = bass.Bass("TRN2", target_bir_lowering=False)
    xnT = nc.dram_tensor("xnT", [nblk, 128, 16, QB], BF16, kind="ExternalInput").ap()
    wA = nc.dram_tensor("wA", [128, 16, 512], F32, kind="ExternalInput").ap()
    wB = nc.dram_tensor("wB", [128, 16, 512], F32, kind="ExternalInput").ap()
    gcol = nc.dram_tensor("gcol", [128, 32], F32, kind="ExternalInput").ap()
    gqk = nc.dram_tensor("gqk", [128, 2], F32, kind="ExternalInput").ap()
    gsub = nc.dram_tensor("gsub", [128, 256], F32, kind="ExternalInput").ap()
    lamp = nc.dram_tensor("lamp", [128, 512], F32, kind="ExternalInput").ap()
    o_out = nc.dram_tensor("o", [s_len, 256], BF16, kind="ExternalOutput").ap()

    P = Prog(nc)
    with contextlib.ExitStack() as es:
        def sb(name, shape, dt):
            return es.enter_context(nc.sbuf_tensor(name, shape, dt))

        KT = sb("KT", [128, 2, s_len], BF16)
        V = sb("V", [128, ntile, 257], BF16)
        wAb = sb("wAb", [128, 16, 512], BF16)
        wBb = sb("wBb", [128, 16, 512], BF16)
        xb = [sb("xb%d" % i, [128, 16, QB], BF16) for i in range(2)]
        QTb = [sb("QTb%d" % i, [128, 2, QB], BF16) for i in range(2)]
        PT = [sb("PT%d" % i, [128, 4, QB], BF16) for i in range(3)]
        sq = sb("sq", [128, 1024], BF16)
        rst = sb("rst", [128, 1024], F32)
        th = [sb("th%d" % i, [128, 256], F32) for i in range(2)]
        Zg = [sb("Zg%d" % i, [128, 256], BF16) for i in range(4)]
        ea = [sb("ea%d" % i, [128, 256], F32) for i in range(2)]
        eo = [sb("eo%d" % i, [128, 256], F32) for i in range(2)]
        esq = ea
        eof = ea
        ob = [sb("ob%d" % i, [128, 256], BF16) for i in range(4)]
        small = [sb("small%d" % i, [128, 8], F32) for i in range(2)]
        gcol_s = sb("gcol_s", [128, 32], F32)
        gqk_s = sb("gqk_s", [128, 2], F32)
        gsub_s = sb("gsub_s", [128, 256], F32)
        lam_v = sb("lam_v", [128, 4], F32)
        ones = sb("ones", [128, 128], BF16)
        wst = [sb("wst%d" % i, [128, 1, 512], F32) for i in range(2)]
        PS = [es.enter_context(nc.psum_tensor("ps%d" % i, [128, 1024], F32)) for i in range(4)]

        Tb = [T("bank%d" % i) for i in range(8)]
        T_KT = [T() for _ in range(nblk)]
        T_V = [T() for _ in range(ntile)]
        T_wA, T_wB, T_g, T_lam, T_ones = T(), T(), T(), T(), T()
        T_wst = [T(), T()]
        T_xb = [T(), T()]
        T_QT = [T(), T()]
        T_PT = [T(), T(), T()]
        T_sq, T_rst = [T(), T()], [T(), T()]
        T_th = [T(), T()]
        T_Zg = [T() for _ in range(4)]
        T_e = [T(), T()]
        T_ob = [T() for _ in range(4)]
        T_out = T()

        P.op("sync", lambda e: e.dma_start(out=gcol_s[:], in_=gcol), writes=[T_g], dma="c0")
        P.op("sync", lambda e: e.dma_start(out=gqk_s[:], in_=gqk), writes=[T_g], dma="c0")
        P.op("sync", lambda e: e.dma_start(out=gsub_s[:], in_=gsub), writes=[T_g], dma="c0")
        P.op("sync", lambda e: e.dma_start(out=ea[0][:], in_=lamp[:, 0:256]), writes=[T_lam], dma="c0")
        P.op("sync", lambda e: e.dma_start(out=ea[1][:], in_=lamp[:, 256:512]), writes=[T_lam], dma="c0")
        P.op("gpsimd", lambda e: e.memset(ones[:], 1.0), writes=[T_ones])
        P.op("gpsimd", lambda e: e.memset(V[:, :, 256:257], 1.0), writes=T_V)
        P.op("vector", lambda e: e.tensor_scalar_mul(out=gqk_s[:, 0:1], in0=gqk_s[:, 0:1], scalar1=math.sqrt(128.0)),
             reads=[T_g], writes=[T_g])
        P.op("vector", lambda e: e.tensor_scalar_mul(out=gsub_s[:], in0=gsub_s[:], scalar1=(1.0 - LAM_INIT) * 16.0),
             reads=[T_g], writes=[T_g])
        P.op("vector", lambda e: e.tensor_tensor(out=eo[0][:, 0:128], in0=ea[0][:, 0:128], in1=ea[0][:, 128:256], op=ALU.mult),
             reads=[T_lam], writes=[T_lam])
        P.op("vector", lambda e: e.tensor_tensor(out=eo[0][:, 128:256], in0=ea[1][:, 0:128], in1=ea[1][:, 128:256], op=ALU.mult),
             reads=[T_lam], writes=[T_lam])
        P.op("vector", lambda e: e.reduce_sum(out=lam_v[:, 0:2], in_=eo[0][:].rearrange("p (a b) -> p a b", a=2), axis=AX.X),
             reads=[T_lam], writes=[T_lam, T_e[0], T_e[1]])
        P.op("scalar", lambda e: e.activation(out=lam_v[:, 2:4], in_=lam_v[:, 0:2], func=ACT.Exp),
             reads=[T_lam], writes=[T_lam])
        P.op("vector", lambda e: e.tensor_tensor(out=lam_v[:, 0:1], in0=lam_v[:, 3:4], in1=lam_v[:, 2:3], op=ALU.subtract),
             reads=[T_lam], writes=[T_lam])
        P.op("vector", lambda e: e.tensor_scalar_add(out=lam_v[:, 1:2], in0=lam_v[:, 0:1], scalar1=-LAM_INIT),
             reads=[T_lam], writes=[T_lam])
        nlam = lam_v[:, 1:2]
        k = 0
        for (wsrc, wdst, Tw) in ((wA, wAb, T_wA), (wB, wBb, T_wB)):
            for ch in range(16):
                b = k % 2
                P.op("sync", lambda e, b=b, ch=ch, wsrc=wsrc: e.dma_start(out=wst[b][:], in_=wsrc[:, ch:ch + 1, :]),
                     writes=[T_wst[b]], dma="wst%d" % b)
                for f in range(1):
                    ft = ch + f
                    for half in range(2):
                        if wsrc is wA:
                            gi = ft if half == 0 else 16 + ft
                        else:
                            gi = ft if half == 0 else 16 + ft
                        P.op("vector", lambda e, b=b, f=f, ft=ft, half=half, gi=gi, wdst=wdst: e.tensor_scalar_mul(
                            out=wdst[:, ft, half * 256:(half + 1) * 256], in0=wst[b][:, f, half * 256:(half + 1) * 256],
                            scalar1=gcol_s[:, gi:gi + 1]),
                            reads=[T_wst[b], T_g], writes=[Tw])
                k += 1

        def load_xb(j):
            b = j % 2
            P.op("sync", lambda e: e.dma_start(out=xb[b][:], in_=xnT[j]), writes=[T_xb[b]], dma="xb%d" % b)

        load_xb(0)
        def do_block(j):
            jb = j % 2
            if j + 1 < nblk:
                load_xb(j + 1)
            def proj_kq(i):
                for ft in range(16):
                    P.op("tensor", lambda e, ft=ft: e.matmul(
                        PS[0][:, i * 256:(i + 1) * 256], lhsT=wAb[:, ft, i * 128:(i + 1) * 128], rhs=xb[jb][:, ft, :],
                        start=(ft == 0), stop=(ft == 15)),
                        reads=[T_wA, T_xb[jb]], writes=[Tb[i // 2]])

            def proj_vz(ti):
                for ft in range(16):
                    P.op("tensor", lambda e, ft=ft: e.matmul(
                        PS[1][:, ti * 512:(ti + 1) * 512], lhsT=xb[jb][:, ft, ti * 128:(ti + 1) * 128], rhs=wBb[:, ft, :],
                        start=(ft == 0), stop=(ft == 15)),
                        reads=[T_wB, T_xb[jb]], writes=[Tb[2 + ti]])

            def square(hf):
                P.op("scalar", lambda e: e.activation(out=sq[:, hf * 512:(hf + 1) * 512], in_=PS[0][:, hf * 512:(hf + 1) * 512],
                                                      func=ACT.Square),
                     reads=[Tb[hf]], writes=[T_sq[hf]])

            def ssq(hf):
                for i in (2 * hf, 2 * hf + 1):
                    P.op("tensor", lambda e, i=i: e.matmul(PS[2][:, i * 256:(i + 1) * 256], lhsT=ones[:], rhs=sq[:, i * 256:(i + 1) * 256],
                                                          start=True, stop=True),
                         reads=[T_ones, T_sq[hf]], writes=[Tb[4 + hf]])

            def rstd(hf):
                P.op("scalar", lambda e: e.activation(out=rst[:, hf * 512:(hf + 1) * 512], in_=PS[2][:, hf * 512:(hf + 1) * 512],
                                                      func=ACT.Ln, bias=128.0 * EPS),
                     reads=[Tb[4 + hf]], writes=[T_rst[hf]])
                P.op("scalar", lambda e: e.activation(out=rst[:, hf * 512:(hf + 1) * 512], in_=rst[:, hf * 512:(hf + 1) * 512],
                                                      func=ACT.Exp, scale=-0.5),
                     reads=[T_rst[hf]], writes=[T_rst[hf]])

            def vz_tile(ti):
                tt = 2 * j + ti
                zi = jb * 2 + ti
                P.op("scalar", lambda e, ti=ti, tt=tt: e.copy(out=V[:, tt, 0:256], in_=PS[1][:, ti * 512:ti * 512 + 256]),
                     reads=[Tb[2 + ti]], writes=[T_V[tt]])
                P.op("scalar", lambda e, ti=ti: e.activation(out=th[ti][:], in_=PS[1][:, ti * 512 + 256:(ti + 1) * 512],
                                                             func=ACT.Exp, scale=-1.0),
                     reads=[Tb[2 + ti]], writes=[T_th[ti]])
                P.op("gpsimd", lambda e, ti=ti: e.tensor_scalar_add(out=th[ti][:], in0=th[ti][:], scalar1=1.0),
                     reads=[T_th[ti]], writes=[T_th[ti]])
                P.op("vector", lambda e, ti=ti: e.reciprocal(out=th[ti][:], in_=th[ti][:]),
                     reads=[T_th[ti]], writes=[T_th[ti]])
                P.op("vector", lambda e, ti=ti, zi=zi: e.tensor_tensor(
                    out=Zg[zi][:], in0=th[ti][:], in1=PS[1][:, ti * 512 + 256:(ti + 1) * 512], op=ALU.mult),
                    reads=[T_th[ti], Tb[2 + ti]], writes=[T_Zg[zi]])

            proj_kq(2); proj_kq(3)
            square(1)
            proj_kq(0); proj_kq(1)
            ssq(1)
            square(0)
            rstd(1)
            proj_vz(0)
            P.op("vector", lambda e: e.scalar_tensor_tensor(
                out=QTb[jb][:], in0=PS[0][:, 512:1024].rearrange("p (m t) -> p m t", m=2),
                scalar=gqk_s[:, 1:2], in1=rst[:, 512:1024].rearrange("p (m t) -> p m t", m=2), op0=ALU.mult, op1=ALU.mult),
                reads=[Tb[1], T_rst[1], T_g], writes=[T_QT[jb]])
            ssq(0)
            rstd(0)
            proj_vz(1)
            P.op("vector", lambda e: e.scalar_tensor_tensor(
                out=KT[:, :, j * QB:(j + 1) * QB], in0=PS[0][:, 0:512].rearrange("p (m t) -> p m t", m=2),
                scalar=gqk_s[:, 0:1], in1=rst[:, 0:512].rearrange("p (m t) -> p m t", m=2), op0=ALU.mult, op1=ALU.mult),
                reads=[Tb[0], T_rst[0], T_g], writes=[T_KT[j]])
            for ti in range(2):
                vz_tile(ti)

            npair = j + 1
            def do_qk(p):
                sp = p % 2
                for kl in range(2):
                    kt = 2 * p + kl
                    for m in range(2):
                        P.op("tensor", lambda e, kt=kt, kl=kl, m=m, sp=sp: e.matmul(
                            PS[sp][:, (kl * 2 + m) * 256:(kl * 2 + m + 1) * 256],
                            lhsT=KT[:, m, kt * 128:(kt + 1) * 128], rhs=QTb[jb][:, m, :], start=True, stop=True),
                            reads=[T_KT[kt // 2], T_QT[jb]], writes=[Tb[2 * sp + kl]])

            def do_pair(p):
                sp = p % 2
                pb = p % 3
                last = (p == npair - 1)
                P.op("scalar", lambda e, sp=sp, pb=pb: e.activation(
                    out=PT[pb][:].rearrange("p a q -> p (a q)"), in_=PS[sp][:], func=ACT.Exp),
                    reads=[Tb[2 * sp], Tb[2 * sp + 1]], writes=[T_PT[pb]])
                if last:
                    P.op("vector", lambda e, pb=pb: e.memset(PT[pb][64:128, 0:2, 0:64], 0.0), writes=[T_PT[pb]])
                    P.op("vector", lambda e, pb=pb: e.memset(PT[pb][64:128, 2:4, 128:192], 0.0), writes=[T_PT[pb]])
                for kl in range(2):
                    kt = 2 * p + kl
                    for qt in range(2):
                        if last and kl == 1 and qt == 0:
                            continue
                        for m in range(2):
                            P.op("tensor", lambda e, kt=kt, kl=kl, m=m, qt=qt, pb=pb: e.matmul(
                                PS[2 + qt][:, m * 512:m * 512 + 257],
                                lhsT=PT[pb][:, kl * 2 + m, qt * 128:(qt + 1) * 128], rhs=V[:, kt, :],
                                start=(kt == 0), stop=(kt == 2 * j + qt)),
                                reads=[T_PT[pb], T_V[kt]], writes=[Tb[4 + 2 * qt + m]])

            do_qk(0)
            for p in range(npair):
                if p + 1 < npair:
                    do_qk(p + 1)
                do_pair(p)

            def epilogue(qt):
                tt = 2 * j + qt
                zi = jb * 2 + qt
                O1 = PS[2 + qt][:, 0:257]
                O2 = PS[2 + qt][:, 512:769]
                b1, b2 = Tb[4 + 2 * qt], Tb[4 + 2 * qt + 1]
                sm = small[qt]
                P.op("vector", lambda e, O1=O1, sm=sm: e.reciprocal(out=sm[:, 0:1], in_=O1[:, 256:257]),
                     reads=[b1], writes=[T_e[qt]])
                P.op("vector", lambda e, O2=O2, sm=sm: e.reciprocal(out=sm[:, 1:2], in_=O2[:, 256:257]),
                     reads=[b2], writes=[T_e[qt]])
                P.op("vector", lambda e, sm=sm: e.tensor_tensor(out=sm[:, 2:3], in0=sm[:, 1:2], in1=nlam, op=ALU.mult),
                     reads=[T_e[qt], T_lam], writes=[T_e[qt]])
                P.op("vector", lambda e, O1=O1, sm=sm, qt=qt: e.tensor_scalar_mul(out=ea[qt][:], in0=O1[:, 0:256], scalar1=sm[:, 0:1]),
                     reads=[b1, T_e[qt]], writes=[T_e[qt]])
                P.op("vector", lambda e, O2=O2, sm=sm, qt=qt: e.scalar_tensor_tensor(
                    out=eo[qt][:], in0=O2[:, 0:256], scalar=sm[:, 2:3], in1=ea[qt][:], op0=ALU.mult, op1=ALU.add),
                    reads=[b2, T_e[qt]], writes=[T_e[qt]])
                P.op("scalar", lambda e, qt=qt: e.activation(out=esq[qt][:], in_=eo[qt][:], func=ACT.Square),
                     reads=[T_e[qt]], writes=[T_e[qt]])
                P.op("vector", lambda e, sm=sm, qt=qt: e.reduce_sum(out=sm[:, 3:4], in_=esq[qt][:], axis=AX.X),
                     reads=[T_e[qt]], writes=[T_e[qt]])
                P.op("scalar", lambda e, sm=sm: e.activation(out=sm[:, 4:5], in_=sm[:, 3:4], func=ACT.Ln, bias=256.0 * EPS),
                     reads=[T_e[qt]], writes=[T_e[qt]])
                P.op("scalar", lambda e, sm=sm: e.activation(out=sm[:, 4:5], in_=sm[:, 4:5], func=ACT.Exp, scale=-0.5),
                     reads=[T_e[qt]], writes=[T_e[qt]])
                P.op("vector", lambda e, sm=sm, qt=qt: e.scalar_tensor_tensor(
                    out=eof[qt][:], in0=eo[qt][:], scalar=sm[:, 4:5], in1=gsub_s[:], op0=ALU.mult, op1=ALU.mult),
                    reads=[T_e[qt], T_g], writes=[T_e[qt]])
                P.op("gpsimd", lambda e, qt=qt, zi=zi: e.tensor_tensor(out=ob[zi][:], in0=eof[qt][:], in1=Zg[zi][:], op=ALU.mult),
                     reads=[T_e[qt], T_Zg[zi]], writes=[T_ob[zi]])
                P.op("gpsimd", lambda e, tt=tt, zi=zi: e.dma_start(out=o_out[tt * 128:(tt + 1) * 128, :], in_=ob[zi][:]),
                     reads=[T_ob[zi]], writes=[T_out], dma="ob%d" % zi)
            for qt in range(2):
                epilogue(qt)

        for j in range(nblk):
            do_block(j)

        if dbg:
            dk = nc.dram_tensor("d_KT", [128, 2, s_len], BF16, kind="ExternalOutput").ap()
            dv = nc.dram_tensor("d_V", [128, ntile, 257], BF16, kind="ExternalOutput").ap()
            dq = nc.dram_tensor("d_QT", [128, 2, QB], BF16, kind="ExternalOutput").ap()
            dr = nc.dram_tensor("d_rst", [128, 1024], F32, kind="ExternalOutput").ap()
            dz = nc.dram_tensor("d_Zg", [128, 256], BF16, kind="ExternalOutput").ap()
            dp = nc.dram_tensor("d_PT", [128, 4, QB], BF16, kind="ExternalOutput").ap()
            de = nc.dram_tensor("d_eo", [128, 256], F32, kind="ExternalOutput").ap()
            dea = nc.dram_tensor("d_ea", [128, 256], F32, kind="ExternalOutput").ap()
            dsm = nc.dram_tensor("d_sm", [128, 8], F32, kind="ExternalOutput").ap()
            dl = nc.dram_tensor("d_lam", [128, 4], F32, kind="ExternalOutput").ap()
            dw = nc.dram_tensor("d_wA", [128, 16, 512], BF16, kind="ExternalOutput").ap()
            allT = T_KT + T_V + T_QT + T_rst + T_Zg + T_PT + T_e + [T_lam, T_wA]
            for (dst, src) in ((dk, KT), (dv, V), (dq, QTb[(nblk - 1) % 2]), (dr, rst), (dz, Zg[((nblk - 1) % 2) * 2]),
                               (dp, PT[(nblk - 1) % 3]), (de, eo[0]), (dea, ea[0]), (dsm, small[0]), (dl, lam_v), (dw, wAb)):
                P.op("sync", lambda e, dst=dst, src=src: e.dma_start(out=dst, in_=src[:]), reads=allT, writes=[T_out], dma="dbg")
        P.emit()
    return nc


def l2_inputs(xnT_full, w_kv, b_w_in, kv_norm_g, b_norm_g, k_norm_g, q_norm_g, lam, subln_g, s_len=S):
    QB = 256
    nblk = s_len // QB
    xb = np.ascontiguousarray(
        xnT_full.reshape(16, 128, nblk, QB).transpose(2, 1, 0, 3))

    def wl(w):
        return np.ascontiguousarray(w.reshape(16, 128, w.shape[1]).transpose(1, 0, 2))

    gcol = np.ascontiguousarray(np.concatenate(
        [kv_norm_g.reshape(16, 128).T, b_norm_g.reshape(16, 128).T], axis=1)).astype(np.float32)
    gqk = np.ascontiguousarray(np.stack([k_norm_g, q_norm_g], axis=1)).astype(np.float32)
    gsub = np.ascontiguousarray(np.broadcast_to(subln_g.reshape(1, 256), (128, 256))).astype(np.float32)
    lamp = np.ascontiguousarray(np.broadcast_to(lam.reshape(1, 512), (128, 512))).astype(np.float32)
    maps = []
    for c in range(NCORES):
        kc = w_kv[:, c * 256:(c + 1) * 256]
        vc = w_kv[:, 2048 + c * 256:2048 + (c + 1) * 256]
        qc = b_w_in[:, c * 256:(c + 1) * 256]
        zc = b_w_in[:, 2048 + c * 256:2048 + (c + 1) * 256]
        maps.append({
            "xnT": xb,
            "wA": wl(np.concatenate([kc, qc], axis=1)),
            "wB": wl(np.concatenate([vc, zc], axis=1)),
            "gcol": gcol, "gqk": gqk, "gsub": gsub, "lamp": lamp,
        })
    return maps


def build_l3(ntok=2048):
    ntt = ntok // 128
    nc = bass.Bass("TRN2", target_bir_lowering=False)
    oT = nc.dram_tensor("oT", [ntt, 128, 16, 128], BF16, kind="ExternalInput").ap()
    x1 = nc.dram_tensor("x1", [ntok, D], F32, kind="ExternalInput").ap()
    wo = nc.dram_tensor("wo", [128, 16, D], F32, kind="ExternalInput").ap()
    y = nc.dram_tensor("y", [ntok, D], F32, kind="ExternalOutput").ap()
    P = Prog(nc)
    with contextlib.ExitStack() as es:
        def sb(name, shape, dt):
            return es.enter_context(nc.sbuf_tensor(name, shape, dt))
        wob = sb("wob", [128, 16, D], BF16)
        ot = [sb("ot%d" % i, [128, 16, 128], BF16) for i in range(2)]
        xt = [sb("xt%d" % i, [128, D], F32) for i in range(2)]
        yt = [sb("yt%d" % i, [128, D], F32) for i in range(2)]
        PS = [es.enter_context(nc.psum_tensor("ps%d" % i, [128, 2048], F32)) for i in range(2)]
        T_w = [T() for _ in range(16)]
        T_ot, T_xt, T_yt, T_ps = [T(), T()], [T(), T()], [T(), T()], [T(), T()]
        T_y = T()
        wstg = [sb("wstg%d" % i, [128, D], F32) for i in range(3)]
        T_wstg = [T(), T(), T()]
        for ct in range(16):
            i = ct % 3
            P.op("sync", lambda e, ct=ct, i=i: e.dma_start(out=wstg[i][:], in_=wo[:, ct, :]), writes=[T_wstg[i]], dma="w%d" % i)
            if ct % 2 == 0:
                P.op("vector", lambda e, ct=ct, i=i: e.tensor_copy(out=wob[:, ct, :], in_=wstg[i][:]), reads=[T_wstg[i]], writes=[T_w[ct]])
            else:
                P.op("scalar", lambda e, ct=ct, i=i: e.copy(out=wob[:, ct, :], in_=wstg[i][:]), reads=[T_wstg[i]], writes=[T_w[ct]])

        def loads(tt):
            b = tt % 2
            P.op("sync", lambda e: e.dma_start(out=ot[b][:], in_=oT[tt]), writes=[T_ot[b]], dma="ot%d" % b)
            P.op("sync", lambda e: e.dma_start(out=xt[b][:], in_=x1[tt * 128:(tt + 1) * 128, :]), writes=[T_xt[b]], dma="xt%d" % b)

        def tile(tt):
            b = tt % 2
            if tt + 1 < ntt:
                loads(tt + 1)
            for fb in range(4):
                for ct in range(16):
                    P.op("tensor", lambda e, fb=fb, ct=ct: e.matmul(
                        PS[b][:, fb * 512:(fb + 1) * 512], lhsT=ot[b][:, ct, :], rhs=wob[:, ct, fb * 512:(fb + 1) * 512],
                        start=(ct == 0), stop=(ct == 15)),
                        reads=[T_ot[b], T_w[ct]], writes=[T_ps[b]])
            P.op("vector", lambda e: e.tensor_tensor(out=yt[b][:], in0=PS[b][:], in1=xt[b][:], op=ALU.add),
                 reads=[T_ps[b], T_xt[b]], writes=[T_yt[b]])
            P.op("gpsimd", lambda e: e.dma_start(out=y[tt * 128:(tt + 1) * 128, :], in_=yt[b][:]),
                 reads=[T_yt[b]], writes=[T_y], dma="yt%d" % b)

        loads(0)
        for tt in range(ntt):
            tile(tt)
        P.emit()
    return nc


def build_l1(ntok=2048, NP=1024):
    npass = ntok // NP
    NB = NP // 512
    NL = NP + HALO
    nc = bass.Bass("TRN2", target_bir_lowering=False)
    x = nc.dram_tensor("x", [ntok + HALO, D], F32, kind="ExternalInput").ap()
    w_in = nc.dram_tensor("w_in", [48, 128, 16, 128], F32, kind="ExternalInput").ap()
    w_out = nc.dram_tensor("w_out", [128, 16, D], F32, kind="ExternalInput").ap()
    gbc_d = nc.dram_tensor("gbc", [128, D], F32, kind="ExternalInput").ap()
    dww_d = nc.dram_tensor("dww", [128, 16, CONV_W], F32, kind="ExternalInput").ap()
    cvec_d = nc.dram_tensor("cvec", [128, 48], F32, kind="ExternalInput").ap()
    identb_d = nc.dram_tensor("identb", [128, 128], BF16, kind="ExternalInput").ap()
    identf_d = nc.dram_tensor("identf", [128, 128], F32, kind="ExternalInput").ap()
    x1_o = nc.dram_tensor("x1", [ntok, D], F32, kind="ExternalOutput").ap()
    xnT_o = nc.dram_tensor("xnT", [ntok // 256, 128, 16, 256], BF16, kind="ExternalOutput").ap()
    wi_b = nc.dram_tensor("wi_b", [48, 128, 2048], BF16).ap()
    wo_b = nc.dram_tensor("wo_b", [16, 128, 2048], BF16).ap()

    P = Prog(nc)
    with contextlib.ExitStack() as es:
        def sb(name, shape, dt):
            return es.enter_context(nc.sbuf_tensor(name, shape, dt))

        HT_N = 16 * NL
        ARENA = max(HT_N + 6 * 2048 + 2 * CONV_W * 128, 16 * D)
        arena = sb("arena", [128, ARENA], BF16)
        hT = arena[:, 0:HT_N].rearrange("p (f t) -> p f t", f=16)
        wt = [arena[:, HT_N + i * 2048: HT_N + (i + 1) * 2048].rearrange("p (f n) -> p f n", f=16) for i in range(6)]
        wa, wb_, wz = wt[0:2], wt[2:4], wt[4:6]
        dgo = HT_N + 6 * 2048
        dg = [arena[:, dgo + i * CONV_W * 128: dgo + (i + 1) * CONV_W * 128].rearrange("p (j n) -> p j n", j=CONV_W) for i in range(2)]
        wob = arena[:, 0:16 * D].rearrange("p (c n) -> p c n", c=16)
        cvT = sb("cvT", [128, 16, NP], BF16)
        yT = [sb("yT%d" % i, [128, NL], BF16) for i in range(2)]
        acc_s = sb("acc_s", [128, NP], F32)
        acc_q = sb("acc_q", [128, NP], F32)
        rstd_bc = sb("rstd_bc", [128, NP], F32)
        nmr_bc = sb("nmr_bc", [128, NP], F32)
        xt = [sb("xt%d" % i, [128, D], F32) for i in range(2)]
        sqx = sb("sqx", [128, D], F32)
        tmp = {k: [sb("%s%d" % (k, i), [128, 512], F32) for i in range(2)] for k in ("ta", "tb", "tc", "td", "te", "tf")}
        xnblk = sb("xnblk", [128, 16, 256], BF16)
        gbc = sb("gbc_s", [128, D], F32)
        dww = sb("dww_s", [128, 16, CONV_W], F32)
        cvec = sb("cvec_s", [128, 48], F32)
        identb = sb("identb_s", [128, 128], BF16)
        identf = sb("identf_s", [128, 128], F32)
        onesf = sb("onesf", [128, 128], F32)
        small = sb("small", [128, 8], F32)
        B = [es.enter_context(nc.psum_tensor("b%d" % i, [128, 512], F32)) for i in range(8)]

        Tb = [T("bank%d" % i) for i in range(8)]
        T_c = T()
        T_hT = T()
        T_wa, T_wb, T_wz, T_dg = [T(), T()], [T(), T()], [T(), T()], [T(), T()]
        T_cv = [T() for _ in range(16)]
        T_yT = [T(), T()]
        T_acc, T_st = T(), T()
        T_xt = [T(), T()]
        T_sqx = T()
        T_tmp = {k: [T(), T()] for k in tmp}
        T_xnblk = T()
        T_sm = T()
        T_x1o, T_xno = T(), T()
        T_wob = T()

        dwb = lambda c: cvec[:, c:c + 1]
        lng = lambda c: cvec[:, 16 + c:17 + c]
        lnb = lambda c: cvec[:, 32 + c:33 + c]

        for (dst, src) in ((gbc, gbc_d), (dww, dww_d), (cvec, cvec_d), (identb, identb_d), (identf, identf_d)):
            P.op("sync", lambda e, dst=dst, src=src: e.dma_start(out=dst[:], in_=src), writes=[T_c], dma="c0")
        P.op("gpsimd", lambda e: e.memset(onesf[:], 1.0), writes=[T_c])

        T_wib = [T() for _ in range(48)]
        T_wobd = [T() for _ in range(16)]
        cvflat = cvT[:].rearrange("p c t -> p (c t)")
        stg_in = [cvflat[:, i * (4 * NP):(i + 1) * (4 * NP)].bitcast(F32) for i in range(4)]
        T_stg_in = [T_cv[4 * i:4 * i + 4] for i in range(4)]
        xnflat = xnblk[:].rearrange("p f t -> p (f t)")
        stg_out = [xnflat[:, i * 2048:(i + 1) * 2048] for i in range(2)]
        T_stg_out = [T(), T()]
        CH = 2 * NP
        per = 2048 // CH
        assert per * CH == 2048

        def wsrc(k):
            if k < 48:
                return w_in[k].rearrange("p f n -> p (f n)"), wi_b[k], T_wib[k]
            return w_out[:, k - 48, :], wo_b[k - 48], T_wobd[k - 48]

        def w_load(k):
            src, _, _ = wsrc(k)
            i = k % 4
            P.op("sync", lambda e: e.dma_start(out=stg_in[i], in_=src), writes=T_stg_in[i], dma="wl%d" % i)

        def w_cast_store(k):
            _, dst, Tdst = wsrc(k)
            i, o = k % 4, k % 2
            if k % 2 == 0:
                P.op("vector", lambda e: e.tensor_copy(out=stg_out[o], in_=stg_in[i]), reads=T_stg_in[i], writes=[T_stg_out[o], T_xnblk])
            else:
                P.op("scalar", lambda e: e.copy(out=stg_out[o], in_=stg_in[i]), reads=T_stg_in[i], writes=[T_stg_out[o], T_xnblk])
            P.op("sync", lambda e: e.dma_start(out=dst, in_=stg_out[o]), reads=[T_stg_out[o]], writes=[Tdst], dma="ws%d" % o)

        assert per == 1
        for k in range(3):
            w_load(k)
        for k in range(64):
            if k + 3 < 64:
                w_load(k + 3)
            w_cast_store(k)

        cnt = {"xt": 0}

        def rstd_from_ss(np_, col_in, col_out, n):
            P.op("scalar", lambda e: e.activation(out=small[0:np_, col_out:col_out + 1], in_=small[0:np_, col_in:col_in + 1],
                                                  func=ACT.Ln, scale=1.0 / n, bias=EPS), reads=[T_sm], writes=[T_sm])
            P.op("scalar", lambda e: e.activation(out=small[0:np_, col_out:col_out + 1], in_=small[0:np_, col_out:col_out + 1],
                                                  func=ACT.Exp, scale=-0.5), reads=[T_sm], writes=[T_sm])

        def norm_tile(row0, np_, col0):
            b = cnt["xt"] % 2
            cnt["xt"] += 1
            P.op("sync", lambda e: e.dma_start(out=xt[b][0:np_, :], in_=x[row0:row0 + np_, :]), writes=[T_xt[b]], dma="xt%d" % b)
            P.op("scalar", lambda e: e.activation(out=sqx[0:np_, :], in_=xt[b][0:np_, :], func=ACT.Square),
                 reads=[T_xt[b]], writes=[T_sqx])
            P.op("vector", lambda e: e.reduce_sum(out=small[0:np_, 0:1], in_=sqx[0:np_, :], axis=AX.X),
                 reads=[T_sqx], writes=[T_sm])
            rstd_from_ss(np_, 0, 1, float(D))
            P.op("scalar", lambda e: e.activation(out=sqx[0:np_, :], in_=xt[b][0:np_, :], func=ACT.Copy, scale=small[0:np_, 1:2]),
                 reads=[T_xt[b], T_sm], writes=[T_sqx])
            for ft in range(16):
                P.op("tensor", lambda e, ft=ft: e.transpose(
                    B[ft // 4][:, (ft % 4) * 128:(ft % 4) * 128 + np_], sqx[0:np_, ft * 128:(ft + 1) * 128], identf[0:np_, 0:np_]),
                    reads=[T_sqx, T_c], writes=[Tb[ft // 4]])
            for q in range(4):
                P.op("vector", lambda e, q=q: e.tensor_tensor(
                    out=hT[:, q * 4:(q + 1) * 4, col0:col0 + np_],
                    in0=B[q][:].rearrange("p (f t) -> p f t", f=4)[:, :, 0:np_],
                    in1=gbc[:, q * 512:(q + 1) * 512].rearrange("p (f t) -> p f t", f=4)[:, :, 0:np_], op=ALU.mult),
                    reads=[Tb[q], T_c], writes=[T_hT])

        def do_pass(h):
            r_h = h * NP
            r_o = HALO + h * NP
            norm_tile(r_h, HALO, 0)
            for t in range(NP // 128):
                norm_tile(r_o + t * 128, 128, HALO + t * 128)
            P.op("gpsimd", lambda e: e.memset(acc_s[:], 0.0), writes=[T_acc])
            P.op("gpsimd", lambda e: e.memset(acc_q[:], 0.0), writes=[T_acc])

            def prefetch1(c):
                b = c % 2
                P.op("sync", lambda e: e.dma_start(out=wa[b], in_=wi_b[c].rearrange("p (f n) -> p f n", f=16)),
                     reads=[T_wib[c]], writes=[T_wa[b]], dma="wa%d" % b)
                P.op("sync", lambda e: e.dma_start(out=wb_[b], in_=wi_b[16 + c].rearrange("p (f n) -> p f n", f=16)),
                     reads=[T_wib[16 + c]], writes=[T_wb[b]], dma="wb%d" % b)
                for j in range(CONV_W):
                    P.op("vector", lambda e, j=j: e.tensor_scalar_mul(out=dg[b][:, j, :], in0=identb[:], scalar1=dww[:, c, j:j + 1]),
                         reads=[T_c], writes=[T_dg[b]])

            def stage2(c):
                b = c % 2
                yt_ = yT[b]
                for (w_, T_w, off) in ((wa[b], T_wa[b], 0), (wb_[b], T_wb[b], 32)):
                    for ft in range(16):
                        P.op("tensor", lambda e, w_=w_, off=off, ft=ft: e.matmul(
                            B[6][:, off:off + HALO], lhsT=w_[:, ft, :], rhs=hT[:, ft, 0:HALO], start=(ft == 0), stop=(ft == 15)),
                            reads=[T_w, T_hT], writes=[Tb[6]])
                P.op("scalar", lambda e: e.activation(out=tmp["ta"][0][:, 0:HALO], in_=B[6][:, 32:32 + HALO], func=ACT.Sigmoid),
                     reads=[Tb[6]], writes=[T_tmp["ta"][0]])
                P.op("vector", lambda e: e.tensor_tensor(out=yt_[:, 0:HALO], in0=B[6][:, 0:HALO], in1=tmp["ta"][0][:, 0:HALO], op=ALU.mult),
                     reads=[Tb[6], T_tmp["ta"][0]], writes=[T_yT[b]])
                for tb in range(NB):
                    pa, pb = B[2 * (tb % 2)], B[2 * (tb % 2) + 1]
                    Ta, Tbb = Tb[2 * (tb % 2)], Tb[2 * (tb % 2) + 1]
                    c0 = HALO + tb * 512
                    for (w_, T_w, ps, Tp) in ((wa[b], T_wa[b], pa, Ta), (wb_[b], T_wb[b], pb, Tbb)):
                        for ft in range(16):
                            P.op("tensor", lambda e, w_=w_, ps=ps, ft=ft, c0=c0: e.matmul(
                                ps[:], lhsT=w_[:, ft, :], rhs=hT[:, ft, c0:c0 + 512], start=(ft == 0), stop=(ft == 15)),
                                reads=[T_w, T_hT], writes=[Tp])
                    sg = tmp["ta"][tb % 2]
                    P.op("scalar", lambda e, pb=pb, sg=sg: e.activation(out=sg[:], in_=pb[:], func=ACT.Sigmoid),
                         reads=[Tbb], writes=[T_tmp["ta"][tb % 2]])
                    P.op("vector", lambda e, pa=pa, sg=sg, c0=c0: e.tensor_tensor(out=yt_[:, c0:c0 + 512], in0=pa[:], in1=sg[:], op=ALU.mult),
                         reads=[Ta, T_tmp["ta"][tb % 2]], writes=[T_yT[b]])
                for tb in range(NB):
                    pc, Tc_ = B[4 + tb % 2], Tb[4 + tb % 2]
                    for j in range(CONV_W):
                        P.op("tensor", lambda e, pc=pc, j=j, tb=tb: e.matmul(
                            pc[:], lhsT=dg[b][:, j, :], rhs=yt_[:, tb * 512 + j: tb * 512 + j + 512],
                            start=(j == 0), stop=(j == CONV_W - 1)),
                            reads=[T_dg[b], T_yT[b]], writes=[Tc_])
                    blk = slice(tb * 512, (tb + 1) * 512)
                    sq_ = tmp["tb"][tb % 2]
                    P.op("scalar", lambda e, pc=pc, blk=blk: e.activation(out=cvT[:, c, blk], in_=pc[:], func=ACT.Identity, bias=dwb(c)),
                         reads=[Tc_, T_c], writes=[T_cv[c]])
                    P.op("scalar", lambda e, pc=pc, sq_=sq_: e.activation(out=sq_[:], in_=pc[:], func=ACT.Square, bias=dwb(c)),
                         reads=[Tc_, T_c], writes=[T_tmp["tb"][tb % 2]])
                    P.op("vector", lambda e, blk=blk: e.tensor_tensor(out=acc_s[:, blk], in0=acc_s[:, blk], in1=cvT[:, c, blk], op=ALU.add),
                         reads=[T_cv[c], T_acc], writes=[T_acc])
                    P.op("vector", lambda e, blk=blk, sq_=sq_: e.tensor_tensor(out=acc_q[:, blk], in0=acc_q[:, blk], in1=sq_[:], op=ALU.add),
                         reads=[T_tmp["tb"][tb % 2], T_acc], writes=[T_acc])

            prefetch1(0)
            for c in range(16):
                if c + 1 < 16:
                    prefetch1(c + 1)
                stage2(c)

            def prefetch_z(c):
                b = c % 2
                P.op("sync", lambda e: e.dma_start(out=wz[b], in_=wi_b[32 + c].rearrange("p (f n) -> p f n", f=16)),
                     reads=[T_wib[32 + c]], writes=[T_wz[b]], dma="wz%d" % b)
            prefetch_z(0)
            for tb in range(NB):
                blk = slice(tb * 512, (tb + 1) * 512)
                P.op("tensor", lambda e, blk=blk: e.matmul(B[0][:], lhsT=onesf[:], rhs=acc_s[:, blk], start=True, stop=True),
                     reads=[T_c, T_acc], writes=[Tb[0]])
                P.op("tensor", lambda e, blk=blk: e.matmul(B[1][:], lhsT=onesf[:], rhs=acc_q[:, blk], start=True, stop=True),
                     reads=[T_c, T_acc], writes=[Tb[1]])
                mean, m2 = tmp["tc"][0], tmp["td"][0]
                P.op("vector", lambda e: e.tensor_scalar_mul(out=mean[:], in0=B[0][:], scalar1=1.0 / D),
                     reads=[Tb[0]], writes=[T_tmp["tc"][0]])
                P.op("vector", lambda e: e.tensor_tensor(out=m2[:], in0=mean[:], in1=mean[:], op=ALU.mult),
                     reads=[T_tmp["tc"][0]], writes=[T_tmp["td"][0]])
                P.op("vector", lambda e: e.scalar_tensor_tensor(out=m2[:], in0=B[1][:], scalar=1.0 / D, in1=m2[:],
                                                                op0=ALU.mult, op1=ALU.subtract),
                     reads=[Tb[1], T_tmp["td"][0]], writes=[T_tmp["td"][0]])
                P.op("scalar", lambda e, blk=blk: e.activation(out=rstd_bc[:, blk], in_=m2[:], func=ACT.Ln, bias=EPS),
                     reads=[T_tmp["td"][0]], writes=[T_st])
                P.op("scalar", lambda e, blk=blk: e.activation(out=rstd_bc[:, blk], in_=rstd_bc[:, blk], func=ACT.Exp, scale=-0.5),
                     reads=[T_st], writes=[T_st])
                P.op("vector", lambda e, blk=blk: e.scalar_tensor_tensor(out=nmr_bc[:, blk], in0=mean[:], scalar=-1.0, in1=rstd_bc[:, blk],
                                                                         op0=ALU.mult, op1=ALU.mult),
                     reads=[T_tmp["tc"][0], T_st], writes=[T_st])

            def stage4(c):
                b = c % 2
                for tb in range(NB):
                    k = tb % 2
                    pz, Tz = B[2 + k], Tb[2 + k]
                    c0 = HALO + tb * 512
                    blk = slice(tb * 512, (tb + 1) * 512)
                    for ft in range(16):
                        P.op("tensor", lambda e, pz=pz, ft=ft, c0=c0: e.matmul(
                            pz[:], lhsT=wz[b][:, ft, :], rhs=hT[:, ft, c0:c0 + 512], start=(ft == 0), stop=(ft == 15)),
                            reads=[T_wz[b], T_hT], writes=[Tz])
                    sz, gz, t1, s2, l_, u_ = (tmp[n][k] for n in ("ta", "tb", "tc", "td", "te", "tf"))
                    Ts = {n: T_tmp[n][k] for n in ("ta", "tb", "tc", "td", "te", "tf")}
                    P.op("scalar", lambda e, pz=pz, sz=sz: e.activation(out=sz[:], in_=pz[:], func=ACT.Sigmoid),
                         reads=[Tz], writes=[Ts["ta"]])
                    P.op("vector", lambda e, pz=pz, sz=sz, gz=gz: e.tensor_tensor(out=gz[:], in0=pz[:], in1=sz[:], op=ALU.mult),
                         reads=[Tz, Ts["ta"]], writes=[Ts["tb"]])
                    P.op("vector", lambda e, t1=t1, blk=blk: e.tensor_tensor(out=t1[:], in0=cvT[:, c, blk], in1=rstd_bc[:, blk], op=ALU.mult),
                         reads=[T_cv[c], T_st], writes=[Ts["tc"]])
                    P.op("vector", lambda e, t1=t1, blk=blk: e.tensor_tensor(out=t1[:], in0=t1[:], in1=nmr_bc[:, blk], op=ALU.add),
                         reads=[Ts["tc"], T_st], writes=[Ts["tc"]])
                    P.op("scalar", lambda e, t1=t1, s2=s2: e.activation(out=s2[:], in_=t1[:], func=ACT.Sigmoid, scale=lng(c), bias=lnb(c)),
                         reads=[Ts["tc"], T_c], writes=[Ts["td"]])
                    P.op("vector", lambda e, t1=t1, l_=l_: e.tensor_scalar(out=l_[:], in0=t1[:], scalar1=lng(c), scalar2=lnb(c),
                                                                          op0=ALU.mult, op1=ALU.add),
                         reads=[Ts["tc"], T_c], writes=[Ts["te"]])
                    P.op("vector", lambda e, l_=l_, s2=s2, u_=u_: e.tensor_tensor(out=u_[:], in0=l_[:], in1=s2[:], op=ALU.mult),
                         reads=[Ts["te"], Ts["td"]], writes=[Ts["tf"]])
                    P.op("vector", lambda e, u_=u_, gz=gz, blk=blk: e.tensor_tensor(out=cvT[:, c, blk], in0=u_[:], in1=gz[:], op=ALU.mult),
                         reads=[Ts["tf"], Ts["tb"]], writes=[T_cv[c]])

            for c in range(16):
                if c + 1 < 16:
                    prefetch_z(c + 1)
                stage4(c)

            alias = [T_hT] + T_wa + T_wb + T_wz + T_dg
            for ct in range(16):
                P.op("sync", lambda e, ct=ct: e.dma_start(out=wob[:, ct, :], in_=wo_b[ct]), reads=[T_wobd[ct]], writes=[T_wob] + alias,
                     dma="wo%d" % (ct % 4))

            def stage5(t):
                b = cnt["xt"] % 2
                cnt["xt"] += 1
                row = h * NP + t * 128
                P.op("sync", lambda e: e.dma_start(out=xt[b][:], in_=x[HALO + row:HALO + row + 128, :]), writes=[T_xt[b]], dma="xt%d" % b)
                for fb in range(4):
                    for ct in range(16):
                        P.op("tensor", lambda e, fb=fb, ct=ct: e.matmul(
                            B[fb][:], lhsT=cvT[:, ct, t * 128:(t + 1) * 128], rhs=wob[:, ct, fb * 512:(fb + 1) * 512],
                            start=(ct == 0), stop=(ct == 15)),
                            reads=[T_cv[ct], T_wob], writes=[Tb[fb]])
                for fb in range(4):
                    P.op("vector", lambda e, fb=fb: e.tensor_tensor(out=xt[b][:, fb * 512:(fb + 1) * 512], in0=B[fb][:],
                                                                    in1=xt[b][:, fb * 512:(fb + 1) * 512], op=ALU.add),
                         reads=[Tb[fb], T_xt[b]], writes=[T_xt[b]])
                P.op("gpsimd", lambda e: e.dma_start(out=x1_o[row:row + 128, :], in_=xt[b][:]), reads=[T_xt[b]], writes=[T_x1o], dma="x1o%d" % b)
                P.op("scalar", lambda e: e.activation(out=sqx[:], in_=xt[b][:], func=ACT.Square), reads=[T_xt[b]], writes=[T_sqx])
                P.op("vector", lambda e: e.reduce_sum(out=small[:, 0:1], in_=sqx[:], axis=AX.X), reads=[T_sqx], writes=[T_sm])
                rstd_from_ss(128, 0, 1, float(D))
                P.op("scalar", lambda e: e.activation(out=sqx[:], in_=xt[b][:], func=ACT.Copy, scale=small[:, 1:2]),
                     reads=[T_xt[b], T_sm], writes=[T_sqx])
                for ft in range(16):
                    P.op("tensor", lambda e, ft=ft: e.transpose(
                        B[4 + ft // 4][:, (ft % 4) * 128:(ft % 4 + 1) * 128], sqx[:, ft * 128:(ft + 1) * 128], identf[:]),
                        reads=[T_sqx, T_c], writes=[Tb[4 + ft // 4]])
                half = t % 2
                for q in range(4):
                    P.op("scalar", lambda e, q=q: e.copy(out=xnblk[:, q * 4:(q + 1) * 4, half * 128:(half + 1) * 128],
                                                         in_=B[4 + q][:].rearrange("p (f t) -> p f t", f=4)),
                         reads=[Tb[4 + q]], writes=[T_xnblk])
                if half == 1:
                    blk_i = (h * NP + t * 128) // 256
                    P.op("gpsimd", lambda e: e.dma_start(out=xnT_o[blk_i], in_=xnblk[:]), reads=[T_xnblk], writes=[T_xno], dma="xno")

            for t in range(NP // 128):
                stage5(t)

        for h in range(npass):
            do_pass(h)
        P.emit()
    return nc


def l1_inputs(x, a_norm_g, a_w_in, a_dw_w, a_dw_b, a_ln_g, a_ln_b, a_w_out, ntok=2048, ncores=NCORES):
    w_in_l = np.ascontiguousarray(a_w_in.reshape(16, 128, 48, 128).transpose(2, 1, 0, 3))
    w_out_l = np.ascontiguousarray(a_w_out.reshape(16, 128, D).transpose(1, 0, 2))
    gbc = np.ascontiguousarray(np.broadcast_to(a_norm_g.reshape(16, 128).T[:, :, None], (128, 16, 128)).reshape(128, D)).astype(np.float32)
    dww = np.ascontiguousarray(a_dw_w.reshape(CONV_W, 16, 128).transpose(2, 1, 0)).astype(np.float32)
    cvec = np.ascontiguousarray(np.concatenate(
        [a_dw_b.reshape(16, 128).T, a_ln_g.reshape(16, 128).T, a_ln_b.reshape(16, 128).T], axis=1)).astype(np.float32)
    identf = np.eye(128, dtype=np.float32)
    identb = identf.astype(ml_dtypes.bfloat16)
    xp = np.concatenate([np.zeros((HALO, D), np.float32), x], axis=0)
    maps = []
    for c in range(ncores):
        maps.append({"x": np.ascontiguousarray(xp[c * ntok: c * ntok + ntok + HALO]), "w_in": w_in_l, "w_out": w_out_l,
                     "gbc": gbc, "dww": dww, "cvec": cvec, "identb": identb, "identf": identf})
    return maps


_CACHE = {}


def _get(name, fn):
    if name not in _CACHE:
        _CACHE[name] = fn()
    return _CACHE[name]


def kernel(x, a_norm_g, a_w_in, a_dw_w, a_dw_b, a_ln_g, a_ln_b, a_w_out,
           kv_norm_g, w_kv, k_norm_g, b_norm_g, b_w_in, b_q_norm_g,
           b_lambda, b_subln_g, b_w_out):
    f = lambda a: np.asarray(a, dtype=np.float32)
    x2 = f(x).reshape(S, D)
    cores = list(range(NCORES))
    ntok = S // NCORES
    nc1 = _get("l1", lambda: build_l1(ntok, 1024))
    m1 = l1_inputs(x2, f(a_norm_g)[0], f(a_w_in)[0], f(a_dw_w)[0], f(a_dw_b)[0], f(a_ln_g)[0], f(a_ln_b)[0], f(a_w_out)[0],
                   ntok=ntok, ncores=NCORES)
    r1 = run_bass_kernel_spmd(nc1, m1, core_ids=cores).results
    x1 = [r1[c]["x1"] for c in cores]
    xnT_blocks = np.concatenate([r1[c]["xnT"] for c in cores], axis=0)
    nc2 = _get("l2", lambda: build_l2(S))
    xnT_full = np.ascontiguousarray(xnT_blocks.transpose(2, 1, 0, 3)).reshape(16, 128, S)
    m2 = l2_inputs(xnT_full, f(w_kv), f(b_w_in)[0], f(kv_norm_g), f(b_norm_g)[0], f(k_norm_g), f(b_q_norm_g)[0],
                   f(b_lambda)[0], f(b_subln_g)[0], s_len=S)
    r2 = run_bass_kernel_spmd(nc2, m2, core_ids=cores).results
    o_full = np.concatenate([r2[c]["o"] for c in cores], axis=1)
    nc3 = _get("l3", lambda: build_l3(ntok))
    wo = np.ascontiguousarray(f(b_w_out)[0].reshape(16, 128, D).transpose(1, 0, 2))
    m3 = []
    for c in cores:
        oc = o_full[c * ntok:(c + 1) * ntok]
        oT = np.ascontiguousarray(oc.reshape(ntok // 128, 128, 16, 128).transpose(0, 3, 2, 1))
        m3.append({"oT": oT, "x1": x1[c], "wo": wo})
    r3 = run_bass_kernel_spmd(nc3, m3, core_ids=cores).results
    out = np.concatenate([r3[c]["y"] for c in cores], axis=0).reshape(1, S, D).astype(np.float32)
    return out
```

```python
import contextlib
import math
import numpy as np
import ml_dtypes
import concourse.bass as bass
import concourse.mybir as mybir
from concourse.bass_utils import run_bass_kernel_spmd

F32 = mybir.dt.float32
BF16 = mybir.dt.bfloat16
ALU = mybir.AluOpType
ACT = mybir.ActivationFunctionType
AX = mybir.AxisListType

NCORES = 8
D = 2048
S = 16384
EPS = 1e-6
CONV_W = 31
HALO = CONV_W - 1
DH = 128
LAM_INIT = 0.8 - 0.6 * math.exp(-0.3 * (2 - 1))

ENGS = ("tensor", "vector", "scalar", "gpsimd", "sync")


class T:
    __slots__ = ("name", "last_w", "readers")

    def __init__(self, name=""):
        self.name = name
        self.last_w = None
        self.readers = []


class Prog:
    def __init__(self, nc):
        self.nc = nc
        self.ops = []
        self.groups = {}

    def op(self, eng, fn, reads=(), writes=(), dma=None, inc=16):
        i = len(self.ops)
        deps = set()
        for t in reads:
            if t.last_w is not None:
                deps.add(t.last_w)
        for t in writes:
            if t.last_w is not None:
                deps.add(t.last_w)
            deps.update(t.readers)
        deps.discard(i)
        for t in reads:
            t.readers.append(i)
        for t in writes:
            t.last_w = i
            t.readers = []
        gneed = {}
        for jd in deps:
            g = self.ops[jd]["dma"]
            if g is not None:
                gneed[g] = self.groups[g]
        o = dict(i=i, eng=eng, fn=fn, deps=deps, dma=dma, sig=False, sidx=0, inc=inc, gneed=gneed)
        if dma is not None:
            self.groups[dma] = self.groups.get(dma, 0) + inc
            o["sidx"] = self.groups[dma]
        self.ops.append(o)
        return i

    def emit(self, final_wait_engine="sync"):
        nc = self.nc
        ops = self.ops
        for o in ops:
            for j in o["deps"]:
                d = ops[j]
                if d["dma"] is not None:
                    continue
                if d["eng"] == "tensor" and o["eng"] == "tensor" and o["dma"] is None:
                    continue
                d["sig"] = True
        cnt = {e: 0 for e in ENGS}
        for o in ops:
            if o["dma"] is None and o["sig"]:
                cnt[o["eng"]] += 1
                o["sidx"] = cnt[o["eng"]]
        with contextlib.ExitStack() as es:
            esem = {e: es.enter_context(nc.semaphore("p_" + e)) for e in ENGS}
            gsem = {g: es.enter_context(nc.semaphore("g_%d" % k))
                    for k, g in enumerate(self.groups)}
            block = es.enter_context(nc.Block())
            final = dict(self.groups)

            def make(ename):
                def body(eng):
                    waited_e = {e: 0 for e in ENGS}
                    waited_g = {g: 0 for g in self.groups}
                    for o in ops:
                        if o["eng"] != ename:
                            continue
                        need_e = {}
                        need_g = o["gneed"]
                        for j in o["deps"]:
                            d = ops[j]
                            if d["dma"] is not None:
                                continue
                            else:
                                if d["eng"] == "tensor" and ename == "tensor" and o["dma"] is None:
                                    continue
                                need_e[d["eng"]] = max(need_e.get(d["eng"], 0), d["sidx"])
                        for e, v in need_e.items():
                            if v > waited_e[e]:
                                eng.wait_ge(esem[e], v)
                                waited_e[e] = v
                        for g, v in need_g.items():
                            if v > waited_g[g]:
                                eng.wait_ge(gsem[g], v)
                                waited_g[g] = v
                        ins = o["fn"](eng)
                        if o["dma"] is not None:
                            ins.then_inc(gsem[o["dma"]], o["inc"])
                        elif o["sig"]:
                            ins.then_inc(esem[ename], 1)
                    if ename == final_wait_engine:
                        for g, v in final.items():
                            if v > waited_g[g]:
                                eng.wait_ge(gsem[g], v)
                        for e in ENGS:
                            if e != ename and cnt[e] > waited_e[e]:
                                eng.wait_ge(esem[e], cnt[e])
                return body

            block.tensor(make("tensor"))
            block.vector(make("vector"))
            block.scalar(make("scalar"))
            block.gpsimd(make("gpsimd"))
            block.sync(make("sync"))


def build_l2(s_len=S, dbg=False):
    QB = 256
    nblk = s_len // QB
    ntile = s_len // 128
    nc = bass.Bass("TRN2", target_bir_lowering=False)
    xnT = nc.dram_tensor("xnT", [nblk, 128, 16, QB], BF16, kind="ExternalInput").ap()
    wA = nc.dram_tensor("wA", [128, 16, 512], F32, kind="ExternalInput").ap()
    wB = nc.dram_tensor("wB", [128, 16, 512], F32, kind="ExternalInput").ap()
    gcol = nc.dram_tensor("gcol", [128, 32], F32, kind="ExternalInput").ap()
    gqk = nc.dram_tensor("gqk", [128, 2], F32, kind="ExternalInput").ap()
    gsub = nc.dram_tensor("gsub", [128, 256], F32, kind="ExternalInput").ap()
    lamp = nc.dram_tensor("lamp", [128, 512], F32, kind="ExternalInput").ap()
    o_out = nc.dram_tensor("o", [s_len, 256], BF16, kind="ExternalOutput").ap()

    P = Prog(nc)
    with contextlib.ExitStack() as es:
        def sb(name, shape, dt):
            return es.enter_context(nc.sbuf_tensor(name, shape, dt))

        KT = sb("KT", [128, 2, s_len], BF16)
        V = sb("V", [128, ntile, 257], BF16)
        wAb = sb("wAb", [128, 16, 512], BF16)
        wBb = sb("wBb", [128, 16, 512], BF16)
        xb = [sb("xb%d" % i, [128, 16, QB], BF16) for i in range(2)]
        QTb = [sb("QTb%d" % i, [128, 2, QB], BF16) for i in range(2)]
        PT = [sb("PT%d" % i, [128, 4, QB], BF16) for i in range(3)]
        sq = sb("sq", [128, 1024], BF16)
        rst = sb("rst", [128, 1024], F32)
        th = [sb("th%d" % i, [128, 256], F32) for i in range(2)]
        Zg = [sb("Zg%d" % i, [128, 256], BF16) for i in range(4)]
        ea = [sb("ea%d" % i, [128, 256], F32) for i in range(2)]
        eo = [sb("eo%d" % i, [128, 256], F32) for i in range(2)]
        esq = ea
        eof = ea
        ob = [sb("ob%d" % i, [128, 256], BF16) for i in range(4)]
        small = [sb("small%d" % i, [128, 8], F32) for i in range(2)]
        gcol_s = sb("gcol_s", [128, 32], F32)
        gqk_s = sb("gqk_s", [128, 2], F32)
        gsub_s = sb("gsub_s", [128, 256], F32)
        lam_v = sb("lam_v", [128, 4], F32)
        ones = sb("ones", [128, 128], BF16)
        wst = [sb("wst%d" % i, [128, 1, 512], F32) for i in range(2)]
        PS = [es.enter_context(nc.psum_tensor("ps%d" % i, [128, 1024], F32)) for i in range(4)]

        Tb = [T("bank%d" % i) for i in range(8)]
        T_KT = [T() for _ in range(nblk)]
        T_V = [T() for _ in range(ntile)]
        T_wA, T_wB, T_g, T_lam, T_ones = T(), T(), T(), T(), T()
        T_wst = [T(), T()]
        T_xb = [T(), T()]
        T_QT = [T(), T()]
        T_PT = [T(), T(), T()]
        T_sq, T_rst = [T(), T()], [T(), T()]
        T_th = [T(), T()]
        T_Zg = [T() for _ in range(4)]
        T_e = [T(), T()]
        T_ob = [T() for _ in range(4)]
        T_out = T()

        P.op("sync", lambda e: e.dma_start(out=gcol_s[:], in_=gcol), writes=[T_g], dma="c0")
        P.op("sync", lambda e: e.dma_start(out=gqk_s[:], in_=gqk), writes=[T_g], dma="c0")
        P.op("sync", lambda e: e.dma_start(out=gsub_s[:], in_=gsub), writes=[T_g], dma="c0")
        P.op("sync", lambda e: e.dma_start(out=ea[0][:], in_=lamp[:, 0:256]), writes=[T_lam], dma="c0")
        P.op("sync", lambda e: e.dma_start(out=ea[1][:], in_=lamp[:, 256:512]), writes=[T_lam], dma="c0")
        P.op("gpsimd", lambda e: e.memset(ones[:], 1.0), writes=[T_ones])
        P.op("gpsimd", lambda e: e.memset(V[:, :, 256:257], 1.0), writes=T_V)
        P.op("vector", lambda e: e.tensor_scalar_mul(out=gqk_s[:, 0:1], in0=gqk_s[:, 0:1], scalar1=math.sqrt(128.0)),
             reads=[T_g], writes=[T_g])
        P.op("vector", lambda e: e.tensor_scalar_mul(out=gsub_s[:], in0=gsub_s[:], scalar1=(1.0 - LAM_INIT) * 16.0),
             reads=[T_g], writes=[T_g])
        P.op("vector", lambda e: e.tensor_tensor(out=eo[0][:, 0:128], in0=ea[0][:, 0:128], in1=ea[0][:, 128:256], op=ALU.mult),
             reads=[T_lam], writes=[T_lam])
        P.op("vector", lambda e: e.tensor_tensor(out=eo[0][:, 128:256], in0=ea[1][:, 0:128], in1=ea[1][:, 128:256], op=ALU.mult),
             reads=[T_lam], writes=[T_lam])
        P.op("vector", lambda e: e.reduce_sum(out=lam_v[:, 0:2], in_=eo[0][:].rearrange("p (a b) -> p a b", a=2), axis=AX.X),
             reads=[T_lam], writes=[T_lam, T_e[0], T_e[1]])
        P.op("scalar", lambda e: e.activation(out=lam_v[:, 2:4], in_=lam_v[:, 0:2], func=ACT.Exp),
             reads=[T_lam], writes=[T_lam])
        P.op("vector", lambda e: e.tensor_tensor(out=lam_v[:, 0:1], in0=lam_v[:, 3:4], in1=lam_v[:, 2:3], op=ALU.subtract),
             reads=[T_lam], writes=[T_lam])
        P.op("vector", lambda e: e.tensor_scalar_add(out=lam_v[:, 1:2], in0=lam_v[:, 0:1], scalar1=-LAM_INIT),
             reads=[T_lam], writes=[T_lam])
        nlam = lam_v[:, 1:2]
        k = 0
        for (wsrc, wdst, Tw) in ((wA, wAb, T_wA), (wB, wBb, T_wB)):
            for ch in range(16):
                b = k % 2
                P.op("sync", lambda e, b=b, ch=ch, wsrc=wsrc: e.dma_start(out=wst[b][:], in_=wsrc[:, ch:ch + 1, :]),
                     writes=[T_wst[b]], dma="wst%d" % b)
                for f in range(1):
                    ft = ch + f
                    for half in range(2):
                        if wsrc is wA:
                            gi = ft if half == 0 else 16 + ft
                        else:
                            gi = ft if half == 0 else 16 + ft
                        P.op("vector", lambda e, b=b, f=f, ft=ft, half=half, gi=gi, wdst=wdst: e.tensor_scalar_mul(
                            out=wdst[:, ft, half * 256:(half + 1) * 256], in0=wst[b][:, f, half * 256:(half + 1) * 256],
                            scalar1=gcol_s[:, gi:gi + 1]),
                            reads=[T_wst[b], T_g], writes=[Tw])
                k += 1

        def load_xb(j):
            b = j % 2
            P.op("sync", lambda e: e.dma_start(out=xb[b][:], in_=xnT[j]), writes=[T_xb[b]], dma="xb%d" % b)

        load_xb(0)
        def do_block(j):
            jb = j % 2
            if j + 1 < nblk:
                load_xb(j + 1)
            def proj_kq(i):
                for ft in range(16):
                    P.op("tensor", lambda e, ft=ft: e.matmul(
                        PS[0][:, i * 256:(i + 1) * 256], lhsT=wAb[:, ft, i * 128:(i + 1) * 128], rhs=xb[jb][:, ft, :],
                        start=(ft == 0), stop=(ft == 15)),
                        reads=[T_wA, T_xb[jb]], writes=[Tb[i // 2]])

            def proj_vz(ti):
                for ft in range(16):
                    P.op("tensor", lambda e, ft=ft: e.matmul(
                        PS[1][:, ti * 512:(ti + 1) * 512], lhsT=xb[jb][:, ft, ti * 128:(ti + 1) * 128], rhs=wBb[:, ft, :],
                        start=(ft == 0), stop=(ft == 15)),
                        reads=[T_wB, T_xb[jb]], writes=[Tb[2 + ti]])

            def square(hf):
                P.op("scalar", lambda e: e.activation(out=sq[:, hf * 512:(hf + 1) * 512], in_=PS[0][:, hf * 512:(hf + 1) * 512],
                                                      func=ACT.Square),
                     reads=[Tb[hf]], writes=[T_sq[hf]])

            def ssq(hf):
                for i in (2 * hf, 2 * hf + 1):
                    P.op("tensor", lambda e, i=i: e.matmul(PS[2][:, i * 256:(i + 1) * 256], lhsT=ones[:], rhs=sq[:, i * 256:(i + 1) * 256],
                                                          start=True, stop=True),
                         reads=[T_ones, T_sq[hf]], writes=[Tb[4 + hf]])

            def rstd(hf):
                P.op("scalar", lambda e: e.activation(out=rst[:, hf * 512:(hf + 1) * 512], in_=PS[2][:, hf * 512:(hf + 1) * 512],
                                                      func=ACT.Ln, bias=128.0 * EPS),
                     reads=[Tb[4 + hf]], writes=[T_rst[hf]])
                P.op("scalar", lambda e: e.activation(out=rst[:, hf * 512:(hf + 1) * 512], in_=rst[:, hf * 512:(hf + 1) * 512],
                                                      func=ACT.Exp, scale=-0.5),
                     reads=[T_rst[hf]], writes=[T_rst[hf]])

            def vz_tile(ti):
                tt = 2 * j + ti
                zi = jb * 2 + ti
                P.op("scalar", lambda e, ti=ti, tt=tt: e.copy(out=V[:, tt, 0:256], in_=PS[1][:, ti * 512:ti * 512 + 256]),
                     reads=[Tb[2 + ti]], writes=[T_V[tt]])
                P.op("scalar", lambda e, ti=ti: e.activation(out=th[ti][:], in_=PS[1][:, ti * 512 + 256:(ti + 1) * 512],
                                                             func=ACT.Exp, scale=-1.0),
                     reads=[Tb[2 + ti]], writes=[T_th[ti]])
                P.op("gpsimd", lambda e, ti=ti: e.tensor_scalar_add(out=th[ti][:], in0=th[ti][:], scalar1=1.0),
                     reads=[T_th[ti]], writes=[T_th[ti]])
                P.op("vector", lambda e, ti=ti: e.reciprocal(out=th[ti][:], in_=th[ti][:]),
                     reads=[T_th[ti]], writes=[T_th[ti]])
                P.op("vector", lambda e, ti=ti, zi=zi: e.tensor_tensor(
                    out=Zg[zi][:], in0=th[ti][:], in1=PS[1][:, ti * 512 + 256:(ti + 1) * 512], op=ALU.mult),
                    reads=[T_th[ti], Tb[2 + ti]], writes=[T_Zg[zi]])

            proj_kq(2); proj_kq(3)
            square(1)
            proj_kq(0); proj_kq(1)
            ssq(1)
            square(0)
            rstd(1)
            proj_vz(0)
            P.op("vector", lambda e: e.scalar_tensor_tensor(
                out=QTb[jb][:], in0=PS[0][:, 512:1024].rearrange("p (m t) -> p m t", m=2),
                scalar=gqk_s[:, 1:2], in1=rst[:, 512:1024].rearrange("p (m t) -> p m t", m=2), op0=ALU.mult, op1=ALU.mult),
                reads=[Tb[1], T_rst[1], T_g], writes=[T_QT[jb]])
            ssq(0)
            rstd(0)
            proj_vz(1)
            P.op("vector", lambda e: e.scalar_tensor_tensor(
                out=KT[:, :, j * QB:(j + 1) * QB], in0=PS[0][:, 0:512].rearrange("p (m t) -> p m t", m=2),
                scalar=gqk_s[:, 0:1], in1=rst[:, 0:512].rearrange("p (m t) -> p m t", m=2), op0=ALU.mult, op1=ALU.mult),
                reads=[Tb[0], T_rst[0], T_g], writes=[T_KT[j]])
            for ti in range(2):
                vz_tile(ti)

            npair = j + 1
            def do_qk(p):
                sp = p % 2
                for kl in range(2):
                    kt = 2 * p + kl
                    for m in range(2):
                        P.op("tensor", lambda e, kt=kt, kl=kl, m=m, sp=sp: e.matmul(
                            PS[sp][:, (kl * 2 + m) * 256:(kl * 2 + m + 1) * 256],
                            lhsT=KT[:, m, kt * 128:(kt + 1) * 128], rhs=QTb[jb][:, m, :], start=True, stop=True),
                            reads=[T_KT[kt // 2], T_QT[jb]], writes=[Tb[2 * sp + kl]])

            def do_pair(p):
                sp = p % 2
                pb = p % 3
                last = (p == npair - 1)
                P.op("scalar", lambda e, sp=sp, pb=pb: e.activation(
                    out=PT[pb][:].rearrange("p a q -> p (a q)"), in_=PS[sp][:], func=ACT.Exp),
                    reads=[Tb[2 * sp], Tb[2 * sp + 1]], writes=[T_PT[pb]])
                if last:
                    P.op("vector", lambda e, pb=pb: e.memset(PT[pb][64:128, 0:2, 0:64], 0.0), writes=[T_PT[pb]])
                    P.op("vector", lambda e, pb=pb: e.memset(PT[pb][64:128, 2:4, 128:192], 0.0), writes=[T_PT[pb]])
                for kl in range(2):
                    kt = 2 * p + kl
                    for qt in range(2):
                        if last and kl == 1 and qt == 0:
                            continue
                        for m in range(2):
                            P.op("tensor", lambda e, kt=kt, kl=kl, m=m, qt=qt, pb=pb: e.matmul(
                                PS[2 + qt][:, m * 512:m * 512 + 257],
                                lhsT=PT[pb][:, kl * 2 + m, qt * 128:(qt + 1) * 128], rhs=V[:, kt, :],
                                start=(kt == 0), stop=(kt == 2 * j + qt)),
                                reads=[T_PT[pb], T_V[kt]], writes=[Tb[4 + 2 * qt + m]])

            do_qk(0)
            for p in range(npair):
                if p + 1 < npair:
                    do_qk(p + 1)
                do_pair(p)

            def epilogue(qt):
                tt = 2 * j + qt
                zi = jb * 2 + qt
                O1 = PS[2 + qt][:, 0:257]
                O2 = PS[2 + qt][:, 512:769]
                b1, b2 = Tb[4 + 2 * qt], Tb[4 + 2 * qt + 1]
                sm = small[qt]
                P.op("vector", lambda e, O1=O1, sm=sm: e.reciprocal(out=sm[:, 0:1], in_=O1[:, 256:257]),
                     reads=[b1], writes=[T_e[qt]])
                P.op("vector", lambda e, O2=O2, sm=sm: e.reciprocal(out=sm[:, 1:2], in_=O2[:, 256:257]),
                     reads=[b2], writes=[T_e[qt]])
                P.op("vector", lambda e, sm=sm: e.tensor_tensor(out=sm[:, 2:3], in0=sm[:, 1:2], in1=nlam, op=ALU.mult),
                     reads=[T_e[qt], T_lam], writes=[T_e[qt]])
                P.op("vector", lambda e, O1=O1, sm=sm, qt=qt: e.tensor_scalar_mul(out=ea[qt][:], in0=O1[:, 0:256], scalar1=sm[:, 0:1]),
                     reads=[b1, T_e[qt]], writes=[T_e[qt]])
                P.op("vector", lambda e, O2=O2, sm=sm, qt=qt: e.scalar_tensor_tensor(
                    out=eo[qt][:], in0=O2[:, 0:256], scalar=sm[:, 2:3], in1=ea[qt][:], op0=ALU.mult, op1=ALU.add),
                    reads=[b2, T_e[qt]], writes=[T_e[qt]])
                P.op("scalar", lambda e, qt=qt: e.activation(out=esq[qt][:], in_=eo[qt][:], func=ACT.Square),
                     reads=[T_e[qt]], writes=[T_e[qt]])
                P.op("vector", lambda e, sm=sm, qt=qt: e.reduce_sum(out=sm[:, 3:4], in_=esq[qt][:], axis=AX.X),
                     reads=[T_e[qt]], writes=[T_e[qt]])
                P.op("scalar", lambda e, sm=sm: e.activation(out=sm[:, 4:5], in_=sm[:, 3:4], func=ACT.Ln, bias=256.0 * EPS),
                     reads=[T_e[qt]], writes=[T_e[qt]])
                P.op("scalar", lambda e, sm=sm: e.activation(out=sm[:, 4:5], in_=sm[:, 4:5], func=ACT.Exp, scale=-0.5),
                     reads=[T_e[qt]], writes=[T_e[qt]])
                P.op("vector", lambda e, sm=sm, qt=qt: e.scalar_tensor_tensor(
                    out=eof[qt][:], in0=eo[qt][:], scalar=sm[:, 4:5], in1=gsub_s[:], op0=ALU.mult, op1=ALU.mult),
                    reads=[T_e[qt], T_g], writes=[T_e[qt]])
                P.op("gpsimd", lambda e, qt=qt, zi=zi: e.tensor_tensor(out=ob[zi][:], in0=eof[qt][:], in1=Zg[zi][:], op=ALU.mult),
                     reads=[T_e[qt], T_Zg[zi]], writes=[T_ob[zi]])
                P.op("gpsimd", lambda e, tt=tt, zi=zi: e.dma_start(out=o_out[tt * 128:(tt + 1) * 128, :], in_=ob[zi][:]),
                     reads=[T_ob[zi]], writes=[T_out], dma="ob%d" % zi)
            for qt in range(2):
                epilogue(qt)

        for j in range(nblk):
            do_block(j)

        if dbg:
            dk = nc.dram_tensor("d_KT", [128, 2, s_len], BF16, kind="ExternalOutput").ap()
            dv = nc.dram_tensor("d_V", [128, ntile, 257], BF16, kind="ExternalOutput").ap()
            dq = nc.dram_tensor("d_QT", [128, 2, QB], BF16, kind="ExternalOutput").ap()
            dr = nc.dram_tensor("d_rst", [128, 1024], F32, kind="ExternalOutput").ap()
            dz = nc.dram_tensor("d_Zg", [128, 256], BF16, kind="ExternalOutput").ap()
            dp = nc.dram_tensor("d_PT", [128, 4, QB], BF16, kind="ExternalOutput").ap()
            de = nc.dram_tensor("d_eo", [128, 256], F32, kind="ExternalOutput").ap()
            dea = nc.dram_tensor("d_ea", [128, 256], F32, kind="ExternalOutput").ap()
            dsm = nc.dram_tensor("d_sm", [128, 8], F32, kind="ExternalOutput").ap()
            dl = nc.dram_tensor("d_lam", [128, 4], F32, kind="ExternalOutput").ap()
            dw = nc.dram_tensor("d_wA", [128, 16, 512], BF16, kind="ExternalOutput").ap()
            allT = T_KT + T_V + T_QT + T_rst + T_Zg + T_PT + T_e + [T_lam, T_wA]
            for (dst, src) in ((dk, KT), (dv, V), (dq, QTb[(nblk - 1) % 2]), (dr, rst), (dz, Zg[((nblk - 1) % 2) * 2]),
                               (dp, PT[(nblk - 1) % 3]), (de, eo[0]), (dea, ea[0]), (dsm, small[0]), (dl, lam_v), (dw, wAb)):
                P.op("sync", lambda e, dst=dst, src=src: e.dma_start(out=dst, in_=src[:]), reads=allT, writes=[T_out], dma="dbg")
        P.emit()
    return nc


def l2_inputs(xnT_full, w_kv, b_w_in, kv_norm_g, b_norm_g, k_norm_g, q_norm_g, lam, subln_g, s_len=S):
    QB = 256
    nblk = s_len // QB
    xb = np.ascontiguousarray(
        xnT_full.reshape(16, 128, nblk, QB).transpose(2, 1, 0, 3))

    def wl(w):
        return np.ascontiguousarray(w.reshape(16, 128, w.shape[1]).transpose(1, 0, 2))

    gcol = np.ascontiguousarray(np.concatenate(
        [kv_norm_g.reshape(16, 128).T, b_norm_g.reshape(16, 128).T], axis=1)).astype(np.float32)
    gqk = np.ascontiguousarray(np.stack([k_norm_g, q_norm_g], axis=1)).astype(np.float32)
    gsub = np.ascontiguousarray(np.broadcast_to(subln_g.reshape(1, 256), (128, 256))).astype(np.float32)
    lamp = np.ascontiguousarray(np.broadcast_to(lam.reshape(1, 512), (128, 512))).astype(np.float32)
    maps = []
    for c in range(NCORES):
        kc = w_kv[:, c * 256:(c + 1) * 256]
        vc = w_kv[:, 2048 + c * 256:2048 + (c + 1) * 256]
        qc = b_w_in[:, c * 256:(c + 1) * 256]
        zc = b_w_in[:, 2048 + c * 256:2048 + (c + 1) * 256]
        maps.append({
            "xnT": xb,
            "wA": wl(np.concatenate([kc, qc], axis=1)),
            "wB": wl(np.concatenate([vc, zc], axis=1)),
            "gcol": gcol, "gqk": gqk, "gsub": gsub, "lamp": lamp,
        })
    return maps


def build_l3(ntok=2048):
    ntt = ntok // 128
    nc = bass.Bass("TRN2", target_bir_lowering=False)
    oT = nc.dram_tensor("oT", [ntt, 128, 16, 128], BF16, kind="ExternalInput").ap()
    x1 = nc.dram_tensor("x1", [ntok, D], F32, kind="ExternalInput").ap()
    wo = nc.dram_tensor("wo", [128, 16, D], F32, kind="ExternalInput").ap()
    y = nc.dram_tensor("y", [ntok, D], F32, kind="ExternalOutput").ap()
    P = Prog(nc)
    with contextlib.ExitStack() as es:
        def sb(name, shape, dt):
            return es.enter_context(nc.sbuf_tensor(name, shape, dt))
        wob = sb("wob", [128, 16, D], BF16)
        ot = [sb("ot%d" % i, [128, 16, 128], BF16) for i in range(2)]
        xt = [sb("xt%d" % i, [128, D], F32) for i in range(2)]
        yt = [sb("yt%d" % i, [128, D], F32) for i in range(2)]
        PS = [es.enter_context(nc.psum_tensor("ps%d" % i, [128, 2048], F32)) for i in range(2)]
        T_w = [T() for _ in range(16)]
        T_ot, T_xt, T_yt, T_ps = [T(), T()], [T(), T()], [T(), T()], [T(), T()]
        T_y = T()
        wstg = [sb("wstg%d" % i, [128, D], F32) for i in range(3)]
        T_wstg = [T(), T(), T()]
        for ct in range(16):
            i = ct % 3
            P.op("sync", lambda e, ct=ct, i=i: e.dma_start(out=wstg[i][:], in_=wo[:, ct, :]), writes=[T_wstg[i]], dma="w%d" % i)
            if ct % 2 == 0:
                P.op("vector", lambda e, ct=ct, i=i: e.tensor_copy(out=wob[:, ct, :], in_=wstg[i][:]), reads=[T_wstg[i]], writes=[T_w[ct]])
            else:
                P.op("scalar", lambda e, ct=ct, i=i: e.copy(out=wob[:, ct, :], in_=wstg[i][:]), reads=[T_wstg[i]], writes=[T_w[ct]])

        def loads(tt):
            b = tt % 2
            P.op("sync", lambda e: e.dma_start(out=ot[b][:], in_=oT[tt]), writes=[T_ot[b]], dma="ot%d" % b)
            P.op("sync", lambda e: e.dma_start(out=xt[b][:], in_=x1[tt * 128:(tt + 1) * 128, :]), writes=[T_xt[b]], dma="xt%d" % b)

        def tile(tt):
            b = tt % 2
            if tt + 1 < ntt:
                loads(tt + 1)
            for fb in range(4):
                for ct in range(16):
                    P.op("tensor", lambda e, fb=fb, ct=ct: e.matmul(
                        PS[b][:, fb * 512:(fb + 1) * 512], lhsT=ot[b][:, ct, :], rhs=wob[:, ct, fb * 512:(fb + 1) * 512],
                        start=(ct == 0), stop=(ct == 15)),
                        reads=[T_ot[b], T_w[ct]], writes=[T_ps[b]])
            P.op("vector", lambda e: e.tensor_tensor(out=yt[b][:], in0=PS[b][:], in1=xt[b][:], op=ALU.add),
                 reads=[T_ps[b], T_xt[b]], writes=[T_yt[b]])
            P.op("gpsimd", lambda e: e.dma_start(out=y[tt * 128:(tt + 1) * 128, :], in_=yt[b][:]),
                 reads=[T_yt[b]], writes=[T_y], dma="yt%d" % b)

        loads(0)
        for tt in range(ntt):
            tile(tt)
        P.emit()
    return nc


def build_l1(ntok=2048, NP=1024):
    npass = ntok // NP
    NB = NP // 512
    NL = NP + HALO
    nc = bass.Bass("TRN2", target_bir_lowering=False)
    x = nc.dram_tensor("x", [ntok + HALO, D], F32, kind="ExternalInput").ap()
    w_in = nc.dram_tensor("w_in", [48, 128, 16, 128], F32, kind="ExternalInput").ap()
    w_out = nc.dram_tensor("w_out", [128, 16, D], F32, kind="ExternalInput").ap()
    gbc_d = nc.dram_tensor("gbc", [128, D], F32, kind="ExternalInput").ap()
    dww_d = nc.dram_tensor("dww", [128, 16, CONV_W], F32, kind="ExternalInput").ap()
    cvec_d = nc.dram_tensor("cvec", [128, 48], F32, kind="ExternalInput").ap()
    identb_d = nc.dram_tensor("identb", [128, 128], BF16, kind="ExternalInput").ap()
    identf_d = nc.dram_tensor("identf", [128, 128], F32, kind="ExternalInput").ap()
    x1_o = nc.dram_tensor("x1", [ntok, D], F32, kind="ExternalOutput").ap()
    xnT_o = nc.dram_tensor("xnT", [ntok // 256, 128, 16, 256], BF16, kind="ExternalOutput").ap()
    wi_b = nc.dram_tensor("wi_b", [48, 128, 2048], BF16).ap()
    wo_b = nc.dram_tensor("wo_b", [16, 128, 2048], BF16).ap()

    P = Prog(nc)
    with contextlib.ExitStack() as es:
        def sb(name, shape, dt):
            return es.enter_context(nc.sbuf_tensor(name, shape, dt))

        HT_N = 16 * NL
        ARENA = max(HT_N + 6 * 2048 + 2 * CONV_W * 128, 16 * D)
        arena = sb("arena", [128, ARENA], BF16)
        hT = arena[:, 0:HT_N].rearrange("p (f t) -> p f t", f=16)
        wt = [arena[:, HT_N + i * 2048: HT_N + (i + 1) * 2048].rearrange("p (f n) -> p f n", f=16) for i in range(6)]
        wa, wb_, wz = wt[0:2], wt[2:4], wt[4:6]
        dgo = HT_N + 6 * 2048
        dg = [arena[:, dgo + i * CONV_W * 128: dgo + (i + 1) * CONV_W * 128].rearrange("p (j n) -> p j n", j=CONV_W) for i in range(2)]
        wob = arena[:, 0:16 * D].rearrange("p (c n) -> p c n", c=16)
        cvT = sb("cvT", [128, 16, NP], BF16)
        yT = [sb("yT%d" % i, [128, NL], BF16) for i in range(2)]
        acc_s = sb("acc_s", [128, NP], F32)
        acc_q = sb("acc_q", [128, NP], F32)
        rstd_bc = sb("rstd_bc", [128, NP], F32)
        nmr_bc = sb("nmr_bc", [128, NP], F32)
        xt = [sb("xt%d" % i, [128, D], F32) for i in range(2)]
        sqx = sb("sqx", [128, D], F32)
        tmp = {k: [sb("%s%d" % (k, i), [128, 512], F32) for i in range(2)] for k in ("ta", "tb", "tc", "td", "te", "tf")}
        xnblk = sb("xnblk", [128, 16, 256], BF16)
        gbc = sb("gbc_s", [128, D], F32)
        dww = sb("dww_s", [128, 16, CONV_W], F32)
        cvec = sb("cvec_s", [128, 48], F32)
        identb = sb("identb_s", [128, 128], BF16)
        identf = sb("identf_s", [128, 128], F32)
        onesf = sb("onesf", [128, 128], F32)
        small = sb("small", [128, 8], F32)
        B = [es.enter_context(nc.psum_tensor("b%d" % i, [128, 512], F32)) for i in range(8)]

        Tb = [T("bank%d" % i) for i in range(8)]
        T_c = T()
        T_hT = T()
        T_wa, T_wb, T_wz, T_dg = [T(), T()], [T(), T()], [T(), T()], [T(), T()]
        T_cv = [T() for _ in range(16)]
        T_yT = [T(), T()]
        T_acc, T_st = T(), T()
        T_xt = [T(), T()]
        T_sqx = T()
        T_tmp = {k: [T(), T()] for k in tmp}
        T_xnblk = T()
        T_sm = T()
        T_x1o, T_xno = T(), T()
        T_wob = T()

        dwb = lambda c: cvec[:, c:c + 1]
        lng = lambda c: cvec[:, 16 + c:17 + c]
        lnb = lambda c: cvec[:, 32 + c:33 + c]

        for (dst, src) in ((gbc, gbc_d), (dww, dww_d), (cvec, cvec_d), (identb, identb_d), (identf, identf_d)):
            P.op("sync", lambda e, dst=dst, src=src: e.dma_start(out=dst[:], in_=src), writes=[T_c], dma="c0")
        P.op("gpsimd", lambda e: e.memset(onesf[:], 1.0), writes=[T_c])

        T_wib = [T() for _ in range(48)]
        T_wobd = [T() for _ in range(16)]
        stg_in = [xt[0][:], xt[1][:], sqx[:]]
        T_stg_in = [[T_xt[0]], [T_xt[1]], [T_sqx]]
        xnflat = xnblk[:].rearrange("p f t -> p (f t)")
        stg_out = [xnflat[:, i * 2048:(i + 1) * 2048] for i in range(2)]
        T_stg_out = [T(), T()]
        pc = {"n": 0}

        def wsrc(k):
            if k < 48:
                return w_in[k].rearrange("p f n -> p (f n)"), wi_b[k], T_wib[k]
            return w_out[:, k - 48, :], wo_b[k - 48], T_wobd[k - 48]

        def precast(ks):
            slots = []
            for k in ks:
                n = pc["n"]; pc["n"] += 1
                i, o = n % 3, n % 2
                src = wsrc(k)[0]
                P.op("sync", lambda e, i=i, src=src: e.dma_start(out=stg_in[i], in_=src), writes=T_stg_in[i], dma="wl%d" % i)
                slots.append((k, n, i, o))
            for (k, n, i, o) in slots:
                _, dst, Tdst = wsrc(k)
                if n % 2 == 0:
                    P.op("vector", lambda e, i=i, o=o: e.tensor_copy(out=stg_out[o], in_=stg_in[i]), reads=T_stg_in[i],
                         writes=[T_stg_out[o], T_xnblk])
                else:
                    P.op("scalar", lambda e, i=i, o=o: e.copy(out=stg_out[o], in_=stg_in[i]), reads=T_stg_in[i],
                         writes=[T_stg_out[o], T_xnblk])
                P.op("sync", lambda e, o=o, dst=dst: e.dma_start(out=dst, in_=stg_out[o]), reads=[T_stg_out[o], T_xnblk], writes=[Tdst],
                     dma="ws%d" % o)

        late = list(range(32, 64))

        cnt = {"xt": 0}

        def rstd_from_ss(np_, col_in, col_out, n):
            P.op("scalar", lambda e: e.activation(out=small[0:np_, col_out:col_out + 1], in_=small[0:np_, col_in:col_in + 1],
                                                  func=ACT.Ln, scale=1.0 / n, bias=EPS), reads=[T_sm], writes=[T_sm])
            P.op("scalar", lambda e: e.activation(out=small[0:np_, col_out:col_out + 1], in_=small[0:np_, col_out:col_out + 1],
                                                  func=ACT.Exp, scale=-0.5), reads=[T_sm], writes=[T_sm])

        def norm_tile(row0, np_, col0):
            b = cnt["xt"] % 2
            cnt["xt"] += 1
            P.op("sync", lambda e: e.dma_start(out=xt[b][0:np_, :], in_=x[row0:row0 + np_, :]), writes=[T_xt[b]], dma="xt%d" % b)
            P.op("scalar", lambda e: e.activation(out=sqx[0:np_, :], in_=xt[b][0:np_, :], func=ACT.Square),
                 reads=[T_xt[b]], writes=[T_sqx])
            P.op("vector", lambda e: e.reduce_sum(out=small[0:np_, 0:1], in_=sqx[0:np_, :], axis=AX.X),
                 reads=[T_sqx], writes=[T_sm])
            rstd_from_ss(np_, 0, 1, float(D))
            P.op("scalar", lambda e: e.activation(out=sqx[0:np_, :], in_=xt[b][0:np_, :], func=ACT.Copy, scale=small[0:np_, 1:2]),
                 reads=[T_xt[b], T_sm], writes=[T_sqx])
            for ft in range(16):
                P.op("tensor", lambda e, ft=ft: e.transpose(
                    B[ft // 4][:, (ft % 4) * 128:(ft % 4) * 128 + np_], sqx[0:np_, ft * 128:(ft + 1) * 128], identf[0:np_, 0:np_]),
                    reads=[T_sqx, T_c], writes=[Tb[ft // 4]])
            for q in range(4):
                P.op("vector", lambda e, q=q: e.tensor_tensor(
                    out=hT[:, q * 4:(q + 1) * 4, col0:col0 + np_],
                    in0=B[q][:].rearrange("p (f t) -> p f t", f=4)[:, :, 0:np_],
                    in1=gbc[:, q * 512:(q + 1) * 512].rearrange("p (f t) -> p f t", f=4)[:, :, 0:np_], op=ALU.mult),
                    reads=[Tb[q], T_c], writes=[T_hT])

        def do_pass(h):
            r_h = h * NP
            r_o = HALO + h * NP
            norm_tile(r_h, HALO, 0)
            for t in range(NP // 128):
                norm_tile(r_o + t * 128, 128, HALO + t * 128)
            P.op("gpsimd", lambda e: e.memset(acc_s[:], 0.0), writes=[T_acc])
            P.op("gpsimd", lambda e: e.memset(acc_q[:], 0.0), writes=[T_acc])

            def prefetch1(c):
                b = c % 2
                P.op("gpsimd", lambda e: e.dma_start(out=wa[b], in_=wi_b[c].rearrange("p (f n) -> p f n", f=16)),
                     reads=[T_wib[c]], writes=[T_wa[b]], dma="wa%d" % b)
                P.op("gpsimd", lambda e: e.dma_start(out=wb_[b], in_=wi_b[16 + c].rearrange("p (f n) -> p f n", f=16)),
                     reads=[T_wib[16 + c]], writes=[T_wb[b]], dma="wb%d" % b)
                for j in range(CONV_W):
                    P.op("vector", lambda e, j=j: e.tensor_scalar_mul(out=dg[b][:, j, :], in0=identb[:], scalar1=dww[:, c, j:j + 1]),
                         reads=[T_c], writes=[T_dg[b]])

            def stage2(c):
                b = c % 2
                yt_ = yT[b]
                for (w_, T_w, off) in ((wa[b], T_wa[b], 0), (wb_[b], T_wb[b], 32)):
                    for ft in range(16):
                        P.op("tensor", lambda e, w_=w_, off=off, ft=ft: e.matmul(
                            B[6][:, off:off + HALO], lhsT=w_[:, ft, :], rhs=hT[:, ft, 0:HALO], start=(ft == 0), stop=(ft == 15)),
                            reads=[T_w, T_hT], writes=[Tb[6]])
                P.op("scalar", lambda e: e.activation(out=tmp["ta"][0][:, 0:HALO], in_=B[6][:, 32:32 + HALO], func=ACT.Sigmoid),
                     reads=[Tb[6]], writes=[T_tmp["ta"][0]])
                P.op("vector", lambda e: e.tensor_tensor(out=yt_[:, 0:HALO], in0=B[6][:, 0:HALO], in1=tmp["ta"][0][:, 0:HALO], op=ALU.mult),
                     reads=[Tb[6], T_tmp["ta"][0]], writes=[T_yT[b]])
                for tb in range(NB):
                    pa, pb = B[2 * (tb % 2)], B[2 * (tb % 2) + 1]
                    Ta, Tbb = Tb[2 * (tb % 2)], Tb[2 * (tb % 2) + 1]
                    c0 = HALO + tb * 512
                    for (w_, T_w, ps, Tp) in ((wa[b], T_wa[b], pa, Ta), (wb_[b], T_wb[b], pb, Tbb)):
                        for ft in range(16):
                            P.op("tensor", lambda e, w_=w_, ps=ps, ft=ft, c0=c0: e.matmul(
                                ps[:], lhsT=w_[:, ft, :], rhs=hT[:, ft, c0:c0 + 512], start=(ft == 0), stop=(ft == 15)),
                                reads=[T_w, T_hT], writes=[Tp])
                    sg = tmp["ta"][tb % 2]
                    P.op("scalar", lambda e, pb=pb, sg=sg: e.activation(out=sg[:], in_=pb[:], func=ACT.Sigmoid),
                         reads=[Tbb], writes=[T_tmp["ta"][tb % 2]])
                    P.op("vector", lambda e, pa=pa, sg=sg, c0=c0: e.tensor_tensor(out=yt_[:, c0:c0 + 512], in0=pa[:], in1=sg[:], op=ALU.mult),
                         reads=[Ta, T_tmp["ta"][tb % 2]], writes=[T_yT[b]])
                for tb in range(NB):
                    pc, Tc_ = B[4 + tb % 2], Tb[4 + tb % 2]
                    for j in range(CONV_W):
                        P.op("tensor", lambda e, pc=pc, j=j, tb=tb: e.matmul(
                            pc[:], lhsT=dg[b][:, j, :], rhs=yt_[:, tb * 512 + j: tb * 512 + j + 512],
                            start=(j == 0), stop=(j == CONV_W - 1)),
                            reads=[T_dg[b], T_yT[b]], writes=[Tc_])
                    blk = slice(tb * 512, (tb + 1) * 512)
                    sq_ = tmp["tb"][tb % 2]
                    P.op("scalar", lambda e, pc=pc, blk=blk: e.activation(out=cvT[:, c, blk], in_=pc[:], func=ACT.Identity, bias=dwb(c)),
                         reads=[Tc_, T_c], writes=[T_cv[c]])
                    P.op("scalar", lambda e, pc=pc, sq_=sq_: e.activation(out=sq_[:], in_=pc[:], func=ACT.Square, bias=dwb(c)),
                         reads=[Tc_, T_c], writes=[T_tmp["tb"][tb % 2]])
                    P.op("vector", lambda e, blk=blk: e.tensor_tensor(out=acc_s[:, blk], in0=acc_s[:, blk], in1=cvT[:, c, blk], op=ALU.add),
                         reads=[T_cv[c], T_acc], writes=[T_acc])
                    P.op("vector", lambda e, blk=blk, sq_=sq_: e.tensor_tensor(out=acc_q[:, blk], in0=acc_q[:, blk], in1=sq_[:], op=ALU.add),
                         reads=[T_tmp["tb"][tb % 2], T_acc], writes=[T_acc])

            if h == 0:
                precast([0, 16]); precast([1, 17])
            prefetch1(0)
            for c in range(16):
                if h == 0:
                    if c + 2 < 16:
                        precast([c + 2, 16 + c + 2])
                    precast([late[2 * c], late[2 * c + 1]])
                if c + 1 < 16:
                    prefetch1(c + 1)
                stage2(c)

            def prefetch_z(c):
                b = c % 2
                P.op("gpsimd", lambda e: e.dma_start(out=wz[b], in_=wi_b[32 + c].rearrange("p (f n) -> p f n", f=16)),
                     reads=[T_wib[32 + c]], writes=[T_wz[b]], dma="wz%d" % b)
            prefetch_z(0)
            for tb in range(NB):
                blk = slice(tb * 512, (tb + 1) * 512)
                P.op("tensor", lambda e, blk=blk: e.matmul(B[0][:], lhsT=onesf[:], rhs=acc_s[:, blk], start=True, stop=True),
                     reads=[T_c, T_acc], writes=[Tb[0]])
                P.op("tensor", lambda e, blk=blk: e.matmul(B[1][:], lhsT=onesf[:], rhs=acc_q[:, blk], start=True, stop=True),
                     reads=[T_c, T_acc], writes=[Tb[1]])
                mean, m2 = tmp["tc"][0], tmp["td"][0]
                P.op("vector", lambda e: e.tensor_scalar_mul(out=mean[:], in0=B[0][:], scalar1=1.0 / D),
                     reads=[Tb[0]], writes=[T_tmp["tc"][0]])
                P.op("vector", lambda e: e.tensor_tensor(out=m2[:], in0=mean[:], in1=mean[:], op=ALU.mult),
                     reads=[T_tmp["tc"][0]], writes=[T_tmp["td"][0]])
                P.op("vector", lambda e: e.scalar_tensor_tensor(out=m2[:], in0=B[1][:], scalar=1.0 / D, in1=m2[:],
                                                                op0=ALU.mult, op1=ALU.subtract),
                     reads=[Tb[1], T_tmp["td"][0]], writes=[T_tmp["td"][0]])
                P.op("scalar", lambda e, blk=blk: e.activation(out=rstd_bc[:, blk], in_=m2[:], func=ACT.Ln, bias=EPS),
                     reads=[T_tmp["td"][0]], writes=[T_st])
                P.op("scalar", lambda e, blk=blk: e.activation(out=rstd_bc[:, blk], in_=rstd_bc[:, blk], func=ACT.Exp, scale=-0.5),
                     reads=[T_st], writes=[T_st])
                P.op("vector", lambda e, blk=blk: e.scalar_tensor_tensor(out=nmr_bc[:, blk], in0=mean[:], scalar=-1.0, in1=rstd_bc[:, blk],
                                                                         op0=ALU.mult, op1=ALU.mult),
                     reads=[T_tmp["tc"][0], T_st], writes=[T_st])

            def stage4(c):
                b = c % 2
                for tb in range(NB):
                    k = tb % 2
                    pz, Tz = B[2 + k], Tb[2 + k]
                    c0 = HALO + tb * 512
                    blk = slice(tb * 512, (tb + 1) * 512)
                    for ft in range(16):
                        P.op("tensor", lambda e, pz=pz, ft=ft, c0=c0: e.matmul(
                            pz[:], lhsT=wz[b][:, ft, :], rhs=hT[:, ft, c0:c0 + 512], start=(ft == 0), stop=(ft == 15)),
                            reads=[T_wz[b], T_hT], writes=[Tz])
                    sz, gz, t1, s2, l_, u_ = (tmp[n][k] for n in ("ta", "tb", "tc", "td", "te", "tf"))
                    Ts = {n: T_tmp[n][k] for n in ("ta", "tb", "tc", "td", "te", "tf")}
                    P.op("scalar", lambda e, pz=pz, sz=sz: e.activation(out=sz[:], in_=pz[:], func=ACT.Sigmoid),
                         reads=[Tz], writes=[Ts["ta"]])
                    P.op("vector", lambda e, pz=pz, sz=sz, gz=gz: e.tensor_tensor(out=gz[:], in0=pz[:], in1=sz[:], op=ALU.mult),
                         reads=[Tz, Ts["ta"]], writes=[Ts["tb"]])
                    P.op("vector", lambda e, t1=t1, blk=blk: e.tensor_tensor(out=t1[:], in0=cvT[:, c, blk], in1=rstd_bc[:, blk], op=ALU.mult),
                         reads=[T_cv[c], T_st], writes=[Ts["tc"]])
                    P.op("vector", lambda e, t1=t1, blk=blk: e.tensor_tensor(out=t1[:], in0=t1[:], in1=nmr_bc[:, blk], op=ALU.add),
                         reads=[Ts["tc"], T_st], writes=[Ts["tc"]])
                    P.op("scalar", lambda e, t1=t1, s2=s2: e.activation(out=s2[:], in_=t1[:], func=ACT.Sigmoid, scale=lng(c), bias=lnb(c)),
                         reads=[Ts["tc"], T_c], writes=[Ts["td"]])
                    P.op("vector", lambda e, t1=t1, l_=l_: e.tensor_scalar(out=l_[:], in0=t1[:], scalar1=lng(c), scalar2=lnb(c),
                                                                          op0=ALU.mult, op1=ALU.add),
                         reads=[Ts["tc"], T_c], writes=[Ts["te"]])
                    P.op("vector", lambda e, l_=l_, s2=s2, u_=u_: e.tensor_tensor(out=u_[:], in0=l_[:], in1=s2[:], op=ALU.mult),
                         reads=[Ts["te"], Ts["td"]], writes=[Ts["tf"]])
                    P.op("vector", lambda e, u_=u_, gz=gz, blk=blk: e.tensor_tensor(out=cvT[:, c, blk], in0=u_[:], in1=gz[:], op=ALU.mult),
                         reads=[Ts["tf"], Ts["tb"]], writes=[T_cv[c]])

            for c in range(16):
                if c + 1 < 16:
                    prefetch_z(c + 1)
                stage4(c)

            alias = [T_hT] + T_wa + T_wb + T_wz + T_dg
            for ct in range(16):
                P.op("sync", lambda e, ct=ct: e.dma_start(out=wob[:, ct, :], in_=wo_b[ct]), reads=[T_wobd[ct]], writes=[T_wob] + alias,
                     dma="wo%d" % (ct % 4))

            xb_of = {}

            def s5_mm(t):
                b = cnt["xt"] % 2
                cnt["xt"] += 1
                xb_of[t] = b
                bs = 4 * (t % 2)
                row = h * NP + t * 128
                P.op("sync", lambda e: e.dma_start(out=xt[b][:], in_=x[HALO + row:HALO + row + 128, :]), writes=[T_xt[b]], dma="xt%d" % b)
                for fb in range(4):
                    for ct in range(16):
                        P.op("tensor", lambda e, fb=fb, ct=ct: e.matmul(
                            B[bs + fb][:], lhsT=cvT[:, ct, t * 128:(t + 1) * 128], rhs=wob[:, ct, fb * 512:(fb + 1) * 512],
                            start=(ct == 0), stop=(ct == 15)),
                            reads=[T_cv[ct], T_wob], writes=[Tb[bs + fb]])

            def s5_rest(t):
                b = xb_of[t]
                bs = 4 * (t % 2)
                row = h * NP + t * 128
                for fb in range(4):
                    P.op("vector", lambda e, fb=fb: e.tensor_tensor(out=xt[b][:, fb * 512:(fb + 1) * 512], in0=B[bs + fb][:],
                                                                    in1=xt[b][:, fb * 512:(fb + 1) * 512], op=ALU.add),
                         reads=[Tb[bs + fb], T_xt[b]], writes=[T_xt[b]])
                P.op("gpsimd", lambda e: e.dma_start(out=x1_o[row:row + 128, :], in_=xt[b][:]), reads=[T_xt[b]], writes=[T_x1o], dma="x1o%d" % b)
                P.op("scalar", lambda e: e.activation(out=sqx[:], in_=xt[b][:], func=ACT.Square), reads=[T_xt[b]], writes=[T_sqx])
                P.op("vector", lambda e: e.reduce_sum(out=small[:, 0:1], in_=sqx[:], axis=AX.X), reads=[T_sqx], writes=[T_sm])
                rstd_from_ss(128, 0, 1, float(D))
                P.op("scalar", lambda e: e.activation(out=sqx[:], in_=xt[b][:], func=ACT.Copy, scale=small[:, 1:2]),
                     reads=[T_xt[b], T_sm], writes=[T_sqx])
                for ft in range(16):
                    P.op("tensor", lambda e, ft=ft: e.transpose(
                        B[bs + ft // 4][:, (ft % 4) * 128:(ft % 4 + 1) * 128], sqx[:, ft * 128:(ft + 1) * 128], identf[:]),
                        reads=[T_sqx, T_c], writes=[Tb[bs + ft // 4]])
                half = t % 2
                for q in range(4):
                    P.op("scalar", lambda e, q=q: e.copy(out=xnblk[:, q * 4:(q + 1) * 4, half * 128:(half + 1) * 128],
                                                         in_=B[bs + q][:].rearrange("p (f t) -> p f t", f=4)),
                         reads=[Tb[bs + q]], writes=[T_xnblk])
                if half == 1:
                    blk_i = (h * NP + t * 128) // 256
                    P.op("gpsimd", lambda e: e.dma_start(out=xnT_o[blk_i], in_=xnblk[:]), reads=[T_xnblk], writes=[T_xno], dma="xno")

            n5 = NP // 128
            s5_mm(0)
            for t in range(n5):
                if t + 1 < n5:
                    s5_mm(t + 1)
                s5_rest(t)

        for h in range(npass):
            do_pass(h)
        P.emit()
    return nc


def l1_inputs(x, a_norm_g, a_w_in, a_dw_w, a_dw_b, a_ln_g, a_ln_b, a_w_out, ntok=2048, ncores=NCORES):
    w_in_l = np.ascontiguousarray(a_w_in.reshape(16, 128, 48, 128).transpose(2, 1, 0, 3))
    w_out_l = np.ascontiguousarray(a_w_out.reshape(16, 128, D).transpose(1, 0, 2))
    gbc = np.ascontiguousarray(np.broadcast_to(a_norm_g.reshape(16, 128).T[:, :, None], (128, 16, 128)).reshape(128, D)).astype(np.float32)
    dww = np.ascontiguousarray(a_dw_w.reshape(CONV_W, 16, 128).transpose(2, 1, 0)).astype(np.float32)
    cvec = np.ascontiguousarray(np.concatenate(
        [a_dw_b.reshape(16, 128).T, a_ln_g.reshape(16, 128).T, a_ln_b.reshape(16, 128).T], axis=1)).astype(np.float32)
    identf = np.eye(128, dtype=np.float32)
    identb = identf.astype(ml_dtypes.bfloat16)
    xp = np.concatenate([np.zeros((HALO, D), np.float32), x], axis=0)
    maps = []
    for c in range(ncores):
        maps.append({"x": np.ascontiguousarray(xp[c * ntok: c * ntok + ntok + HALO]), "w_in": w_in_l, "w_out": w_out_l,
                     "gbc": gbc, "dww": dww, "cvec": cvec, "identb": identb, "identf": identf})
    return maps


_CACHE = {}


def _get(name, fn):
    if name not in _CACHE:
        _CACHE[name] = fn()
    return _CACHE[name]


def kernel(x, a_norm_g, a_w_in, a_dw_w, a_dw_b, a_ln_g, a_ln_b, a_w_out,
           kv_norm_g, w_kv, k_norm_g, b_norm_g, b_w_in, b_q_norm_g,
           b_lambda, b_subln_g, b_w_out):
    f = lambda a: np.asarray(a, dtype=np.float32)
    x2 = f(x).reshape(S, D)
    cores = list(range(NCORES))
    ntok = S // NCORES
    nc1 = _get("l1", lambda: build_l1(ntok, 1024))
    m1 = l1_inputs(x2, f(a_norm_g)[0], f(a_w_in)[0], f(a_dw_w)[0], f(a_dw_b)[0], f(a_ln_g)[0], f(a_ln_b)[0], f(a_w_out)[0],
                   ntok=ntok, ncores=NCORES)
    r1 = run_bass_kernel_spmd(nc1, m1, core_ids=cores).results
    x1 = [r1[c]["x1"] for c in cores]
    xnT_blocks = np.concatenate([r1[c]["xnT"] for c in cores], axis=0)
    nc2 = _get("l2", lambda: build_l2(S))
    xnT_full = np.ascontiguousarray(xnT_blocks.transpose(2, 1, 0, 3)).reshape(16, 128, S)
    m2 = l2_inputs(xnT_full, f(w_kv), f(b_w_in)[0], f(kv_norm_g), f(b_norm_g)[0], f(k_norm_g), f(b_q_norm_g)[0],
                   f(b_lambda)[0], f(b_subln_g)[0], s_len=S)
    r2 = run_bass_kernel_spmd(nc2, m2, core_ids=cores).results
    o_full = np.concatenate([r2[c]["o"] for c in cores], axis=1)
    nc3 = _get("l3", lambda: build_l3(ntok))
    wo = np.ascontiguousarray(f(b_w_out)[0].reshape(16, 128, D).transpose(1, 0, 2))
    m3 = []
    for c in cores:
        oc = o_full[c * ntok:(c + 1) * ntok]
        oT = np.ascontiguousarray(oc.reshape(ntok // 128, 128, 16, 128).transpose(0, 3, 2, 1))
        m3.append({"oT": oT, "x1": x1[c], "wo": wo})
    r3 = run_bass_kernel_spmd(nc3, m3, core_ids=cores).results
    out = np.concatenate([r3[c]["y"] for c in cores], axis=0).reshape(1, S, D).astype(np.float32)
    return out
```

```python
import contextlib
import math
import numpy as np
import ml_dtypes
import concourse.bass as bass
import concourse.mybir as mybir
from concourse.bass_utils import run_bass_kernel_spmd

F32 = mybir.dt.float32
BF16 = mybir.dt.bfloat16
ALU = mybir.AluOpType
ACT = mybir.ActivationFunctionType
AX = mybir.AxisListType

NCORES = 8
D = 2048
S = 16384
EPS = 1e-6
CONV_W = 31
HALO = CONV_W - 1
DH = 128
LAM_INIT = 0.8 - 0.6 * math.exp(-0.3 * (2 - 1))

ENGS = ("tensor", "vector", "scalar", "gpsimd", "sync")


class T:
    __slots__ = ("name", "last_w", "readers")

    def __init__(self, name=""):
        self.name = name
        self.last_w = None
        self.readers = []


class Prog:
    def __init__(self, nc):
        self.nc = nc
        self.ops = []
        self.groups = {}

    def op(self, eng, fn, reads=(), writes=(), dma=None, inc=16):
        i = len(self.ops)
        deps = set()
        for t in reads:
            if t.last_w is not None:
                deps.add(t.last_w)
        for t in writes:
            if t.last_w is not None:
                deps.add(t.last_w)
            deps.update(t.readers)
        deps.discard(i)
        for t in reads:
            t.readers.append(i)
        for t in writes:
            t.last_w = i
            t.readers = []
        gneed = {}
        for jd in deps:
            g = self.ops[jd]["dma"]
            if g is not None:
                gneed[g] = self.groups[g]
        o = dict(i=i, eng=eng, fn=fn, deps=deps, dma=dma, sig=False, sidx=0, inc=inc, gneed=gneed)
        if dma is not None:
            self.groups[dma] = self.groups.get(dma, 0) + inc
            o["sidx"] = self.groups[dma]
        self.ops.append(o)
        return i

    def emit(self, final_wait_engine="sync"):
        nc = self.nc
        ops = self.ops
        for o in ops:
            for j in o["deps"]:
                d = ops[j]
                if d["dma"] is not None:
                    continue
                if d["eng"] == "tensor" and o["eng"] == "tensor" and o["dma"] is None:
                    continue
                d["sig"] = True
        cnt = {e: 0 for e in ENGS}
        for o in ops:
            if o["dma"] is None and o["sig"]:
                cnt[o["eng"]] += 1
                o["sidx"] = cnt[o["eng"]]
        with contextlib.ExitStack() as es:
            esem = {e: es.enter_context(nc.semaphore("p_" + e)) for e in ENGS}
            gsem = {g: es.enter_context(nc.semaphore("g_%d" % k))
                    for k, g in enumerate(self.groups)}
            block = es.enter_context(nc.Block())
            final = dict(self.groups)

            def make(ename):
                def body(eng):
                    waited_e = {e: 0 for e in ENGS}
                    waited_g = {g: 0 for g in self.groups}
                    for o in ops:
                        if o["eng"] != ename:
                            continue
                        need_e = {}
                        need_g = o["gneed"]
                        for j in o["deps"]:
                            d = ops[j]
                            if d["dma"] is not None:
                                continue
                            else:
                                if d["eng"] == "tensor" and ename == "tensor" and o["dma"] is None:
                                    continue
                                need_e[d["eng"]] = max(need_e.get(d["eng"], 0), d["sidx"])
                        for e, v in need_e.items():
                            if v > waited_e[e]:
                                eng.wait_ge(esem[e], v)
                                waited_e[e] = v
                        for g, v in need_g.items():
                            if v > waited_g[g]:
                                eng.wait_ge(gsem[g], v)
                                waited_g[g] = v
                        ins = o["fn"](eng)
                        if o["dma"] is not None:
                            ins.then_inc(gsem[o["dma"]], o["inc"])
                        elif o["sig"]:
                            ins.then_inc(esem[ename], 1)
                    if ename == final_wait_engine:
                        for g, v in final.items():
                            if v > waited_g[g]:
                                eng.wait_ge(gsem[g], v)
                        for e in ENGS:
                            if e != ename and cnt[e] > waited_e[e]:
                                eng.wait_ge(esem[e], cnt[e])
                return body

            block.tensor(make("tensor"))
            block.vector(make("vector"))
            block.scalar(make("scalar"))
            block.gpsimd(make("gpsimd"))
            block.sync(make("sync"))


def build_l2(s_len=S, dbg=False):
    QB = 256
    nblk = s_len // QB
    ntile = s_len // 128
    nc = bass.Bass("TRN2", target_bir_lowering=False)
    xnT = nc.dram_tensor("xnT", [nblk, 128, 16, QB], BF16, kind="ExternalInput").ap()
    wA = nc.dram_tensor("wA", [128, 16, 512], F32, kind="ExternalInput").ap()
    wB = nc.dram_tensor("wB", [128, 16, 512], F32, kind="ExternalInput").ap()
    gcol = nc.dram_tensor("gcol", [128, 32], F32, kind="ExternalInput").ap()
    gqk = nc.dram_tensor("gqk", [128, 2], F32, kind="ExternalInput").ap()
    gsub = nc.dram_tensor("gsub", [128, 256], F32, kind="ExternalInput").ap()
    lamp = nc.dram_tensor("lamp", [128, 512], F32, kind="ExternalInput").ap()
    o_out = nc.dram_tensor("o", [s_len, 256], BF16, kind="ExternalOutput").ap()

    P = Prog(nc)
    with contextlib.ExitStack() as es:
        def sb(name, shape, dt):
            return es.enter_context(nc.sbuf_tensor(name, shape, dt))

        KT = sb("KT", [128, 2, s_len], BF16)
        V = sb("V", [128, ntile, 257], BF16)
        wAb = sb("wAb", [128, 16, 512], BF16)
        wBb = sb("wBb", [128, 16, 512], BF16)
        xb = [sb("xb%d" % i, [128, 16, QB], BF16) for i in range(2)]
        QTb = [sb("QTb%d" % i, [128, 2, QB], BF16) for i in range(2)]
        PT = [sb("PT%d" % i, [128, 4, QB], BF16) for i in range(3)]
        sq = sb("sq", [128, 1024], BF16)
        rst = sb("rst", [128, 1024], F32)
        th = [sb("th%d" % i, [128, 256], F32) for i in range(2)]
        Zg = [sb("Zg%d" % i, [128, 256], BF16) for i in range(4)]
        ea = [sb("ea%d" % i, [128, 256], F32) for i in range(2)]
        eo = [sb("eo%d" % i, [128, 256], F32) for i in range(2)]
        esq = ea
        eof = ea
        ob = [sb("ob%d" % i, [128, 256], BF16) for i in range(4)]
        small = [sb("small%d" % i, [128, 8], F32) for i in range(2)]
        gcol_s = sb("gcol_s", [128, 32], F32)
        gqk_s = sb("gqk_s", [128, 2], F32)
        gsub_s = sb("gsub_s", [128, 256], F32)
        lam_v = sb("lam_v", [128, 4], F32)
        ones = sb("ones", [128, 128], BF16)
        wst = [sb("wst%d" % i, [128, 1, 512], F32) for i in range(2)]
        PS = [es.enter_context(nc.psum_tensor("ps%d" % i, [128, 1024], F32)) for i in range(4)]

        Tb = [T("bank%d" % i) for i in range(8)]
        T_KT = [T() for _ in range(nblk)]
        T_V = [T() for _ in range(ntile)]
        T_wA, T_wB, T_g, T_lam, T_ones = T(), T(), T(), T(), T()
        T_wst = [T(), T()]
        T_xb = [T(), T()]
        T_QT = [T(), T()]
        T_PT = [T(), T(), T()]
        T_sq, T_rst = [T(), T()], [T(), T()]
        T_th = [T(), T()]
        T_Zg = [T() for _ in range(4)]
        T_e = [T(), T()]
        T_ob = [T() for _ in range(4)]
        T_out = T()

        P.op("sync", lambda e: e.dma_start(out=gcol_s[:], in_=gcol), writes=[T_g], dma="c0")
        P.op("sync", lambda e: e.dma_start(out=gqk_s[:], in_=gqk), writes=[T_g], dma="c0")
        P.op("sync", lambda e: e.dma_start(out=gsub_s[:], in_=gsub), writes=[T_g], dma="c0")
        P.op("sync", lambda e: e.dma_start(out=ea[0][:], in_=lamp[:, 0:256]), writes=[T_lam], dma="c0")
        P.op("sync", lambda e: e.dma_start(out=ea[1][:], in_=lamp[:, 256:512]), writes=[T_lam], dma="c0")
        P.op("gpsimd", lambda e: e.memset(ones[:], 1.0), writes=[T_ones])
        P.op("gpsimd", lambda e: e.memset(V[:, :, 256:257], 1.0), writes=T_V)
        P.op("vector", lambda e: e.tensor_scalar_mul(out=gqk_s[:, 0:1], in0=gqk_s[:, 0:1], scalar1=math.sqrt(128.0)),
             reads=[T_g], writes=[T_g])
        P.op("vector", lambda e: e.tensor_scalar_mul(out=gsub_s[:], in0=gsub_s[:], scalar1=(1.0 - LAM_INIT) * 16.0),
             reads=[T_g], writes=[T_g])
        P.op("vector", lambda e: e.tensor_tensor(out=eo[0][:, 0:128], in0=ea[0][:, 0:128], in1=ea[0][:, 128:256], op=ALU.mult),
             reads=[T_lam], writes=[T_lam])
        P.op("vector", lambda e: e.tensor_tensor(out=eo[0][:, 128:256], in0=ea[1][:, 0:128], in1=ea[1][:, 128:256], op=ALU.mult),
             reads=[T_lam], writes=[T_lam])
        P.op("vector", lambda e: e.reduce_sum(out=lam_v[:, 0:2], in_=eo[0][:].rearrange("p (a b) -> p a b", a=2), axis=AX.X),
             reads=[T_lam], writes=[T_lam, T_e[0], T_e[1]])
        P.op("scalar", lambda e: e.activation(out=lam_v[:, 2:4], in_=lam_v[:, 0:2], func=ACT.Exp),
             reads=[T_lam], writes=[T_lam])
        P.op("vector", lambda e: e.tensor_tensor(out=lam_v[:, 0:1], in0=lam_v[:, 3:4], in1=lam_v[:, 2:3], op=ALU.subtract),
             reads=[T_lam], writes=[T_lam])
        P.op("vector", lambda e: e.tensor_scalar_add(out=lam_v[:, 1:2], in0=lam_v[:, 0:1], scalar1=-LAM_INIT),
             reads=[T_lam], writes=[T_lam])
        nlam = lam_v[:, 1:2]
        k = 0
        for (wsrc, wdst, Tw) in ((wA, wAb, T_wA), (wB, wBb, T_wB)):
            for ch in range(16):
                b = k % 2
                P.op("sync", lambda e, b=b, ch=ch, wsrc=wsrc: e.dma_start(out=wst[b][:], in_=wsrc[:, ch:ch + 1, :]),
                     writes=[T_wst[b]], dma="wst%d" % b)
                for f in range(1):
                    ft = ch + f
                    for half in range(2):
                        if wsrc is wA:
                            gi = ft if half == 0 else 16 + ft
                        else:
                            gi = ft if half == 0 else 16 + ft
                        P.op("vector", lambda e, b=b, f=f, ft=ft, half=half, gi=gi, wdst=wdst: e.tensor_scalar_mul(
                            out=wdst[:, ft, half * 256:(half + 1) * 256], in0=wst[b][:, f, half * 256:(half + 1) * 256],
                            scalar1=gcol_s[:, gi:gi + 1]),
                            reads=[T_wst[b], T_g], writes=[Tw])
                k += 1

        def load_xb(j):
            b = j % 2
            P.op("sync", lambda e: e.dma_start(out=xb[b][:], in_=xnT[j]), writes=[T_xb[b]], dma="xb%d" % b)

        load_xb(0)
        def do_block(j):
            jb = j % 2
            if j + 1 < nblk:
                load_xb(j + 1)
            def proj_kq(i):
                for ft in range(16):
                    P.op("tensor", lambda e, ft=ft: e.matmul(
                        PS[0][:, i * 256:(i + 1) * 256], lhsT=wAb[:, ft, i * 128:(i + 1) * 128], rhs=xb[jb][:, ft, :],
                        start=(ft == 0), stop=(ft == 15)),
                        reads=[T_wA, T_xb[jb]], writes=[Tb[i // 2]])

            def proj_vz(ti):
                for ft in range(16):
                    P.op("tensor", lambda e, ft=ft: e.matmul(
                        PS[1][:, ti * 512:(ti + 1) * 512], lhsT=xb[jb][:, ft, ti * 128:(ti + 1) * 128], rhs=wBb[:, ft, :],
                        start=(ft == 0), stop=(ft == 15)),
                        reads=[T_wB, T_xb[jb]], writes=[Tb[2 + ti]])

            def square(hf):
                P.op("scalar", lambda e: e.activation(out=sq[:, hf * 512:(hf + 1) * 512], in_=PS[0][:, hf * 512:(hf + 1) * 512],
                                                      func=ACT.Square),
                     reads=[Tb[hf]], writes=[T_sq[hf]])

            def ssq(hf):
                for i in (2 * hf, 2 * hf + 1):
                    P.op("tensor", lambda e, i=i: e.matmul(PS[2][:, i * 256:(i + 1) * 256], lhsT=ones[:], rhs=sq[:, i * 256:(i + 1) * 256],
                                                          start=True, stop=True),
                         reads=[T_ones, T_sq[hf]], writes=[Tb[4 + hf]])

            def rstd(hf):
                P.op("scalar", lambda e: e.activation(out=rst[:, hf * 512:(hf + 1) * 512], in_=PS[2][:, hf * 512:(hf + 1) * 512],
                                                      func=ACT.Ln, bias=128.0 * EPS),
                     reads=[Tb[4 + hf]], writes=[T_rst[hf]])
                P.op("scalar", lambda e: e.activation(out=rst[:, hf * 512:(hf + 1) * 512], in_=rst[:, hf * 512:(hf + 1) * 512],
                                                      func=ACT.Exp, scale=-0.5),
                     reads=[T_rst[hf]], writes=[T_rst[hf]])

            def vz_tile(ti):
                tt = 2 * j + ti
                zi = jb * 2 + ti
                P.op("scalar", lambda e, ti=ti, tt=tt: e.copy(out=V[:, tt, 0:256], in_=PS[1][:, ti * 512:ti * 512 + 256]),
                     reads=[Tb[2 + ti]], writes=[T_V[tt]])
                P.op("scalar", lambda e, ti=ti: e.activation(out=th[ti][:], in_=PS[1][:, ti * 512 + 256:(ti + 1) * 512],
                                                             func=ACT.Exp, scale=-1.0),
                     reads=[Tb[2 + ti]], writes=[T_th[ti]])
                P.op("gpsimd", lambda e, ti=ti: e.tensor_scalar_add(out=th[ti][:], in0=th[ti][:], scalar1=1.0),
                     reads=[T_th[ti]], writes=[T_th[ti]])
                P.op("vector", lambda e, ti=ti: e.reciprocal(out=th[ti][:], in_=th[ti][:]),
                     reads=[T_th[ti]], writes=[T_th[ti]])
                P.op("vector", lambda e, ti=ti, zi=zi: e.tensor_tensor(
                    out=Zg[zi][:], in0=th[ti][:], in1=PS[1][:, ti * 512 + 256:(ti + 1) * 512], op=ALU.mult),
                    reads=[T_th[ti], Tb[2 + ti]], writes=[T_Zg[zi]])

            proj_kq(2); proj_kq(3)
            square(1)
            proj_kq(0); proj_kq(1)
            ssq(1)
            square(0)
            rstd(1)
            proj_vz(0)
            P.op("vector", lambda e: e.scalar_tensor_tensor(
                out=QTb[jb][:], in0=PS[0][:, 512:1024].rearrange("p (m t) -> p m t", m=2),
                scalar=gqk_s[:, 1:2], in1=rst[:, 512:1024].rearrange("p (m t) -> p m t", m=2), op0=ALU.mult, op1=ALU.mult),
                reads=[Tb[1], T_rst[1], T_g], writes=[T_QT[jb]])
            ssq(0)
            rstd(0)
            proj_vz(1)
            P.op("vector", lambda e: e.scalar_tensor_tensor(
                out=KT[:, :, j * QB:(j + 1) * QB], in0=PS[0][:, 0:512].rearrange("p (m t) -> p m t", m=2),
                scalar=gqk_s[:, 0:1], in1=rst[:, 0:512].rearrange("p (m t) -> p m t", m=2), op0=ALU.mult, op1=ALU.mult),
                reads=[Tb[0], T_rst[0], T_g], writes=[T_KT[j]])
            for ti in range(2):
                vz_tile(ti)

            npair = j + 1
            def do_qk(p):
                sp = p % 2
                for kl in range(2):
                    kt = 2 * p + kl
                    for m in range(2):
                        P.op("tensor", lambda e, kt=kt, kl=kl, m=m, sp=sp: e.matmul(
                            PS[sp][:, (kl * 2 + m) * 256:(kl * 2 + m + 1) * 256],
                            lhsT=KT[:, m, kt * 128:(kt + 1) * 128], rhs=QTb[jb][:, m, :], start=True, stop=True),
                            reads=[T_KT[kt // 2], T_QT[jb]], writes=[Tb[2 * sp + kl]])

            def do_pair(p):
                sp = p % 2
                pb = p % 3
                last = (p == npair - 1)
                P.op("scalar", lambda e, sp=sp, pb=pb: e.activation(
                    out=PT[pb][:].rearrange("p a q -> p (a q)"), in_=PS[sp][:], func=ACT.Exp),
                    reads=[Tb[2 * sp], Tb[2 * sp + 1]], writes=[T_PT[pb]])
                if last:
                    P.op("vector", lambda e, pb=pb: e.memset(PT[pb][64:128, 0:2, 0:64], 0.0), writes=[T_PT[pb]])
                    P.op("vector", lambda e, pb=pb: e.memset(PT[pb][64:128, 2:4, 128:192], 0.0), writes=[T_PT[pb]])
                for kl in range(2):
                    kt = 2 * p + kl
                    for qt in range(2):
                        if last and kl == 1 and qt == 0:
                            continue
                        for m in range(2):
                            P.op("tensor", lambda e, kt=kt, kl=kl, m=m, qt=qt, pb=pb: e.matmul(
                                PS[2 + qt][:, m * 512:m * 512 + 257],
                                lhsT=PT[pb][:, kl * 2 + m, qt * 128:(qt + 1) * 128], rhs=V[:, kt, :],
                                start=(kt == 0), stop=(kt == 2 * j + qt)),
                                reads=[T_PT[pb], T_V[kt]], writes=[Tb[4 + 2 * qt + m]])

            do_qk(0)
            for p in range(npair):
                if p + 1 < npair:
                    do_qk(p + 1)
                do_pair(p)

            def epilogue(qt):
                tt = 2 * j + qt
                zi = jb * 2 + qt
                O1 = PS[2 + qt][:, 0:257]
                O2 = PS[2 + qt][:, 512:769]
                b1, b2 = Tb[4 + 2 * qt], Tb[4 + 2 * qt + 1]
                sm = small[qt]
                P.op("vector", lambda e, O1=O1, sm=sm: e.reciprocal(out=sm[:, 0:1], in_=O1[:, 256:257]),
                     reads=[b1], writes=[T_e[qt]])
                P.op("vector", lambda e, O2=O2, sm=sm: e.reciprocal(out=sm[:, 1:2], in_=O2[:, 256:257]),
                     reads=[b2], writes=[T_e[qt]])
                P.op("vector", lambda e, sm=sm: e.tensor_tensor(out=sm[:, 2:3], in0=sm[:, 1:2], in1=nlam, op=ALU.mult),
                     reads=[T_e[qt], T_lam], writes=[T_e[qt]])
                P.op("vector", lambda e, O1=O1, sm=sm, qt=qt: e.tensor_scalar_mul(out=ea[qt][:], in0=O1[:, 0:256], scalar1=sm[:, 0:1]),
                     reads=[b1, T_e[qt]], writes=[T_e[qt]])
                P.op("vector", lambda e, O2=O2, sm=sm, qt=qt: e.scalar_tensor_tensor(
                    out=eo[qt][:], in0=O2[:, 0:256], scalar=sm[:, 2:3], in1=ea[qt][:], op0=ALU.mult, op1=ALU.add),
                    reads=[b2, T_e[qt]], writes=[T_e[qt]])
                P.op("scalar", lambda e, qt=qt: e.activation(out=esq[qt][:], in_=eo[qt][:], func=ACT.Square),
                     reads=[T_e[qt]], writes=[T_e[qt]])
                P.op("vector", lambda e, sm=sm, qt=qt: e.reduce_sum(out=sm[:, 3:4], in_=esq[qt][:], axis=AX.X),
                     reads=[T_e[qt]], writes=[T_e[qt]])
                P.op("scalar", lambda e, sm=sm: e.activation(out=sm[:, 4:5], in_=sm[:, 3:4], func=ACT.Ln, bias=256.0 * EPS),
                     reads=[T_e[qt]], writes=[T_e[qt]])
                P.op("scalar", lambda e, sm=sm: e.activation(out=sm[:, 4:5], in_=sm[:, 4:5], func=ACT.Exp, scale=-0.5),
                     reads=[T_e[qt]], writes=[T_e[qt]])
                P.op("vector", lambda e, sm=sm, qt=qt: e.scalar_tensor_tensor(
                    out=eof[qt][:], in0=eo[qt][:], scalar=sm[:, 4:5], in1=gsub_s[:], op0=ALU.mult, op1=ALU.mult),
                    reads=[T_e[qt], T_g], writes=[T_e[qt]])
                P.op("gpsimd", lambda e, qt=qt, zi=zi: e.tensor_tensor(out=ob[zi][:], in0=eof[qt][:], in1=Zg[zi][:], op=ALU.mult),
                     reads=[T_e[qt], T_Zg[zi]], writes=[T_ob[zi]])
                P.op("gpsimd", lambda e, tt=tt, zi=zi: e.dma_start(out=o_out[tt * 128:(tt + 1) * 128, :], in_=ob[zi][:]),
                     reads=[T_ob[zi]], writes=[T_out], dma="ob%d" % zi)
            for qt in range(2):
                epilogue(qt)

        for j in range(nblk):
            do_block(j)

        if dbg:
            dk = nc.dram_tensor("d_KT", [128, 2, s_len], BF16, kind="ExternalOutput").ap()
            dv = nc.dram_tensor("d_V", [128, ntile, 257], BF16, kind="ExternalOutput").ap()
            dq = nc.dram_tensor("d_QT", [128, 2, QB], BF16, kind="ExternalOutput").ap()
            dr = nc.dram_tensor("d_rst", [128, 1024], F32, kind="ExternalOutput").ap()
            dz = nc.dram_tensor("d_Zg", [128, 256], BF16, kind="ExternalOutput").ap()
            dp = nc.dram_tensor("d_PT", [128, 4, QB], BF16, kind="ExternalOutput").ap()
            de = nc.dram_tensor("d_eo", [128, 256], F32, kind="ExternalOutput").ap()
            dea = nc.dram_tensor("d_ea", [128, 256], F32, kind="ExternalOutput").ap()
            dsm = nc.dram_tensor("d_sm", [128, 8], F32, kind="ExternalOutput").ap()
            dl = nc.dram_tensor("d_lam", [128, 4], F32, kind="ExternalOutput").ap()
            dw = nc.dram_tensor("d_wA", [128, 16, 512], BF16, kind="ExternalOutput").ap()
            allT = T_KT + T_V + T_QT + T_rst + T_Zg + T_PT + T_e + [T_lam, T_wA]
            for (dst, src) in ((dk, KT), (dv, V), (dq, QTb[(nblk - 1) % 2]), (dr, rst), (dz, Zg[((nblk - 1) % 2) * 2]),
                               (dp, PT[(nblk - 1) % 3]), (de, eo[0]), (dea, ea[0]), (dsm, small[0]), (dl, lam_v), (dw, wAb)):
                P.op("sync", lambda e, dst=dst, src=src: e.dma_start(out=dst, in_=src[:]), reads=allT, writes=[T_out], dma="dbg")
        P.emit()
    return nc


def l2_inputs(xnT_full, w_kv, b_w_in, kv_norm_g, b_norm_g, k_norm_g, q_norm_g, lam, subln_g, s_len=S):
    QB = 256
    nblk = s_len // QB
    xb = np.ascontiguousarray(
        xnT_full.reshape(16, 128, nblk, QB).transpose(2, 1, 0, 3))

    def wl(w):
        return np.ascontiguousarray(w.reshape(16, 128, w.shape[1]).transpose(1, 0, 2))

    gcol = np.ascontiguousarray(np.concatenate(
        [kv_norm_g.reshape(16, 128).T, b_norm_g.reshape(16, 128).T], axis=1)).astype(np.float32)
    gqk = np.ascontiguousarray(np.stack([k_norm_g, q_norm_g], axis=1)).astype(np.float32)
    gsub = np.ascontiguousarray(np.broadcast_to(subln_g.reshape(1, 256), (128, 256))).astype(np.float32)
    lamp = np.ascontiguousarray(np.broadcast_to(lam.reshape(1, 512), (128, 512))).astype(np.float32)
    maps = []
    for c in range(NCORES):
        kc = w_kv[:, c * 256:(c + 1) * 256]
        vc = w_kv[:, 2048 + c * 256:2048 + (c + 1) * 256]
        qc = b_w_in[:, c * 256:(c + 1) * 256]
        zc = b_w_in[:, 2048 + c * 256:2048 + (c + 1) * 256]
        maps.append({
            "xnT": xb,
            "wA": wl(np.concatenate([kc, qc], axis=1)),
            "wB": wl(np.concatenate([vc, zc], axis=1)),
            "gcol": gcol, "gqk": gqk, "gsub": gsub, "lamp": lamp,
        })
    return maps


def build_l3(ntok=2048):
    ntt = ntok // 128
    nc = bass.Bass("TRN2", target_bir_lowering=False)
    oT = nc.dram_tensor("oT", [ntt, 128, 16, 128], BF16, kind="ExternalInput").ap()
    x1 = nc.dram_tensor("x1", [ntok, D], F32, kind="ExternalInput").ap()
    wo = nc.dram_tensor("wo", [128, 16, D], F32, kind="ExternalInput").ap()
    y = nc.dram_tensor("y", [ntok, D], F32, kind="ExternalOutput").ap()
    P = Prog(nc)
    with contextlib.ExitStack() as es:
        def sb(name, shape, dt):
            return es.enter_context(nc.sbuf_tensor(name, shape, dt))
        wob = sb("wob", [128, 16, D], BF16)
        ot = [sb("ot%d" % i, [128, 16, 128], BF16) for i in range(2)]
        xt = [sb("xt%d" % i, [128, D], F32) for i in range(2)]
        yt = [sb("yt%d" % i, [128, D], F32) for i in range(2)]
        PS = [es.enter_context(nc.psum_tensor("ps%d" % i, [128, 2048], F32)) for i in range(2)]
        T_w = [T() for _ in range(16)]
        T_ot, T_xt, T_yt, T_ps = [T(), T()], [T(), T()], [T(), T()], [T(), T()]
        T_y = T()
        wstg = [sb("wstg%d" % i, [128, D], F32) for i in range(3)]
        T_wstg = [T(), T(), T()]
        for ct in range(16):
            i = ct % 3
            P.op("sync", lambda e, ct=ct, i=i: e.dma_start(out=wstg[i][:], in_=wo[:, ct, :]), writes=[T_wstg[i]], dma="w%d" % i)
            if ct % 2 == 0:
                P.op("vector", lambda e, ct=ct, i=i: e.tensor_copy(out=wob[:, ct, :], in_=wstg[i][:]), reads=[T_wstg[i]], writes=[T_w[ct]])
            else:
                P.op("scalar", lambda e, ct=ct, i=i: e.copy(out=wob[:, ct, :], in_=wstg[i][:]), reads=[T_wstg[i]], writes=[T_w[ct]])

        def loads(tt):
            b = tt % 2
            P.op("sync", lambda e: e.dma_start(out=ot[b][:], in_=oT[tt]), writes=[T_ot[b]], dma="ot%d" % b)
            P.op("sync", lambda e: e.dma_start(out=xt[b][:], in_=x1[tt * 128:(tt + 1) * 128, :]), writes=[T_xt[b]], dma="xt%d" % b)

        def tile(tt):
            b = tt % 2
            if tt + 1 < ntt:
                loads(tt + 1)
            for fb in range(4):
                for ct in range(16):
                    P.op("tensor", lambda e, fb=fb, ct=ct: e.matmul(
                        PS[b][:, fb * 512:(fb + 1) * 512], lhsT=ot[b][:, ct, :], rhs=wob[:, ct, fb * 512:(fb + 1) * 512],
                        start=(ct == 0), stop=(ct == 15)),
                        reads=[T_ot[b], T_w[ct]], writes=[T_ps[b]])
            P.op("vector", lambda e: e.tensor_tensor(out=yt[b][:], in0=PS[b][:], in1=xt[b][:], op=ALU.add),
                 reads=[T_ps[b], T_xt[b]], writes=[T_yt[b]])
            P.op("gpsimd", lambda e: e.dma_start(out=y[tt * 128:(tt + 1) * 128, :], in_=yt[b][:]),
                 reads=[T_yt[b]], writes=[T_y], dma="yt%d" % b)

        loads(0)
        for tt in range(ntt):
            tile(tt)
        P.emit()
    return nc


def build_l1(ntok=2048, NP=1024):
    npass = ntok // NP
    NB = NP // 512
    NL = NP + HALO
    nc = bass.Bass("TRN2", target_bir_lowering=False)
    x = nc.dram_tensor("x", [ntok + HALO, D], F32, kind="ExternalInput").ap()
    w_in = nc.dram_tensor("w_in", [48, 128, 16, 128], F32, kind="ExternalInput").ap()
    w_out = nc.dram_tensor("w_out", [128, 16, D], F32, kind="ExternalInput").ap()
    gbc_d = nc.dram_tensor("gbc", [128, D], F32, kind="ExternalInput").ap()
    dww_d = nc.dram_tensor("dww", [128, 16, CONV_W], F32, kind="ExternalInput").ap()
    cvec_d = nc.dram_tensor("cvec", [128, 48], F32, kind="ExternalInput").ap()
    identb_d = nc.dram_tensor("identb", [128, 128], BF16, kind="ExternalInput").ap()
    identf_d = nc.dram_tensor("identf", [128, 128], F32, kind="ExternalInput").ap()
    x1_o = nc.dram_tensor("x1", [ntok, D], F32, kind="ExternalOutput").ap()
    xnT_o = nc.dram_tensor("xnT", [ntok // 256, 128, 16, 256], BF16, kind="ExternalOutput").ap()
    wi_b = nc.dram_tensor("wi_b", [48, 128, 2048], BF16).ap()
    wo_b = nc.dram_tensor("wo_b", [16, 128, 2048], BF16).ap()

    P = Prog(nc)
    with contextlib.ExitStack() as es:
        def sb(name, shape, dt):
            return es.enter_context(nc.sbuf_tensor(name, shape, dt))

        HT_N = 16 * NL
        ARENA = max(HT_N + 6 * 2048 + 2 * CONV_W * 128, 16 * D)
        arena = sb("arena", [128, ARENA], BF16)
        hT = arena[:, 0:HT_N].rearrange("p (f t) -> p f t", f=16)
        wt = [arena[:, HT_N + i * 2048: HT_N + (i + 1) * 2048].rearrange("p (f n) -> p f n", f=16) for i in range(6)]
        wa, wb_, wz = wt[0:2], wt[2:4], wt[4:6]
        dgo = HT_N + 6 * 2048
        dg = [arena[:, dgo + i * CONV_W * 128: dgo + (i + 1) * CONV_W * 128].rearrange("p (j n) -> p j n", j=CONV_W) for i in range(2)]
        wob = arena[:, 0:16 * D].rearrange("p (c n) -> p c n", c=16)
        cvT = sb("cvT", [128, 16, NP], BF16)
        yT = [sb("yT%d" % i, [128, NL], BF16) for i in range(2)]
        acc_s = sb("acc_s", [128, NP], F32)
        acc_q = sb("acc_q", [128, NP], F32)
        rstd_bc = sb("rstd_bc", [128, NP], F32)
        nmr_bc = sb("nmr_bc", [128, NP], F32)
        xt = [sb("xt%d" % i, [128, D], F32) for i in range(2)]
        sqx = sb("sqx", [128, D], F32)
        tmp = {k: [sb("%s%d" % (k, i), [128, 512], F32) for i in range(2)] for k in ("ta", "tb", "tc", "td", "te", "tf")}
        xnblk = sb("xnblk", [128, 16, 256], BF16)
        gbc = sb("gbc_s", [128, D], F32)
        dww = sb("dww_s", [128, 16, CONV_W], F32)
        cvec = sb("cvec_s", [128, 48], F32)
        identb = sb("identb_s", [128, 128], BF16)
        identf = sb("identf_s", [128, 128], F32)
        onesf = sb("onesf", [128, 128], F32)
        small = sb("small", [128, 8], F32)
        B = [es.enter_context(nc.psum_tensor("b%d" % i, [128, 512], F32)) for i in range(8)]

        Tb = [T("bank%d" % i) for i in range(8)]
        T_c = T()
        T_hT = T()
        T_wa, T_wb, T_wz, T_dg = [T(), T()], [T(), T()], [T(), T()], [T(), T()]
        T_cv = [T() for _ in range(16)]
        T_yT = [T(), T()]
        T_acc, T_st = T(), T()
        T_xt = [T(), T()]
        T_sqx = T()
        T_tmp = {k: [T(), T()] for k in tmp}
        T_xnblk = T()
        T_sm = T()
        T_x1o, T_xno = T(), T()
        T_wob = T()

        dwb = lambda c: cvec[:, c:c + 1]
        lng = lambda c: cvec[:, 16 + c:17 + c]
        lnb = lambda c: cvec[:, 32 + c:33 + c]

        for (dst, src) in ((gbc, gbc_d), (dww, dww_d), (cvec, cvec_d), (identb, identb_d), (identf, identf_d)):
            P.op("sync", lambda e, dst=dst, src=src: e.dma_start(out=dst[:], in_=src), writes=[T_c], dma="c0")
        P.op("gpsimd", lambda e: e.memset(onesf[:], 1.0), writes=[T_c])

        T_wib = [T() for _ in range(48)]
        T_wobd = [T() for _ in range(16)]
        stg_in = [xt[0][:], xt[1][:], sqx[:]]
        T_stg_in = [[T_xt[0]], [T_xt[1]], [T_sqx]]
        xnflat = xnblk[:].rearrange("p f t -> p (f t)")
        stg_out = [xnflat[:, i * 2048:(i + 1) * 2048] for i in range(2)]
        T_stg_out = [T(), T()]
        pc = {"n": 0}

        def wsrc(k):
            if k < 48:
                return w_in[k].rearrange("p f n -> p (f n)"), wi_b[k], T_wib[k]
            return w_out[:, k - 48, :], wo_b[k - 48], T_wobd[k - 48]

        def precast(ks):
            slots = []
            for k in ks:
                n = pc["n"]; pc["n"] += 1
                i, o = n % 3, n % 2
                src = wsrc(k)[0]
                P.op("sync", lambda e, i=i, src=src: e.dma_start(out=stg_in[i], in_=src), writes=T_stg_in[i], dma="wl%d" % i)
                slots.append((k, n, i, o))
            for (k, n, i, o) in slots:
                _, dst, Tdst = wsrc(k)
                if n % 2 == 0:
                    P.op("vector", lambda e, i=i, o=o: e.tensor_copy(out=stg_out[o], in_=stg_in[i]), reads=T_stg_in[i],
                         writes=[T_stg_out[o], T_xnblk])
                else:
                    P.op("scalar", lambda e, i=i, o=o: e.copy(out=stg_out[o], in_=stg_in[i]), reads=T_stg_in[i],
                         writes=[T_stg_out[o], T_xnblk])
                P.op("sync", lambda e, o=o, dst=dst: e.dma_start(out=dst, in_=stg_out[o]), reads=[T_stg_out[o], T_xnblk], writes=[Tdst],
                     dma="ws%d" % o)

        late = list(range(32, 64))

        cnt = {"xt": 0}

        def rstd_from_ss(np_, col_in, col_out, n):
            P.op("scalar", lambda e: e.activation(out=small[0:np_, col_out:col_out + 1], in_=small[0:np_, col_in:col_in + 1],
                                                  func=ACT.Ln, scale=1.0 / n, bias=EPS), reads=[T_sm], writes=[T_sm])
            P.op("scalar", lambda e: e.activation(out=small[0:np_, col_out:col_out + 1], in_=small[0:np_, col_out:col_out + 1],
                                                  func=ACT.Exp, scale=-0.5), reads=[T_sm], writes=[T_sm])

        junk = cvT[:].rearrange("p c t -> p (c t)")[:, 0:2 * D].bitcast(F32)
        T_junk = T_cv[0:(2 * D + NP - 1) // NP]
        T_smp = [T(), T()]

        def norm_tile(row0, np_, col0):
            b = cnt["xt"] % 2
            cnt["xt"] += 1
            c0_, c1_ = 4 + 2 * b, 5 + 2 * b
            Ts = T_smp[b]
            P.op("sync", lambda e: e.dma_start(out=xt[b][0:np_, :], in_=x[row0:row0 + np_, :]), writes=[T_xt[b]], dma="xt%d" % b)
            P.op("scalar", lambda e: e.activation(out=junk[0:np_, :], in_=xt[b][0:np_, :], func=ACT.Square),
                 reads=[T_xt[b]], writes=T_junk)
            P.op("vector", lambda e: e.reduce_sum(out=small[0:np_, c0_:c0_ + 1], in_=junk[0:np_, :], axis=AX.X),
                 reads=T_junk, writes=[Ts])
            P.op("scalar", lambda e: e.activation(out=small[0:np_, c1_:c1_ + 1], in_=small[0:np_, c0_:c0_ + 1],
                                                  func=ACT.Ln, scale=1.0 / D, bias=EPS), reads=[Ts], writes=[Ts])
            P.op("scalar", lambda e: e.activation(out=small[0:np_, c1_:c1_ + 1], in_=small[0:np_, c1_:c1_ + 1],
                                                  func=ACT.Exp, scale=-0.5), reads=[Ts], writes=[Ts])
            P.op("scalar", lambda e: e.activation(out=sqx[0:np_, :], in_=xt[b][0:np_, :], func=ACT.Copy, scale=small[0:np_, c1_:c1_ + 1]),
                 reads=[T_xt[b], Ts], writes=[T_sqx])
            for ft in range(16):
                P.op("tensor", lambda e, ft=ft: e.transpose(
                    B[ft // 4][:, (ft % 4) * 128:(ft % 4) * 128 + np_], sqx[0:np_, ft * 128:(ft + 1) * 128], identf[0:np_, 0:np_]),
                    reads=[T_sqx, T_c], writes=[Tb[ft // 4]])
            for q in range(4):
                P.op("vector", lambda e, q=q: e.tensor_tensor(
                    out=hT[:, q * 4:(q + 1) * 4, col0:col0 + np_],
                    in0=B[q][:].rearrange("p (f t) -> p f t", f=4)[:, :, 0:np_],
                    in1=gbc[:, q * 512:(q + 1) * 512].rearrange("p (f t) -> p f t", f=4)[:, :, 0:np_], op=ALU.mult),
                    reads=[Tb[q], T_c], writes=[T_hT])

        def do_pass(h):
            r_h = h * NP
            r_o = HALO + h * NP
            norm_tile(r_h, HALO, 0)
            for t in range(NP // 128):
                norm_tile(r_o + t * 128, 128, HALO + t * 128)
            P.op("gpsimd", lambda e: e.memset(acc_s[:], 0.0), writes=[T_acc])
            P.op("gpsimd", lambda e: e.memset(acc_q[:], 0.0), writes=[T_acc])

            def prefetch1(c):
                b = c % 2
                P.op("gpsimd", lambda e: e.dma_start(out=wa[b], in_=wi_b[c].rearrange("p (f n) -> p f n", f=16)),
                     reads=[T_wib[c]], writes=[T_wa[b]], dma="wa%d" % b)
                P.op("gpsimd", lambda e: e.dma_start(out=wb_[b], in_=wi_b[16 + c].rearrange("p (f n) -> p f n", f=16)),
                     reads=[T_wib[16 + c]], writes=[T_wb[b]], dma="wb%d" % b)
                for j in range(CONV_W):
                    P.op("vector", lambda e, j=j: e.tensor_scalar_mul(out=dg[b][:, j, :], in0=identb[:], scalar1=dww[:, c, j:j + 1]),
                         reads=[T_c], writes=[T_dg[b]])

            def stage2(c):
                b = c % 2
                yt_ = yT[b]
                for (w_, T_w, off) in ((wa[b], T_wa[b], 0), (wb_[b], T_wb[b], 32)):
                    for ft in range(16):
                        P.op("tensor", lambda e, w_=w_, off=off, ft=ft: e.matmul(
                            B[6][:, off:off + HALO], lhsT=w_[:, ft, :], rhs=hT[:, ft, 0:HALO], start=(ft == 0), stop=(ft == 15)),
                            reads=[T_w, T_hT], writes=[Tb[6]])
                P.op("scalar", lambda e: e.activation(out=tmp["ta"][0][:, 0:HALO], in_=B[6][:, 32:32 + HALO], func=ACT.Sigmoid),
                     reads=[Tb[6]], writes=[T_tmp["ta"][0]])
                P.op("vector", lambda e: e.tensor_tensor(out=yt_[:, 0:HALO], in0=B[6][:, 0:HALO], in1=tmp["ta"][0][:, 0:HALO], op=ALU.mult),
                     reads=[Tb[6], T_tmp["ta"][0]], writes=[T_yT[b]])
                for tb in range(NB):
                    pa, pb = B[2 * (tb % 2)], B[2 * (tb % 2) + 1]
                    Ta, Tbb = Tb[2 * (tb % 2)], Tb[2 * (tb % 2) + 1]
                    c0 = HALO + tb * 512
                    for (w_, T_w, ps, Tp) in ((wa[b], T_wa[b], pa, Ta), (wb_[b], T_wb[b], pb, Tbb)):
                        for ft in range(16):
                            P.op("tensor", lambda e, w_=w_, ps=ps, ft=ft, c0=c0: e.matmul(
                                ps[:], lhsT=w_[:, ft, :], rhs=hT[:, ft, c0:c0 + 512], start=(ft == 0), stop=(ft == 15)),
                                reads=[T_w, T_hT], writes=[Tp])
                    sg = tmp["ta"][tb % 2]
                    P.op("scalar", lambda e, pb=pb, sg=sg: e.activation(out=sg[:], in_=pb[:], func=ACT.Sigmoid),
                         reads=[Tbb], writes=[T_tmp["ta"][tb % 2]])
                    P.op("vector", lambda e, pa=pa, sg=sg, c0=c0: e.tensor_tensor(out=yt_[:, c0:c0 + 512], in0=pa[:], in1=sg[:], op=ALU.mult),
                         reads=[Ta, T_tmp["ta"][tb % 2]], writes=[T_yT[b]])
                for tb in range(NB):
                    pc, Tc_ = B[4 + tb % 2], Tb[4 + tb % 2]
                    for j in range(CONV_W):
                        P.op("tensor", lambda e, pc=pc, j=j, tb=tb: e.matmul(
                            pc[:], lhsT=dg[b][:, j, :], rhs=yt_[:, tb * 512 + j: tb * 512 + j + 512],
                            start=(j == 0), stop=(j == CONV_W - 1)),
                            reads=[T_dg[b], T_yT[b]], writes=[Tc_])
                    blk = slice(tb * 512, (tb + 1) * 512)
                    sq_ = tmp["tb"][tb % 2]
                    P.op("scalar", lambda e, pc=pc, blk=blk: e.activation(out=cvT[:, c, blk], in_=pc[:], func=ACT.Identity, bias=dwb(c)),
                         reads=[Tc_, T_c], writes=[T_cv[c]])
                    P.op("scalar", lambda e, pc=pc, sq_=sq_: e.activation(out=sq_[:], in_=pc[:], func=ACT.Square, bias=dwb(c)),
                         reads=[Tc_, T_c], writes=[T_tmp["tb"][tb % 2]])
                    P.op("vector", lambda e, blk=blk: e.tensor_tensor(out=acc_s[:, blk], in0=acc_s[:, blk], in1=cvT[:, c, blk], op=ALU.add),
                         reads=[T_cv[c], T_acc], writes=[T_acc])
                    P.op("vector", lambda e, blk=blk, sq_=sq_: e.tensor_tensor(out=acc_q[:, blk], in0=acc_q[:, blk], in1=sq_[:], op=ALU.add),
                         reads=[T_tmp["tb"][tb % 2], T_acc], writes=[T_acc])

            if h == 0:
                precast([0, 16]); precast([1, 17])
            prefetch1(0)
            for c in range(16):
                if h == 0:
                    if c + 2 < 16:
                        precast([c + 2, 16 + c + 2])
                    precast([late[2 * c], late[2 * c + 1]])
                if c + 1 < 16:
                    prefetch1(c + 1)
                stage2(c)

            def prefetch_z(c):
                b = c % 2
                P.op("gpsimd", lambda e: e.dma_start(out=wz[b], in_=wi_b[32 + c].rearrange("p (f n) -> p f n", f=16)),
                     reads=[T_wib[32 + c]], writes=[T_wz[b]], dma="wz%d" % b)
            prefetch_z(0)
            for tb in range(NB):
                blk = slice(tb * 512, (tb + 1) * 512)
                P.op("tensor", lambda e, blk=blk: e.matmul(B[0][:], lhsT=onesf[:], rhs=acc_s[:, blk], start=True, stop=True),
                     reads=[T_c, T_acc], writes=[Tb[0]])
                P.op("tensor", lambda e, blk=blk: e.matmul(B[1][:], lhsT=onesf[:], rhs=acc_q[:, blk], start=True, stop=True),
                     reads=[T_c, T_acc], writes=[Tb[1]])
                mean, m2 = tmp["tc"][0], tmp["td"][0]
                P.op("vector", lambda e: e.tensor_scalar_mul(out=mean[:], in0=B[0][:], scalar1=1.0 / D),
                     reads=[Tb[0]], writes=[T_tmp["tc"][0]])
                P.op("vector", lambda e: e.tensor_tensor(out=m2[:], in0=mean[:], in1=mean[:], op=ALU.mult),
                     reads=[T_tmp["tc"][0]], writes=[T_tmp["td"][0]])
                P.op("vector", lambda e: e.scalar_tensor_tensor(out=m2[:], in0=B[1][:], scalar=1.0 / D, in1=m2[:],
                                                                op0=ALU.mult, op1=ALU.subtract),
                     reads=[Tb[1], T_tmp["td"][0]], writes=[T_tmp["td"][0]])
                P.op("scalar", lambda e, blk=blk: e.activation(out=rstd_bc[:, blk], in_=m2[:], func=ACT.Ln, bias=EPS),
                     reads=[T_tmp["td"][0]], writes=[T_st])
                P.op("scalar", lambda e, blk=blk: e.activation(out=rstd_bc[:, blk], in_=rstd_bc[:, blk], func=ACT.Exp, scale=-0.5),
                     reads=[T_st], writes=[T_st])
                P.op("vector", lambda e, blk=blk: e.scalar_tensor_tensor(out=nmr_bc[:, blk], in0=mean[:], scalar=-1.0, in1=rstd_bc[:, blk],
                                                                         op0=ALU.mult, op1=ALU.mult),
                     reads=[T_tmp["tc"][0], T_st], writes=[T_st])

            def stage4(c):
                b = c % 2
                for tb in range(NB):
                    k = tb % 2
                    pz, Tz = B[2 + k], Tb[2 + k]
                    c0 = HALO + tb * 512
                    blk = slice(tb * 512, (tb + 1) * 512)
                    for ft in range(16):
                        P.op("tensor", lambda e, pz=pz, ft=ft, c0=c0: e.matmul(
                            pz[:], lhsT=wz[b][:, ft, :], rhs=hT[:, ft, c0:c0 + 512], start=(ft == 0), stop=(ft == 15)),
                            reads=[T_wz[b], T_hT], writes=[Tz])
                    sz, gz, t1, s2, l_, u_ = (tmp[n][k] for n in ("ta", "tb", "tc", "td", "te", "tf"))
                    Ts = {n: T_tmp[n][k] for n in ("ta", "tb", "tc", "td", "te", "tf")}
                    P.op("scalar", lambda e, pz=pz, sz=sz: e.activation(out=sz[:], in_=pz[:], func=ACT.Sigmoid),
                         reads=[Tz], writes=[Ts["ta"]])
                    P.op("vector", lambda e, pz=pz, sz=sz, gz=gz: e.tensor_tensor(out=gz[:], in0=pz[:], in1=sz[:], op=ALU.mult),
                         reads=[Tz, Ts["ta"]], writes=[Ts["tb"]])
                    P.op("vector", lambda e, t1=t1, blk=blk: e.tensor_tensor(out=t1[:], in0=cvT[:, c, blk], in1=rstd_bc[:, blk], op=ALU.mult),
                         reads=[T_cv[c], T_st], writes=[Ts["tc"]])
                    P.op("vector", lambda e, t1=t1, blk=blk: e.tensor_tensor(out=t1[:], in0=t1[:], in1=nmr_bc[:, blk], op=ALU.add),
                         reads=[Ts["tc"], T_st], writes=[Ts["tc"]])
                    P.op("scalar", lambda e, t1=t1, s2=s2: e.activation(out=s2[:], in_=t1[:], func=ACT.Sigmoid, scale=lng(c), bias=lnb(c)),
                         reads=[Ts["tc"], T_c], writes=[Ts["td"]])
                    P.op("vector", lambda e, t1=t1, l_=l_: e.tensor_scalar(out=l_[:], in0=t1[:], scalar1=lng(c), scalar2=lnb(c),
                                                                          op0=ALU.mult, op1=ALU.add),
                         reads=[Ts["tc"], T_c], writes=[Ts["te"]])
                    P.op("vector", lambda e, l_=l_, s2=s2, u_=u_: e.tensor_tensor(out=u_[:], in0=l_[:], in1=s2[:], op=ALU.mult),
                         reads=[Ts["te"], Ts["td"]], writes=[Ts["tf"]])
                    P.op("vector", lambda e, u_=u_, gz=gz, blk=blk: e.tensor_tensor(out=cvT[:, c, blk], in0=u_[:], in1=gz[:], op=ALU.mult),
                         reads=[Ts["tf"], Ts["tb"]], writes=[T_cv[c]])

            for c in range(16):
                if c + 1 < 16:
                    prefetch_z(c + 1)
                stage4(c)

            alias = [T_hT] + T_wa + T_wb + T_wz + T_dg
            for ct in range(16):
                P.op("sync", lambda e, ct=ct: e.dma_start(out=wob[:, ct, :], in_=wo_b[ct]), reads=[T_wobd[ct]], writes=[T_wob] + alias,
                     dma="wo%d" % (ct % 4))

            xb_of = {}

            def s5_mm(t):
                b = cnt["xt"] % 2
                cnt["xt"] += 1
                xb_of[t] = b
                bs = 4 * (t % 2)
                row = h * NP + t * 128
                P.op("sync", lambda e: e.dma_start(out=xt[b][:], in_=x[HALO + row:HALO + row + 128, :]), writes=[T_xt[b]], dma="xt%d" % b)
                for fb in range(4):
                    for ct in range(16):
                        P.op("tensor", lambda e, fb=fb, ct=ct: e.matmul(
                            B[bs + fb][:], lhsT=cvT[:, ct, t * 128:(t + 1) * 128], rhs=wob[:, ct, fb * 512:(fb + 1) * 512],
                            start=(ct == 0), stop=(ct == 15)),
                            reads=[T_cv[ct], T_wob], writes=[Tb[bs + fb]])

            def s5_rest(t):
                b = xb_of[t]
                bs = 4 * (t % 2)
                row = h * NP + t * 128
                for fb in range(4):
                    P.op("vector", lambda e, fb=fb: e.tensor_tensor(out=xt[b][:, fb * 512:(fb + 1) * 512], in0=B[bs + fb][:],
                                                                    in1=xt[b][:, fb * 512:(fb + 1) * 512], op=ALU.add),
                         reads=[Tb[bs + fb], T_xt[b]], writes=[T_xt[b]])
                P.op("gpsimd", lambda e: e.dma_start(out=x1_o[row:row + 128, :], in_=xt[b][:]), reads=[T_xt[b]], writes=[T_x1o], dma="x1o%d" % b)
                P.op("scalar", lambda e: e.activation(out=sqx[:], in_=xt[b][:], func=ACT.Square), reads=[T_xt[b]], writes=[T_sqx])
                P.op("vector", lambda e: e.reduce_sum(out=small[:, 0:1], in_=sqx[:], axis=AX.X), reads=[T_sqx], writes=[T_sm])
                rstd_from_ss(128, 0, 1, float(D))
                P.op("scalar", lambda e: e.activation(out=sqx[:], in_=xt[b][:], func=ACT.Copy, scale=small[:, 1:2]),
                     reads=[T_xt[b], T_sm], writes=[T_sqx])
                for ft in range(16):
                    P.op("tensor", lambda e, ft=ft: e.transpose(
                        B[bs + ft // 4][:, (ft % 4) * 128:(ft % 4 + 1) * 128], sqx[:, ft * 128:(ft + 1) * 128], identf[:]),
                        reads=[T_sqx, T_c], writes=[Tb[bs + ft // 4]])
                half = t % 2
                for q in range(4):
                    P.op("scalar", lambda e, q=q: e.copy(out=xnblk[:, q * 4:(q + 1) * 4, half * 128:(half + 1) * 128],
                                                         in_=B[bs + q][:].rearrange("p (f t) -> p f t", f=4)),
                         reads=[Tb[bs + q]], writes=[T_xnblk])
                if half == 1:
                    blk_i = (h * NP + t * 128) // 256
                    P.op("gpsimd", lambda e: e.dma_start(out=xnT_o[blk_i], in_=xnblk[:]), reads=[T_xnblk], writes=[T_xno], dma="xno")

            n5 = NP // 128
            s5_mm(0)
            for t in range(n5):
                if t + 1 < n5:
                    s5_mm(t + 1)
                s5_rest(t)

        for h in range(npass):
            do_pass(h)
        P.emit()
    return nc


def l1_inputs(x, a_norm_g, a_w_in, a_dw_w, a_dw_b, a_ln_g, a_ln_b, a_w_out, ntok=2048, ncores=NCORES):
    w_in_l = np.ascontiguousarray(a_w_in.reshape(16, 128, 48, 128).transpose(2, 1, 0, 3))
    w_out_l = np.ascontiguousarray(a_w_out.reshape(16, 128, D).transpose(1, 0, 2))
    gbc = np.ascontiguousarray(np.broadcast_to(a_norm_g.reshape(16, 128).T[:, :, None], (128, 16, 128)).reshape(128, D)).astype(np.float32)
    dww = np.ascontiguousarray(a_dw_w.reshape(CONV_W, 16, 128).transpose(2, 1, 0)).astype(np.float32)
    cvec = np.ascontiguousarray(np.concatenate(
        [a_dw_b.reshape(16, 128).T, a_ln_g.reshape(16, 128).T, a_ln_b.reshape(16, 128).T], axis=1)).astype(np.float32)
    identf = np.eye(128, dtype=np.float32)
    identb = identf.astype(ml_dtypes.bfloat16)
    xp = np.concatenate([np.zeros((HALO, D), np.float32), x], axis=0)
    maps = []
    for c in range(ncores):
        maps.append({"x": np.ascontiguousarray(xp[c * ntok: c * ntok + ntok + HALO]), "w_in": w_in_l, "w_out": w_out_l,
                     "gbc": gbc, "dww": dww, "cvec": cvec, "identb": identb, "identf": identf})
    return maps


_CACHE = {}


def _get(name, fn):
    if name not in _CACHE:
        _CACHE[name] = fn()
    return _CACHE[name]


def kernel(x, a_norm_g, a_w_in, a_dw_w, a_dw_b, a_ln_g, a_ln_b, a_w_out,
           kv_norm_g, w_kv, k_norm_g, b_norm_g, b_w_in, b_q_norm_g,
           b_lambda, b_subln_g, b_w_out):
    f = lambda a: np.asarray(a, dtype=np.float32)
    x2 = f(x).reshape(S, D)
    cores = list(range(NCORES))
    ntok = S // NCORES
    nc1 = _get("l1", lambda: build_l1(ntok, 1024))
    m1 = l1_inputs(x2, f(a_norm_g)[0], f(a_w_in)[0], f(a_dw_w)[0], f(a_dw_b)[0], f(a_ln_g)[0], f(a_ln_b)[0], f(a_w_out)[0],
                   ntok=ntok, ncores=NCORES)
    r1 = run_bass_kernel_spmd(nc1, m1, core_ids=cores).results
    x1 = [r1[c]["x1"] for c in cores]
    xnT_blocks = np.concatenate([r1[c]["xnT"] for c in cores], axis=0)
    nc2 = _get("l2", lambda: build_l2(S))
    xnT_full = np.ascontiguousarray(xnT_blocks.transpose(2, 1, 0, 3)).reshape(16, 128, S)
    m2 = l2_inputs(xnT_full, f(w_kv), f(b_w_in)[0], f(kv_norm_g), f(b_norm_g)[0], f(k_norm_g), f(b_q_norm_g)[0],
                   f(b_lambda)[0], f(b_subln_g)[0], s_len=S)
    r2 = run_bass_kernel_spmd(nc2, m2, core_ids=cores).results
    o_full = np.concatenate([r2[c]["o"] for c in cores], axis=1)
    nc3 = _get("l3", lambda: build_l3(ntok))
    wo = np.ascontiguousarray(f(b_w_out)[0].reshape(16, 128, D).transpose(1, 0, 2))
    m3 = []
    for c in cores:
        oc = o_full[c * ntok:(c + 1) * ntok]
        oT = np.ascontiguousarray(oc.reshape(ntok // 128, 128, 16, 128).transpose(0, 3, 2, 1))
        m3.append({"oT": oT, "x1": x1[c], "wo": wo})
    r3 = run_bass_kernel_spmd(nc3, m3, core_ids=cores).results
    out = np.concatenate([r3[c]["y"] for c in cores], axis=0).reshape(1, S, D).astype(np.float32)
    return out
```

```python
import contextlib
import math
import numpy as np
import ml_dtypes
import concourse.bass as bass
import concourse.mybir as mybir
from concourse.bass_utils import run_bass_kernel_spmd

F32 = mybir.dt.float32
BF16 = mybir.dt.bfloat16
ALU = mybir.AluOpType
ACT = mybir.ActivationFunctionType
AX = mybir.AxisListType

NCORES = 8
D = 2048
S = 16384
EPS = 1e-6
CONV_W = 31
HALO = CONV_W - 1
DH = 128
LAM_INIT = 0.8 - 0.6 * math.exp(-0.3 * (2 - 1))

ENGS = ("tensor", "vector", "scalar", "gpsimd", "sync")


class T:
    __slots__ = ("name", "last_w", "readers")

    def __init__(self, name=""):
        self.name = name
        self.last_w = None
        self.readers = []


class Prog:
    def __init__(self, nc):
        self.nc = nc
        self.ops = []
        self.groups = {}

    def op(self, eng, fn, reads=(), writes=(), dma=None, inc=16):
        i = len(self.ops)
        deps = set()
        for t in reads:
            if t.last_w is not None:
                deps.add(t.last_w)
        for t in writes:
            if t.last_w is not None:
                deps.add(t.last_w)
            deps.update(t.readers)
        deps.discard(i)
        for t in reads:
            t.readers.append(i)
        for t in writes:
            t.last_w = i
            t.readers = []
        gneed = {}
        for jd in deps:
            g = self.ops[jd]["dma"]
            if g is not None:
                gneed[g] = self.groups[g]
        o = dict(i=i, eng=eng, fn=fn, deps=deps, dma=dma, sig=False, sidx=0, inc=inc, gneed=gneed)
        if dma is not None:
            self.groups[dma] = self.groups.get(dma, 0) + inc
            o["sidx"] = self.groups[dma]
        self.ops.append(o)
        return i

    def emit(self, final_wait_engine="sync"):
        nc = self.nc
        ops = self.ops
        for o in ops:
            for j in o["deps"]:
                d = ops[j]
                if d["dma"] is not None:
                    continue
                if d["eng"] == "tensor" and o["eng"] == "tensor" and o["dma"] is None:
                    continue
                d["sig"] = True
        cnt = {e: 0 for e in ENGS}
        for o in ops:
            if o["dma"] is None and o["sig"]:
                cnt[o["eng"]] += 1
                o["sidx"] = cnt[o["eng"]]
        with contextlib.ExitStack() as es:
            esem = {e: es.enter_context(nc.semaphore("p_" + e)) for e in ENGS}
            gsem = {g: es.enter_context(nc.semaphore("g_%d" % k))
                    for k, g in enumerate(self.groups)}
            block = es.enter_context(nc.Block())
            final = dict(self.groups)

            def make(ename):
                def body(eng):
                    waited_e = {e: 0 for e in ENGS}
                    waited_g = {g: 0 for g in self.groups}
                    for o in ops:
                        if o["eng"] != ename:
                            continue
                        need_e = {}
                        need_g = o["gneed"]
                        for j in o["deps"]:
                            d = ops[j]
                            if d["dma"] is not None:
                                continue
                            else:
                                if d["eng"] == "tensor" and ename == "tensor" and o["dma"] is None:
                                    continue
                                need_e[d["eng"]] = max(need_e.get(d["eng"], 0), d["sidx"])
                        for e, v in need_e.items():
                            if v > waited_e[e]:
                                eng.wait_ge(esem[e], v)
                                waited_e[e] = v
                        for g, v in need_g.items():
                            if v > waited_g[g]:
                                eng.wait_ge(gsem[g], v)
                                waited_g[g] = v
                        ins = o["fn"](eng)
                        if o["dma"] is not None:
                            ins.then_inc(gsem[o["dma"]], o["inc"])
                        elif o["sig"]:
                            ins.then_inc(esem[ename], 1)
                    if ename == final_wait_engine:
                        for g, v in final.items():
                            if v > waited_g[g]:
                                eng.wait_ge(gsem[g], v)
                        for e in ENGS:
                            if e != ename and cnt[e] > waited_e[e]:
                                eng.wait_ge(esem[e], cnt[e])
                return body

            block.tensor(make("tensor"))
            block.vector(make("vector"))
            block.scalar(make("scalar"))
            block.gpsimd(make("gpsimd"))
            block.sync(make("sync"))


def build_l2(s_len=S, dbg=False):
    QB = 256
    nblk = s_len // QB
    ntile = s_len // 128
    nc = bass.Bass("TRN2", target_bir_lowering=False)
    xnT = nc.dram_tensor("xnT", [nblk, 128, 16, QB], BF16, kind="ExternalInput").ap()
    wA = nc.dram_tensor("wA", [128, 16, 512], F32, kind="ExternalInput").ap()
    wB = nc.dram_tensor("wB", [128, 16, 512], F32, kind="ExternalInput").ap()
    gcol = nc.dram_tensor("gcol", [128, 32], F32, kind="ExternalInput").ap()
    gqk = nc.dram_tensor("gqk", [128, 2], F32, kind="ExternalInput").ap()
    gsub = nc.dram_tensor("gsub", [128, 256], F32, kind="ExternalInput").ap()
    lamp = nc.dram_tensor("lamp", [128, 512], F32, kind="ExternalInput").ap()
    o_out = nc.dram_tensor("o", [s_len, 256], BF16, kind="ExternalOutput").ap()

    P = Prog(nc)
    with contextlib.ExitStack() as es:
        def sb(name, shape, dt):
            return es.enter_context(nc.sbuf_tensor(name, shape, dt))

        KT = sb("KT", [128, 2, s_len], BF16)
        V = sb("V", [128, ntile, 257], BF16)
        wAb = sb("wAb", [128, 16, 512], BF16)
        wBb = sb("wBb", [128, 16, 512], BF16)
        xb = [sb("xb%d" % i, [128, 16, QB], BF16) for i in range(2)]
        QTb = [sb("QTb%d" % i, [128, 2, QB], BF16) for i in range(2)]
        PT = [sb("PT%d" % i, [128, 4, QB], BF16) for i in range(3)]
        sq = sb("sq", [128, 1024], BF16)
        rst = sb("rst", [128, 1024], F32)
        th = [sb("th%d" % i, [128, 256], F32) for i in range(2)]
        Zg = [sb("Zg%d" % i, [128, 256], BF16) for i in range(4)]
        ea = [sb("ea%d" % i, [128, 256], F32) for i in range(2)]
        eo = [sb("eo%d" % i, [128, 256], F32) for i in range(2)]
        esq = ea
        eof = ea
        ob = [sb("ob%d" % i, [128, 256], BF16) for i in range(4)]
        small = [sb("small%d" % i, [128, 8], F32) for i in range(2)]
        gcol_s = sb("gcol_s", [128, 32], F32)
        gqk_s = sb("gqk_s", [128, 2], F32)
        gsub_s = sb("gsub_s", [128, 256], F32)
        lam_v = sb("lam_v", [128, 4], F32)
        ones = sb("ones", [128, 128], BF16)
        wst = [sb("wst%d" % i, [128, 1, 512], F32) for i in range(2)]
        PS = [es.enter_context(nc.psum_tensor("ps%d" % i, [128, 1024], F32)) for i in range(4)]

        Tb = [T("bank%d" % i) for i in range(8)]
        T_KT = [T() for _ in range(nblk)]
        T_V = [T() for _ in range(ntile)]
        T_wA, T_wB, T_g, T_lam, T_ones = T(), T(), T(), T(), T()
        T_wst = [T(), T()]
        T_xb = [T(), T()]
        T_QT = [T(), T()]
        T_PT = [T(), T(), T()]
        T_sq, T_rst = [T(), T()], [T(), T()]
        T_th = [T(), T()]
        T_Zg = [T() for _ in range(4)]
        T_e = [T(), T()]
        T_ob = [T() for _ in range(4)]
        T_out = T()

        P.op("sync", lambda e: e.dma_start(out=gcol_s[:], in_=gcol), writes=[T_g], dma="c0")
        P.op("sync", lambda e: e.dma_start(out=gqk_s[:], in_=gqk), writes=[T_g], dma="c0")
        P.op("sync", lambda e: e.dma_start(out=gsub_s[:], in_=gsub), writes=[T_g], dma="c0")
        P.op("sync", lambda e: e.dma_start(out=ea[0][:], in_=lamp[:, 0:256]), writes=[T_lam], dma="c0")
        P.op("sync", lambda e: e.dma_start(out=ea[1][:], in_=lamp[:, 256:512]), writes=[T_lam], dma="c0")
        P.op("gpsimd", lambda e: e.memset(ones[:], 1.0), writes=[T_ones])
        P.op("gpsimd", lambda e: e.memset(V[:, :, 256:257], 1.0), writes=T_V)
        P.op("vector", lambda e: e.tensor_scalar_mul(out=gqk_s[:, 0:1], in0=gqk_s[:, 0:1], scalar1=math.sqrt(128.0)),
             reads=[T_g], writes=[T_g])
        P.op("vector", lambda e: e.tensor_scalar_mul(out=gsub_s[:], in0=gsub_s[:], scalar1=(1.0 - LAM_INIT) * 16.0),
             reads=[T_g], writes=[T_g])
        P.op("vector", lambda e: e.tensor_tensor(out=eo[0][:, 0:128], in0=ea[0][:, 0:128], in1=ea[0][:, 128:256], op=ALU.mult),
             reads=[T_lam], writes=[T_lam])
        P.op("vector", lambda e: e.tensor_tensor(out=eo[0][:, 128:256], in0=ea[1][:, 0:128], in1=ea[1][:, 128:256], op=ALU.mult),
             reads=[T_lam], writes=[T_lam])
        P.op("vector", lambda e: e.reduce_sum(out=lam_v[:, 0:2], in_=eo[0][:].rearrange("p (a b) -> p a b", a=2), axis=AX.X),
             reads=[T_lam], writes=[T_lam, T_e[0], T_e[1]])
        P.op("scalar", lambda e: e.activation(out=lam_v[:, 2:4], in_=lam_v[:, 0:2], func=ACT.Exp),
             reads=[T_lam], writes=[T_lam])
        P.op("vector", lambda e: e.tensor_tensor(out=lam_v[:, 0:1], in0=lam_v[:, 3:4], in1=lam_v[:, 2:3], op=ALU.subtract),
             reads=[T_lam], writes=[T_lam])
        P.op("vector", lambda e: e.tensor_scalar_add(out=lam_v[:, 1:2], in0=lam_v[:, 0:1], scalar1=-LAM_INIT),
             reads=[T_lam], writes=[T_lam])
        nlam = lam_v[:, 1:2]
        k = 0
        for (wsrc, wdst, Tw) in ((wA, wAb, T_wA), (wB, wBb, T_wB)):
            for ch in range(16):
                b = k % 2
                P.op("sync", lambda e, b=b, ch=ch, wsrc=wsrc: e.dma_start(out=wst[b][:], in_=wsrc[:, ch:ch + 1, :]),
                     writes=[T_wst[b]], dma="wst%d" % b)
                for f in range(1):
                    ft = ch + f
                    for half in range(2):
                        if wsrc is wA:
                            gi = ft if half == 0 else 16 + ft
                        else:
                            gi = ft if half == 0 else 16 + ft
                        P.op("vector", lambda e, b=b, f=f, ft=ft, half=half, gi=gi, wdst=wdst: e.tensor_scalar_mul(
                            out=wdst[:, ft, half * 256:(half + 1) * 256], in0=wst[b][:, f, half * 256:(half + 1) * 256],
                            scalar1=gcol_s[:, gi:gi + 1]),
                            reads=[T_wst[b], T_g], writes=[Tw])
                k += 1

        def load_xb(j):
            b = j % 2
            P.op("sync", lambda e: e.dma_start(out=xb[b][:], in_=xnT[j]), writes=[T_xb[b]], dma="xb%d" % b)

        load_xb(0)
        def do_block(j):
            jb = j % 2
            if j + 1 < nblk:
                load_xb(j + 1)
            def proj_kq(i):
                for ft in range(16):
                    P.op("tensor", lambda e, ft=ft: e.matmul(
                        PS[0][:, i * 256:(i + 1) * 256], lhsT=wAb[:, ft, i * 128:(i + 1) * 128], rhs=xb[jb][:, ft, :],
                        start=(ft == 0), stop=(ft == 15)),
                        reads=[T_wA, T_xb[jb]], writes=[Tb[i // 2]])

            def proj_vz(ti):
                for ft in range(16):
                    P.op("tensor", lambda e, ft=ft: e.matmul(
                        PS[1][:, ti * 512:(ti + 1) * 512], lhsT=xb[jb][:, ft, ti * 128:(ti + 1) * 128], rhs=wBb[:, ft, :],
                        start=(ft == 0), stop=(ft == 15)),
                        reads=[T_wB, T_xb[jb]], writes=[Tb[2 + ti]])

            def square(hf):
                P.op("scalar", lambda e: e.activation(out=sq[:, hf * 512:(hf + 1) * 512], in_=PS[0][:, hf * 512:(hf + 1) * 512],
                                                      func=ACT.Square),
                     reads=[Tb[hf]], writes=[T_sq[hf]])

            def ssq(hf):
                for i in (2 * hf, 2 * hf + 1):
                    P.op("tensor", lambda e, i=i: e.matmul(PS[2][:, i * 256:(i + 1) * 256], lhsT=ones[:], rhs=sq[:, i * 256:(i + 1) * 256],
                                                          start=True, stop=True),
                         reads=[T_ones, T_sq[hf]], writes=[Tb[4 + hf]])

            def rstd(hf):
                P.op("scalar", lambda e: e.activation(out=rst[:, hf * 512:(hf + 1) * 512], in_=PS[2][:, hf * 512:(hf + 1) * 512],
                                                      func=ACT.Ln, bias=128.0 * EPS),
                     reads=[Tb[4 + hf]], writes=[T_rst[hf]])
                P.op("scalar", lambda e: e.activation(out=rst[:, hf * 512:(hf + 1) * 512], in_=rst[:, hf * 512:(hf + 1) * 512],
                                                      func=ACT.Exp, scale=-0.5),
                     reads=[T_rst[hf]], writes=[T_rst[hf]])

            def vz_evac(ti):
                tt = 2 * j + ti
                P.op("scalar", lambda e, ti=ti, tt=tt: e.copy(out=V[:, tt, 0:256], in_=PS[1][:, ti * 512:ti * 512 + 256]),
                     reads=[Tb[2 + ti]], writes=[T_V[tt]])
                P.op("scalar", lambda e, ti=ti: e.copy(out=eo[ti][:], in_=PS[1][:, ti * 512 + 256:(ti + 1) * 512]),
                     reads=[Tb[2 + ti]], writes=[T_e[ti]])

            def vz_gate(ti):
                zi = jb * 2 + ti
                P.op("scalar", lambda e, ti=ti: e.activation(out=th[ti][:], in_=eo[ti][:], func=ACT.Exp, scale=-1.0),
                     reads=[T_e[ti]], writes=[T_th[ti]])
                P.op("gpsimd", lambda e, ti=ti: e.tensor_scalar_add(out=th[ti][:], in0=th[ti][:], scalar1=1.0),
                     reads=[T_th[ti]], writes=[T_th[ti]])
                P.op("vector", lambda e, ti=ti: e.reciprocal(out=th[ti][:], in_=th[ti][:]),
                     reads=[T_th[ti]], writes=[T_th[ti]])
                P.op("vector", lambda e, ti=ti, zi=zi: e.tensor_tensor(out=Zg[zi][:], in0=th[ti][:], in1=eo[ti][:], op=ALU.mult),
                     reads=[T_th[ti], T_e[ti]], writes=[T_Zg[zi]])

            proj_kq(2); proj_kq(3)
            square(1)
            proj_kq(0); proj_kq(1)
            ssq(1)
            square(0)
            rstd(1)
            proj_vz(0)
            P.op("vector", lambda e: e.scalar_tensor_tensor(
                out=QTb[jb][:], in0=PS[0][:, 512:1024].rearrange("p (m t) -> p m t", m=2),
                scalar=gqk_s[:, 1:2], in1=rst[:, 512:1024].rearrange("p (m t) -> p m t", m=2), op0=ALU.mult, op1=ALU.mult),
                reads=[Tb[1], T_rst[1], T_g], writes=[T_QT[jb]])
            ssq(0)
            rstd(0)
            proj_vz(1)
            P.op("vector", lambda e: e.scalar_tensor_tensor(
                out=KT[:, :, j * QB:(j + 1) * QB], in0=PS[0][:, 0:512].rearrange("p (m t) -> p m t", m=2),
                scalar=gqk_s[:, 0:1], in1=rst[:, 0:512].rearrange("p (m t) -> p m t", m=2), op0=ALU.mult, op1=ALU.mult),
                reads=[Tb[0], T_rst[0], T_g], writes=[T_KT[j]])
            for ti in range(2):
                vz_evac(ti)
            for ti in range(2):
                vz_gate(ti)

            npair = j + 1
            def do_qk(p):
                sp = p % 2
                for kl in range(2):
                    kt = 2 * p + kl
                    for m in range(2):
                        P.op("tensor", lambda e, kt=kt, kl=kl, m=m, sp=sp: e.matmul(
                            PS[sp][:, (kl * 2 + m) * 256:(kl * 2 + m + 1) * 256],
                            lhsT=KT[:, m, kt * 128:(kt + 1) * 128], rhs=QTb[jb][:, m, :], start=True, stop=True),
                            reads=[T_KT[kt // 2], T_QT[jb]], writes=[Tb[2 * sp + kl]])

            def do_pair(p):
                sp = p % 2
                pb = p % 3
                last = (p == npair - 1)
                P.op("scalar", lambda e, sp=sp, pb=pb: e.activation(
                    out=PT[pb][:].rearrange("p a q -> p (a q)"), in_=PS[sp][:], func=ACT.Exp),
                    reads=[Tb[2 * sp], Tb[2 * sp + 1]], writes=[T_PT[pb]])
                if last:
                    P.op("vector", lambda e, pb=pb: e.memset(PT[pb][64:128, 0:2, 0:64], 0.0), writes=[T_PT[pb]])
                    P.op("vector", lambda e, pb=pb: e.memset(PT[pb][64:128, 2:4, 128:192], 0.0), writes=[T_PT[pb]])
                for kl in range(2):
                    kt = 2 * p + kl
                    for qt in range(2):
                        if last and kl == 1 and qt == 0:
                            continue
                        for m in range(2):
                            P.op("tensor", lambda e, kt=kt, kl=kl, m=m, qt=qt, pb=pb: e.matmul(
                                PS[2 + qt][:, m * 512:m * 512 + 257],
                                lhsT=PT[pb][:, kl * 2 + m, qt * 128:(qt + 1) * 128], rhs=V[:, kt, :],
                                start=(kt == 0), stop=(kt == 2 * j + qt)),
                                reads=[T_PT[pb], T_V[kt]], writes=[Tb[4 + 2 * qt + m]])

            do_qk(0)
            for p in range(npair):
                if p + 1 < npair:
                    do_qk(p + 1)
                do_pair(p)

            def epilogue(qt):
                tt = 2 * j + qt
                zi = jb * 2 + qt
                O1 = PS[2 + qt][:, 0:257]
                O2 = PS[2 + qt][:, 512:769]
                b1, b2 = Tb[4 + 2 * qt], Tb[4 + 2 * qt + 1]
                sm = small[qt]
                P.op("vector", lambda e, O1=O1, sm=sm: e.reciprocal(out=sm[:, 0:1], in_=O1[:, 256:257]),
                     reads=[b1], writes=[T_e[qt]])
                P.op("vector", lambda e, O2=O2, sm=sm: e.reciprocal(out=sm[:, 1:2], in_=O2[:, 256:257]),
                     reads=[b2], writes=[T_e[qt]])
                P.op("vector", lambda e, sm=sm: e.tensor_tensor(out=sm[:, 2:3], in0=sm[:, 1:2], in1=nlam, op=ALU.mult),
                     reads=[T_e[qt], T_lam], writes=[T_e[qt]])
                P.op("vector", lambda e, O1=O1, sm=sm, qt=qt: e.tensor_scalar_mul(out=ea[qt][:], in0=O1[:, 0:256], scalar1=sm[:, 0:1]),
                     reads=[b1, T_e[qt]], writes=[T_e[qt]])
                P.op("vector", lambda e, O2=O2, sm=sm, qt=qt: e.scalar_tensor_tensor(
                    out=eo[qt][:], in0=O2[:, 0:256], scalar=sm[:, 2:3], in1=ea[qt][:], op0=ALU.mult, op1=ALU.add),
                    reads=[b2, T_e[qt]], writes=[T_e[qt]])
                P.op("scalar", lambda e, qt=qt: e.activation(out=esq[qt][:], in_=eo[qt][:], func=ACT.Square),
                     reads=[T_e[qt]], writes=[T_e[qt]])
                P.op("vector", lambda e, sm=sm, qt=qt: e.reduce_sum(out=sm[:, 3:4], in_=esq[qt][:], axis=AX.X),
                     reads=[T_e[qt]], writes=[T_e[qt]])
                P.op("scalar", lambda e, sm=sm: e.activation(out=sm[:, 4:5], in_=sm[:, 3:4], func=ACT.Ln, bias=256.0 * EPS),
                     reads=[T_e[qt]], writes=[T_e[qt]])
                P.op("scalar", lambda e, sm=sm: e.activation(out=sm[:, 4:5], in_=sm[:, 4:5], func=ACT.Exp, scale=-0.5),
                     reads=[T_e[qt]], writes=[T_e[qt]])
                P.op("vector", lambda e, sm=sm, qt=qt: e.scalar_tensor_tensor(
                    out=eof[qt][:], in0=eo[qt][:], scalar=sm[:, 4:5], in1=gsub_s[:], op0=ALU.mult, op1=ALU.mult),
                    reads=[T_e[qt], T_g], writes=[T_e[qt]])
                P.op("gpsimd", lambda e, qt=qt, zi=zi: e.tensor_tensor(out=ob[zi][:], in0=eof[qt][:], in1=Zg[zi][:], op=ALU.mult),
                     reads=[T_e[qt], T_Zg[zi]], writes=[T_ob[zi]])
                P.op("gpsimd", lambda e, tt=tt, zi=zi: e.dma_start(out=o_out[tt * 128:(tt + 1) * 128, :], in_=ob[zi][:]),
                     reads=[T_ob[zi]], writes=[T_out], dma="ob%d" % zi)
            for qt in range(2):
                epilogue(qt)

        for j in range(nblk):
            do_block(j)

        if dbg:
            dk = nc.dram_tensor("d_KT", [128, 2, s_len], BF16, kind="ExternalOutput").ap()
            dv = nc.dram_tensor("d_V", [128, ntile, 257], BF16, kind="ExternalOutput").ap()
            dq = nc.dram_tensor("d_QT", [128, 2, QB], BF16, kind="ExternalOutput").ap()
            dr = nc.dram_tensor("d_rst", [128, 1024], F32, kind="ExternalOutput").ap()
            dz = nc.dram_tensor("d_Zg", [128, 256], BF16, kind="ExternalOutput").ap()
            dp = nc.dram_tensor("d_PT", [128, 4, QB], BF16, kind="ExternalOutput").ap()
            de = nc.dram_tensor("d_eo", [128, 256], F32, kind="ExternalOutput").ap()
            dea = nc.dram_tensor("d_ea", [128, 256], F32, kind="ExternalOutput").ap()
            dsm = nc.dram_tensor("d_sm", [128, 8], F32, kind="ExternalOutput").ap()
            dl = nc.dram_tensor("d_lam", [128, 4], F32, kind="ExternalOutput").ap()
            dw = nc.dram_tensor("d_wA", [128, 16, 512], BF16, kind="ExternalOutput").ap()
            allT = T_KT + T_V + T_QT + T_rst + T_Zg + T_PT + T_e + [T_lam, T_wA]
            for (dst, src) in ((dk, KT), (dv, V), (dq, QTb[(nblk - 1) % 2]), (dr, rst), (dz, Zg[((nblk - 1) % 2) * 2]),
                               (dp, PT[(nblk - 1) % 3]), (de, eo[0]), (dea, ea[0]), (dsm, small[0]), (dl, lam_v), (dw, wAb)):
                P.op("sync", lambda e, dst=dst, src=src: e.dma_start(out=dst, in_=src[:]), reads=allT, writes=[T_out], dma="dbg")
        P.emit()
    return nc


def l2_inputs(xnT_full, w_kv, b_w_in, kv_norm_g, b_norm_g, k_norm_g, q_norm_g, lam, subln_g, s_len=S):
    QB = 256
    nblk = s_len // QB
    xb = np.ascontiguousarray(
        xnT_full.reshape(16, 128, nblk, QB).transpose(2, 1, 0, 3))

    def wl(w):
        return np.ascontiguousarray(w.reshape(16, 128, w.shape[1]).transpose(1, 0, 2))

    gcol = np.ascontiguousarray(np.concatenate(
        [kv_norm_g.reshape(16, 128).T, b_norm_g.reshape(16, 128).T], axis=1)).astype(np.float32)
    gqk = np.ascontiguousarray(np.stack([k_norm_g, q_norm_g], axis=1)).astype(np.float32)
    gsub = np.ascontiguousarray(np.broadcast_to(subln_g.reshape(1, 256), (128, 256))).astype(np.float32)
    lamp = np.ascontiguousarray(np.broadcast_to(lam.reshape(1, 512), (128, 512))).astype(np.float32)
    maps = []
    for c in range(NCORES):
        kc = w_kv[:, c * 256:(c + 1) * 256]
        vc = w_kv[:, 2048 + c * 256:2048 + (c + 1) * 256]
        qc = b_w_in[:, c * 256:(c + 1) * 256]
        zc = b_w_in[:, 2048 + c * 256:2048 + (c + 1) * 256]
        maps.append({
            "xnT": xb,
            "wA": wl(np.concatenate([kc, qc], axis=1)),
            "wB": wl(np.concatenate([vc, zc], axis=1)),
            "gcol": gcol, "gqk": gqk, "gsub": gsub, "lamp": lamp,
        })
    return maps


def build_l3(ntok=2048):
    ntt = ntok // 128
    nc = bass.Bass("TRN2", target_bir_lowering=False)
    oT = nc.dram_tensor("oT", [ntt, 128, 16, 128], BF16, kind="ExternalInput").ap()
    x1 = nc.dram_tensor("x1", [ntok, D], F32, kind="ExternalInput").ap()
    wo = nc.dram_tensor("wo", [128, 16, D], F32, kind="ExternalInput").ap()
    y = nc.dram_tensor("y", [ntok, D], F32, kind="ExternalOutput").ap()
    P = Prog(nc)
    with contextlib.ExitStack() as es:
        def sb(name, shape, dt):
            return es.enter_context(nc.sbuf_tensor(name, shape, dt))
        wob = sb("wob", [128, 16, D], BF16)
        ot = [sb("ot%d" % i, [128, 16, 128], BF16) for i in range(2)]
        xt = [sb("xt%d" % i, [128, D], F32) for i in range(2)]
        yt = [sb("yt%d" % i, [128, D], F32) for i in range(2)]
        PS = [es.enter_context(nc.psum_tensor("ps%d" % i, [128, 2048], F32)) for i in range(2)]
        T_w = [T() for _ in range(16)]
        T_ot, T_xt, T_yt, T_ps = [T(), T()], [T(), T()], [T(), T()], [T(), T()]
        T_y = T()
        wstg = [sb("wstg%d" % i, [128, D], F32) for i in range(3)]
        T_wstg = [T(), T(), T()]
        for ct in range(16):
            i = ct % 3
            P.op("sync", lambda e, ct=ct, i=i: e.dma_start(out=wstg[i][:], in_=wo[:, ct, :]), writes=[T_wstg[i]], dma="w%d" % i)
            if ct % 2 == 0:
                P.op("vector", lambda e, ct=ct, i=i: e.tensor_copy(out=wob[:, ct, :], in_=wstg[i][:]), reads=[T_wstg[i]], writes=[T_w[ct]])
            else:
                P.op("scalar", lambda e, ct=ct, i=i: e.copy(out=wob[:, ct, :], in_=wstg[i][:]), reads=[T_wstg[i]], writes=[T_w[ct]])

        def loads(tt):
            b = tt % 2
            P.op("sync", lambda e: e.dma_start(out=ot[b][:], in_=oT[tt]), writes=[T_ot[b]], dma="ot%d" % b)
            P.op("sync", lambda e: e.dma_start(out=xt[b][:], in_=x1[tt * 128:(tt + 1) * 128, :]), writes=[T_xt[b]], dma="xt%d" % b)

        def tile(tt):
            b = tt % 2
            if tt + 1 < ntt:
                loads(tt + 1)
            for fb in range(4):
                for ct in range(16):
                    P.op("tensor", lambda e, fb=fb, ct=ct: e.matmul(
                        PS[b][:, fb * 512:(fb + 1) * 512], lhsT=ot[b][:, ct, :], rhs=wob[:, ct, fb * 512:(fb + 1) * 512],
                        start=(ct == 0), stop=(ct == 15)),
                        reads=[T_ot[b], T_w[ct]], writes=[T_ps[b]])
            P.op("vector", lambda e: e.tensor_tensor(out=yt[b][:], in0=PS[b][:], in1=xt[b][:], op=ALU.add),
                 reads=[T_ps[b], T_xt[b]], writes=[T_yt[b]])
            P.op("gpsimd", lambda e: e.dma_start(out=y[tt * 128:(tt + 1) * 128, :], in_=yt[b][:]),
                 reads=[T_yt[b]], writes=[T_y], dma="yt%d" % b)

        loads(0)
        for tt in range(ntt):
            tile(tt)
        P.emit()
    return nc


def build_l1(ntok=2048, NP=1024):
    npass = ntok // NP
    NB = NP // 512
    NL = NP + HALO
    nc = bass.Bass("TRN2", target_bir_lowering=False)
    x = nc.dram_tensor("x", [ntok + HALO, D], F32, kind="ExternalInput").ap()
    w_in = nc.dram_tensor("w_in", [48, 128, 16, 128], F32, kind="ExternalInput").ap()
    w_out = nc.dram_tensor("w_out", [128, 16, D], F32, kind="ExternalInput").ap()
    gbc_d = nc.dram_tensor("gbc", [128, D], F32, kind="ExternalInput").ap()
    dww_d = nc.dram_tensor("dww", [128, 16, CONV_W], F32, kind="ExternalInput").ap()
    cvec_d = nc.dram_tensor("cvec", [128, 48], F32, kind="ExternalInput").ap()
    identb_d = nc.dram_tensor("identb", [128, 128], BF16, kind="ExternalInput").ap()
    identf_d = nc.dram_tensor("identf", [128, 128], F32, kind="ExternalInput").ap()
    x1_o = nc.dram_tensor("x1", [ntok, D], F32, kind="ExternalOutput").ap()
    xnT_o = nc.dram_tensor("xnT", [ntok // 256, 128, 16, 256], BF16, kind="ExternalOutput").ap()
    wi_b = nc.dram_tensor("wi_b", [48, 128, 2048], BF16).ap()
    wo_b = nc.dram_tensor("wo_b", [16, 128, 2048], BF16).ap()

    P = Prog(nc)
    with contextlib.ExitStack() as es:
        def sb(name, shape, dt):
            return es.enter_context(nc.sbuf_tensor(name, shape, dt))

        HT_N = 16 * NL
        ARENA = max(HT_N + 6 * 2048 + 2 * CONV_W * 128, 16 * D)
        arena = sb("arena", [128, ARENA], BF16)
        hT = arena[:, 0:HT_N].rearrange("p (f t) -> p f t", f=16)
        wt = [arena[:, HT_N + i * 2048: HT_N + (i + 1) * 2048].rearrange("p (f n) -> p f n", f=16) for i in range(6)]
        wa, wb_, wz = wt[0:2], wt[2:4], wt[4:6]
        dgo = HT_N + 6 * 2048
        dg = [arena[:, dgo + i * CONV_W * 128: dgo + (i + 1) * CONV_W * 128].rearrange("p (j n) -> p j n", j=CONV_W) for i in range(2)]
        wob = arena[:, 0:16 * D].rearrange("p (c n) -> p c n", c=16)
        cvT = sb("cvT", [128, 16, NP], BF16)
        yT = [sb("yT%d" % i, [128, NL], BF16) for i in range(2)]
        acc_s = sb("acc_s", [128, NP], F32)
        acc_q = sb("acc_q", [128, NP], F32)
        rstd_bc = sb("rstd_bc", [128, NP], F32)
        nmr_bc = sb("nmr_bc", [128, NP], F32)
        xt = [sb("xt%d" % i, [128, D], F32) for i in range(2)]
        sqx = sb("sqx", [128, D], F32)
        tmp = {k: [sb("%s%d" % (k, i), [128, 512], F32) for i in range(2)] for k in ("ta", "tb", "tc", "td", "te", "tf")}
        xnblk = sb("xnblk", [128, 16, 256], BF16)
        gbc = sb("gbc_s", [128, D], F32)
        dww = sb("dww_s", [128, 16, CONV_W], F32)
        cvec = sb("cvec_s", [128, 48], F32)
        identb = sb("identb_s", [128, 128], BF16)
        identf = sb("identf_s", [128, 128], F32)
        onesf = sb("onesf", [128, 128], F32)
        small = sb("small", [128, 8], F32)
        B = [es.enter_context(nc.psum_tensor("b%d" % i, [128, 512], F32)) for i in range(8)]

        Tb = [T("bank%d" % i) for i in range(8)]
        T_c = T()
        T_hT = T()
        T_wa, T_wb, T_wz, T_dg = [T(), T()], [T(), T()], [T(), T()], [T(), T()]
        T_cv = [T() for _ in range(16)]
        T_yT = [T(), T()]
        T_acc, T_st = T(), T()
        T_xt = [T(), T()]
        T_sqx = T()
        T_tmp = {k: [T(), T()] for k in tmp}
        T_xnblk = T()
        T_sm = T()
        T_x1o, T_xno = T(), T()
        T_wob = T()

        dwb = lambda c: cvec[:, c:c + 1]
        lng = lambda c: cvec[:, 16 + c:17 + c]
        lnb = lambda c: cvec[:, 32 + c:33 + c]

        for (dst, src) in ((gbc, gbc_d), (dww, dww_d), (cvec, cvec_d), (identb, identb_d), (identf, identf_d)):
            P.op("sync", lambda e, dst=dst, src=src: e.dma_start(out=dst[:], in_=src), writes=[T_c], dma="c0")
        P.op("gpsimd", lambda e: e.memset(onesf[:], 1.0), writes=[T_c])

        T_wib = [T() for _ in range(48)]
        T_wobd = [T() for _ in range(16)]
        stg_in = [xt[0][:], xt[1][:], sqx[:]]
        T_stg_in = [[T_xt[0]], [T_xt[1]], [T_sqx]]
        xnflat = xnblk[:].rearrange("p f t -> p (f t)")
        stg_out = [xnflat[:, i * 2048:(i + 1) * 2048] for i in range(2)]
        T_stg_out = [T(), T()]
        pc = {"n": 0}

        def wsrc(k):
            if k < 48:
                return w_in[k].rearrange("p f n -> p (f n)"), wi_b[k], T_wib[k]
            return w_out[:, k - 48, :], wo_b[k - 48], T_wobd[k - 48]

        def precast(ks):
            slots = []
            for k in ks:
                n = pc["n"]; pc["n"] += 1
                i, o = n % 3, n % 2
                src = wsrc(k)[0]
                P.op("sync", lambda e, i=i, src=src: e.dma_start(out=stg_in[i], in_=src), writes=T_stg_in[i], dma="wl%d" % i)
                slots.append((k, n, i, o))
            for (k, n, i, o) in slots:
                _, dst, Tdst = wsrc(k)
                if n % 2 == 0:
                    P.op("vector", lambda e, i=i, o=o: e.tensor_copy(out=stg_out[o], in_=stg_in[i]), reads=T_stg_in[i],
                         writes=[T_stg_out[o], T_xnblk])
                else:
                    P.op("scalar", lambda e, i=i, o=o: e.copy(out=stg_out[o], in_=stg_in[i]), reads=T_stg_in[i],
                         writes=[T_stg_out[o], T_xnblk])
                P.op("sync", lambda e, o=o, dst=dst: e.dma_start(out=dst, in_=stg_out[o]), reads=[T_stg_out[o], T_xnblk], writes=[Tdst],
                     dma="ws%d" % o)

        late = list(range(32, 64))

        cnt = {"xt": 0}

        def rstd_from_ss(np_, col_in, col_out, n):
            P.op("scalar", lambda e: e.activation(out=small[0:np_, col_out:col_out + 1], in_=small[0:np_, col_in:col_in + 1],
                                                  func=ACT.Ln, scale=1.0 / n, bias=EPS), reads=[T_sm], writes=[T_sm])
            P.op("scalar", lambda e: e.activation(out=small[0:np_, col_out:col_out + 1], in_=small[0:np_, col_out:col_out + 1],
                                                  func=ACT.Exp, scale=-0.5), reads=[T_sm], writes=[T_sm])

        junk = cvT[:].rearrange("p c t -> p (c t)")[:, 0:2 * D].bitcast(F32)
        T_junk = T_cv[0:(2 * D + NP - 1) // NP]
        T_smp = [T(), T()]

        def norm_tile(row0, np_, col0):
            b = cnt["xt"] % 2
            cnt["xt"] += 1
            c0_, c1_ = 4 + 2 * b, 5 + 2 * b
            Ts = T_smp[b]
            P.op("sync", lambda e: e.dma_start(out=xt[b][0:np_, :], in_=x[row0:row0 + np_, :]), writes=[T_xt[b]], dma="xt%d" % b)
            P.op("scalar", lambda e: e.activation(out=junk[0:np_, :], in_=xt[b][0:np_, :], func=ACT.Square),
                 reads=[T_xt[b]], writes=T_junk)
            P.op("vector", lambda e: e.reduce_sum(out=small[0:np_, c0_:c0_ + 1], in_=junk[0:np_, :], axis=AX.X),
                 reads=T_junk, writes=[Ts])
            P.op("scalar", lambda e: e.activation(out=small[0:np_, c1_:c1_ + 1], in_=small[0:np_, c0_:c0_ + 1],
                                                  func=ACT.Ln, scale=1.0 / D, bias=EPS), reads=[Ts], writes=[Ts])
            P.op("scalar", lambda e: e.activation(out=small[0:np_, c1_:c1_ + 1], in_=small[0:np_, c1_:c1_ + 1],
                                                  func=ACT.Exp, scale=-0.5), reads=[Ts], writes=[Ts])
            P.op("scalar", lambda e: e.activation(out=sqx[0:np_, :], in_=xt[b][0:np_, :], func=ACT.Copy, scale=small[0:np_, c1_:c1_ + 1]),
                 reads=[T_xt[b], Ts], writes=[T_sqx])
            for ft in range(16):
                P.op("tensor", lambda e, ft=ft: e.transpose(
                    B[ft // 4][:, (ft % 4) * 128:(ft % 4) * 128 + np_], sqx[0:np_, ft * 128:(ft + 1) * 128], identf[0:np_, 0:np_]),
                    reads=[T_sqx, T_c], writes=[Tb[ft // 4]])
            for q in range(4):
                P.op("vector", lambda e, q=q: e.tensor_tensor(
                    out=hT[:, q * 4:(q + 1) * 4, col0:col0 + np_],
                    in0=B[q][:].rearrange("p (f t) -> p f t", f=4)[:, :, 0:np_],
                    in1=gbc[:, q * 512:(q + 1) * 512].rearrange("p (f t) -> p f t", f=4)[:, :, 0:np_], op=ALU.mult),
                    reads=[Tb[q], T_c], writes=[T_hT])

        def do_pass(h):
            r_h = h * NP
            r_o = HALO + h * NP
            norm_tile(r_h, HALO, 0)
            for t in range(NP // 128):
                norm_tile(r_o + t * 128, 128, HALO + t * 128)
            P.op("gpsimd", lambda e: e.memset(acc_s[:], 0.0), writes=[T_acc])
            P.op("gpsimd", lambda e: e.memset(acc_q[:], 0.0), writes=[T_acc])

            def prefetch1(c):
                b = c % 2
                P.op("gpsimd", lambda e: e.dma_start(out=wa[b], in_=wi_b[c].rearrange("p (f n) -> p f n", f=16)),
                     reads=[T_wib[c]], writes=[T_wa[b]], dma="wa%d" % b)
                P.op("gpsimd", lambda e: e.dma_start(out=wb_[b], in_=wi_b[16 + c].rearrange("p (f n) -> p f n", f=16)),
                     reads=[T_wib[16 + c]], writes=[T_wb[b]], dma="wb%d" % b)
                for j in range(CONV_W):
                    P.op("vector", lambda e, j=j: e.tensor_scalar_mul(out=dg[b][:, j, :], in0=identb[:], scalar1=dww[:, c, j:j + 1]),
                         reads=[T_c], writes=[T_dg[b]])

            def stage2(c):
                b = c % 2
                yt_ = yT[b]
                for (w_, T_w, off) in ((wa[b], T_wa[b], 0), (wb_[b], T_wb[b], 32)):
                    for ft in range(16):
                        P.op("tensor", lambda e, w_=w_, off=off, ft=ft: e.matmul(
                            B[6][:, off:off + HALO], lhsT=w_[:, ft, :], rhs=hT[:, ft, 0:HALO], start=(ft == 0), stop=(ft == 15)),
                            reads=[T_w, T_hT], writes=[Tb[6]])
                P.op("scalar", lambda e: e.activation(out=tmp["ta"][0][:, 0:HALO], in_=B[6][:, 32:32 + HALO], func=ACT.Sigmoid),
                     reads=[Tb[6]], writes=[T_tmp["ta"][0]])
                P.op("vector", lambda e: e.tensor_tensor(out=yt_[:, 0:HALO], in0=B[6][:, 0:HALO], in1=tmp["ta"][0][:, 0:HALO], op=ALU.mult),
                     reads=[Tb[6], T_tmp["ta"][0]], writes=[T_yT[b]])
                for tb in range(NB):
                    pa, pb = B[2 * (tb % 2)], B[2 * (tb % 2) + 1]
                    Ta, Tbb = Tb[2 * (tb % 2)], Tb[2 * (tb % 2) + 1]
                    c0 = HALO + tb * 512
                    for (w_, T_w, ps, Tp) in ((wa[b], T_wa[b], pa, Ta), (wb_[b], T_wb[b], pb, Tbb)):
                        for ft in range(16):
                            P.op("tensor", lambda e, w_=w_, ps=ps, ft=ft, c0=c0: e.matmul(
                                ps[:], lhsT=w_[:, ft, :], rhs=hT[:, ft, c0:c0 + 512], start=(ft == 0), stop=(ft == 15)),
                                reads=[T_w, T_hT], writes=[Tp])
                    sg = tmp["ta"][tb % 2]
                    P.op("scalar", lambda e, pb=pb, sg=sg: e.activation(out=sg[:], in_=pb[:], func=ACT.Sigmoid),
                         reads=[Tbb], writes=[T_tmp["ta"][tb % 2]])
                    P.op("vector", lambda e, pa=pa, sg=sg, c0=c0: e.tensor_tensor(out=yt_[:, c0:c0 + 512], in0=pa[:], in1=sg[:], op=ALU.mult),
                         reads=[Ta, T_tmp["ta"][tb % 2]], writes=[T_yT[b]])
                for tb in range(NB):
                    pc, Tc_ = B[4 + tb % 2], Tb[4 + tb % 2]
                    for j in range(CONV_W):
                        P.op("tensor", lambda e, pc=pc, j=j, tb=tb: e.matmul(
                            pc[:], lhsT=dg[b][:, j, :], rhs=yt_[:, tb * 512 + j: tb * 512 + j + 512],
                            start=(j == 0), stop=(j == CONV_W - 1)),
                            reads=[T_dg[b], T_yT[b]], writes=[Tc_])
                    blk = slice(tb * 512, (tb + 1) * 512)
                    sq_ = tmp["tb"][tb % 2]
                    P.op("scalar", lambda e, pc=pc, blk=blk: e.activation(out=cvT[:, c, blk], in_=pc[:], func=ACT.Identity, bias=dwb(c)),
                         reads=[Tc_, T_c], writes=[T_cv[c]])
                    P.op("scalar", lambda e, pc=pc, sq_=sq_: e.activation(out=sq_[:], in_=pc[:], func=ACT.Square, bias=dwb(c)),
                         reads=[Tc_, T_c], writes=[T_tmp["tb"][tb % 2]])
                    P.op("vector", lambda e, blk=blk: e.tensor_tensor(out=acc_s[:, blk], in0=acc_s[:, blk], in1=cvT[:, c, blk], op=ALU.add),
                         reads=[T_cv[c], T_acc], writes=[T_acc])
                    P.op("vector", lambda e, blk=blk, sq_=sq_: e.tensor_tensor(out=acc_q[:, blk], in0=acc_q[:, blk], in1=sq_[:], op=ALU.add),
                         reads=[T_tmp["tb"][tb % 2], T_acc], writes=[T_acc])

            if h == 0:
                precast([0, 16]); precast([1, 17])
            prefetch1(0)
            for c in range(16):
                if h == 0:
                    if c + 2 < 16:
                        precast([c + 2, 16 + c + 2])
                    precast([late[2 * c], late[2 * c + 1]])
                if c + 1 < 16:
                    prefetch1(c + 1)
                stage2(c)

            def prefetch_z(c):
                b = c % 2
                P.op("gpsimd", lambda e: e.dma_start(out=wz[b], in_=wi_b[32 + c].rearrange("p (f n) -> p f n", f=16)),
                     reads=[T_wib[32 + c]], writes=[T_wz[b]], dma="wz%d" % b)
            prefetch_z(0)
            for tb in range(NB):
                blk = slice(tb * 512, (tb + 1) * 512)
                P.op("tensor", lambda e, blk=blk: e.matmul(B[0][:], lhsT=onesf[:], rhs=acc_s[:, blk], start=True, stop=True),
                     reads=[T_c, T_acc], writes=[Tb[0]])
                P.op("tensor", lambda e, blk=blk: e.matmul(B[1][:], lhsT=onesf[:], rhs=acc_q[:, blk], start=True, stop=True),
                     reads=[T_c, T_acc], writes=[Tb[1]])
                mean, m2 = tmp["tc"][0], tmp["td"][0]
                P.op("vector", lambda e: e.tensor_scalar_mul(out=mean[:], in0=B[0][:], scalar1=1.0 / D),
                     reads=[Tb[0]], writes=[T_tmp["tc"][0]])
                P.op("vector", lambda e: e.tensor_tensor(out=m2[:], in0=mean[:], in1=mean[:], op=ALU.mult),
                     reads=[T_tmp["tc"][0]], writes=[T_tmp["td"][0]])
                P.op("vector", lambda e: e.scalar_tensor_tensor(out=m2[:], in0=B[1][:], scalar=1.0 / D, in1=m2[:],
                                                                op0=ALU.mult, op1=ALU.subtract),
                     reads=[Tb[1], T_tmp["td"][0]], writes=[T_tmp["td"][0]])
                P.op("scalar", lambda e, blk=blk: e.activation(out=rstd_bc[:, blk], in_=m2[:], func=ACT.Ln, bias=EPS),
                     reads=[T_tmp["td"][0]], writes=[T_st])
                P.op("scalar", lambda e, blk=blk: e.activation(out=rstd_bc[:, blk], in_=rstd_bc[:, blk], func=ACT.Exp, scale=-0.5),
                     reads=[T_st], writes=[T_st])
                P.op("vector", lambda e, blk=blk: e.scalar_tensor_tensor(out=nmr_bc[:, blk], in0=mean[:], scalar=-1.0, in1=rstd_bc[:, blk],
                                                                         op0=ALU.mult, op1=ALU.mult),
                     reads=[T_tmp["tc"][0], T_st], writes=[T_st])

            def stage4(c):
                b = c % 2
                for tb in range(NB):
                    k = tb % 2
                    pz, Tz = B[2 + k], Tb[2 + k]
                    c0 = HALO + tb * 512
                    blk = slice(tb * 512, (tb + 1) * 512)
                    for ft in range(16):
                        P.op("tensor", lambda e, pz=pz, ft=ft, c0=c0: e.matmul(
                            pz[:], lhsT=wz[b][:, ft, :], rhs=hT[:, ft, c0:c0 + 512], start=(ft == 0), stop=(ft == 15)),
                            reads=[T_wz[b], T_hT], writes=[Tz])
                    sz, gz, t1, s2, l_, u_ = (tmp[n][k] for n in ("ta", "tb", "tc", "td", "te", "tf"))
                    Ts = {n: T_tmp[n][k] for n in ("ta", "tb", "tc", "td", "te", "tf")}
                    P.op("scalar", lambda e, pz=pz, sz=sz: e.activation(out=sz[:], in_=pz[:], func=ACT.Sigmoid),
                         reads=[Tz], writes=[Ts["ta"]])
                    P.op("vector", lambda e, pz=pz, sz=sz, gz=gz: e.tensor_tensor(out=gz[:], in0=pz[:], in1=sz[:], op=ALU.mult),
                         reads=[Tz, Ts["ta"]], writes=[Ts["tb"]])
                    P.op("vector", lambda e, t1=t1, blk=blk: e.tensor_tensor(out=t1[:], in0=cvT[:, c, blk], in1=rstd_bc[:, blk], op=ALU.mult),
                         reads=[T_cv[c], T_st], writes=[Ts["tc"]])
                    P.op("vector", lambda e, t1=t1, blk=blk: e.tensor_tensor(out=t1[:], in0=t1[:], in1=nmr_bc[:, blk], op=ALU.add),
                         reads=[Ts["tc"], T_st], writes=[Ts["tc"]])
                    P.op("scalar", lambda e, t1=t1, s2=s2: e.activation(out=s2[:], in_=t1[:], func=ACT.Sigmoid, scale=lng(c), bias=lnb(c)),
                         reads=[Ts["tc"], T_c], writes=[Ts["td"]])
                    P.op("vector", lambda e, t1=t1, l_=l_: e.tensor_scalar(out=l_[:], in0=t1[:], scalar1=lng(c), scalar2=lnb(c),
                                                                          op0=ALU.mult, op1=ALU.add),
                         reads=[Ts["tc"], T_c], writes=[Ts["te"]])
                    P.op("vector", lambda e, l_=l_, s2=s2, u_=u_: e.tensor_tensor(out=u_[:], in0=l_[:], in1=s2[:], op=ALU.mult),
                         reads=[Ts["te"], Ts["td"]], writes=[Ts["tf"]])
                    P.op("vector", lambda e, u_=u_, gz=gz, blk=blk: e.tensor_tensor(out=cvT[:, c, blk], in0=u_[:], in1=gz[:], op=ALU.mult),
                         reads=[Ts["tf"], Ts["tb"]], writes=[T_cv[c]])

            for c in range(16):
                if c + 1 < 16:
                    prefetch_z(c + 1)
                stage4(c)

            alias = [T_hT] + T_wa + T_wb + T_wz + T_dg
            for ct in range(16):
                P.op("sync", lambda e, ct=ct: e.dma_start(out=wob[:, ct, :], in_=wo_b[ct]), reads=[T_wobd[ct]], writes=[T_wob] + alias,
                     dma="wo%d" % (ct % 4))

            xb_of = {}

            def s5_mm(t):
                b = cnt["xt"] % 2
                cnt["xt"] += 1
                xb_of[t] = b
                bs = 4 * (t % 2)
                row = h * NP + t * 128
                P.op("sync", lambda e: e.dma_start(out=xt[b][:], in_=x[HALO + row:HALO + row + 128, :]), writes=[T_xt[b]], dma="xt%d" % b)
                for fb in range(4):
                    for ct in range(16):
                        P.op("tensor", lambda e, fb=fb, ct=ct: e.matmul(
                            B[bs + fb][:], lhsT=cvT[:, ct, t * 128:(t + 1) * 128], rhs=wob[:, ct, fb * 512:(fb + 1) * 512],
                            start=(ct == 0), stop=(ct == 15)),
                            reads=[T_cv[ct], T_wob], writes=[Tb[bs + fb]])

            def s5_rest(t):
                b = xb_of[t]
                bs = 4 * (t % 2)
                row = h * NP + t * 128
                for fb in range(4):
                    P.op("vector", lambda e, fb=fb: e.tensor_tensor(out=xt[b][:, fb * 512:(fb + 1) * 512], in0=B[bs + fb][:],
                                                                    in1=xt[b][:, fb * 512:(fb + 1) * 512], op=ALU.add),
                         reads=[Tb[bs + fb], T_xt[b]], writes=[T_xt[b]])
                P.op("gpsimd", lambda e: e.dma_start(out=x1_o[row:row + 128, :], in_=xt[b][:]), reads=[T_xt[b]], writes=[T_x1o], dma="x1o%d" % b)
                P.op("scalar", lambda e: e.activation(out=sqx[:], in_=xt[b][:], func=ACT.Square), reads=[T_xt[b]], writes=[T_sqx])
                P.op("vector", lambda e: e.reduce_sum(out=small[:, 0:1], in_=sqx[:], axis=AX.X), reads=[T_sqx], writes=[T_sm])
                rstd_from_ss(128, 0, 1, float(D))
                P.op("scalar", lambda e: e.activation(out=sqx[:], in_=xt[b][:], func=ACT.Copy, scale=small[:, 1:2]),
                     reads=[T_xt[b], T_sm], writes=[T_sqx])
                for ft in range(16):
                    P.op("tensor", lambda e, ft=ft: e.transpose(
                        B[bs + ft // 4][:, (ft % 4) * 128:(ft % 4 + 1) * 128], sqx[:, ft * 128:(ft + 1) * 128], identf[:]),
                        reads=[T_sqx, T_c], writes=[Tb[bs + ft // 4]])
                half = t % 2
                for q in range(4):
                    P.op("scalar", lambda e, q=q: e.copy(out=xnblk[:, q * 4:(q + 1) * 4, half * 128:(half + 1) * 128],
                                                         in_=B[bs + q][:].rearrange("p (f t) -> p f t", f=4)),
                         reads=[Tb[bs + q]], writes=[T_xnblk])
                if half == 1:
                    blk_i = (h * NP + t * 128) // 256
                    P.op("gpsimd", lambda e: e.dma_start(out=xnT_o[blk_i], in_=xnblk[:]), reads=[T_xnblk], writes=[T_xno], dma="xno")

            n5 = NP // 128
            s5_mm(0)
            for t in range(n5):
                if t + 1 < n5:
                    s5_mm(t + 1)
                s5_rest(t)

        for h in range(npass):
            do_pass(h)
        P.emit()
    return nc


def l1_inputs(x, a_norm_g, a_w_in, a_dw_w, a_dw_b, a_ln_g, a_ln_b, a_w_out, ntok=2048, ncores=NCORES):
    w_in_l = np.ascontiguousarray(a_w_in.reshape(16, 128, 48, 128).transpose(2, 1, 0, 3))
    w_out_l = np.ascontiguousarray(a_w_out.reshape(16, 128, D).transpose(1, 0, 2))
    gbc = np.ascontiguousarray(np.broadcast_to(a_norm_g.reshape(16, 128).T[:, :, None], (128, 16, 128)).reshape(128, D)).astype(np.float32)
    dww = np.ascontiguousarray(a_dw_w.reshape(CONV_W, 16, 128).transpose(2, 1, 0)).astype(np.float32)
    cvec = np.ascontiguousarray(np.concatenate(
        [a_dw_b.reshape(16, 128).T, a_ln_g.reshape(16, 128).T, a_ln_b.reshape(16, 128).T], axis=1)).astype(np.float32)
    identf = np.eye(128, dtype=np.float32)
    identb = identf.astype(ml_dtypes.bfloat16)
    xp = np.concatenate([np.zeros((HALO, D), np.float32), x], axis=0)
    maps = []
    for c in range(ncores):
        maps.append({"x": np.ascontiguousarray(xp[c * ntok: c * ntok + ntok + HALO]), "w_in": w_in_l, "w_out": w_out_l,
                     "gbc": gbc, "dww": dww, "cvec": cvec, "identb": identb, "identf": identf})
    return maps


_CACHE = {}


def _get(name, fn):
    if name not in _CACHE:
        _CACHE[name] = fn()
    return _CACHE[name]


def kernel(x, a_norm_g, a_w_in, a_dw_w, a_dw_b, a_ln_g, a_ln_b, a_w_out,
           kv_norm_g, w_kv, k_norm_g, b_norm_g, b_w_in, b_q_norm_g,
           b_lambda, b_subln_g, b_w_out):
    f = lambda a: np.asarray(a, dtype=np.float32)
    x2 = f(x).reshape(S, D)
    cores = list(range(NCORES))
    ntok = S // NCORES
    nc1 = _get("l1", lambda: build_l1(ntok, 1024))
    m1 = l1_inputs(x2, f(a_norm_g)[0], f(a_w_in)[0], f(a_dw_w)[0], f(a_dw_b)[0], f(a_ln_g)[0], f(a_ln_b)[0], f(a_w_out)[0],
                   ntok=ntok, ncores=NCORES)
    r1 = run_bass_kernel_spmd(nc1, m1, core_ids=cores).results
    x1 = [r1[c]["x1"] for c in cores]
    xnT_blocks = np.concatenate([r1[c]["xnT"] for c in cores], axis=0)
    nc2 = _get("l2", lambda: build_l2(S))
    xnT_full = np.ascontiguousarray(xnT_blocks.transpose(2, 1, 0, 3)).reshape(16, 128, S)
    m2 = l2_inputs(xnT_full, f(w_kv), f(b_w_in)[0], f(kv_norm_g), f(b_norm_g)[0], f(k_norm_g), f(b_q_norm_g)[0],
                   f(b_lambda)[0], f(b_subln_g)[0], s_len=S)
    r2 = run_bass_kernel_spmd(nc2, m2, core_ids=cores).results
    o_full = np.concatenate([r2[c]["o"] for c in cores], axis=1)
    nc3 = _get("l3", lambda: build_l3(ntok))
    wo = np.ascontiguousarray(f(b_w_out)[0].reshape(16, 128, D).transpose(1, 0, 2))
    m3 = []
    for c in cores:
        oc = o_full[c * ntok:(c + 1) * ntok]
        oT = np.ascontiguousarray(oc.reshape(ntok // 128, 128, 16, 128).transpose(0, 3, 2, 1))
        m3.append({"oT": oT, "x1": x1[c], "wo": wo})
    r3 = run_bass_kernel_spmd(nc3, m3, core_ids=cores).results
    out = np.concatenate([r3[c]["y"] for c in cores], axis=0).reshape(1, S, D).astype(np.float32)
    return out
```
